# Optimizing a Trainium2 kernel written in Bass

```python
import math
import jax, jax.numpy as jnp
from jax import lax
import numpy as np

D_MODEL = 1024
BATCH = 8
SEQ = 4096
DEPTH = 2

MIX_WIDTH = 2 * D_MODEL
SSD_WIDTH = D_MODEL
SSD_HEAD_DIM = 64
SSD_HEADS = SSD_WIDTH // SSD_HEAD_DIM
SSD_GROUPS = 4
SSD_STATE = 128
SSD_CHUNK = 128
CONV_WIDTH = 5
CONV_CH = SSD_WIDTH + 2 * SSD_GROUPS * SSD_STATE
S5_WIDTH = MIX_WIDTH - SSD_WIDTH
S5_GROUP_CH = 16
S5_GROUPS = S5_WIDTH // S5_GROUP_CH
S5_STATE = 64
D_FF = 4 * D_MODEL
IN_PROJ = SSD_WIDTH + CONV_CH + 2 * SSD_HEADS + S5_WIDTH
DEEPNORM_ALPHA = (2 * DEPTH) ** 0.25
DEEPNORM_BETA = (8 * DEPTH) ** -0.25
NORM_EPS = 1e-5

kernel_name = 'hybrid_ssd_s5_deepnorm_encoder'


def layer_norm(x, g, b):
    xf = x.astype(jnp.float32)
    mu = jnp.mean(xf, axis=-1, keepdims=True)
    var = jnp.mean(jnp.square(xf - mu), axis=-1, keepdims=True)
    out = (xf - mu) * lax.rsqrt(var + NORM_EPS)
    return (out * g.astype(jnp.float32) + b.astype(jnp.float32)).astype(x.dtype)


def rms_norm(x, w):
    xf = x.astype(jnp.float32)
    out = xf * lax.rsqrt(jnp.mean(jnp.square(xf), axis=-1, keepdims=True) + NORM_EPS)
    return (out * w.astype(jnp.float32)).astype(x.dtype)


def depthwise_conv_centred(x, w, b):
    pad = w.shape[0] // 2
    out = lax.conv_general_dilated(
        x, w[:, None, :].astype(x.dtype), window_strides=(1,), padding=[(pad, pad)],
        dimension_numbers=('NWC', 'WIO', 'NWC'), feature_group_count=x.shape[-1])
    return out + b.astype(x.dtype)


def _segsum(x):
    T = x.shape[-1]
    xx = jnp.broadcast_to(x[..., None], x.shape + (T,))
    strict = jnp.tril(jnp.ones((T, T), dtype=bool), -1)
    xs = jnp.cumsum(jnp.where(strict, xx, 0.0), axis=-2)
    incl = jnp.tril(jnp.ones((T, T), dtype=bool), 0)
    return jnp.where(incl, xs, -jnp.inf)


def ssd_chunked_scan(xh, dt, A, Bm, Cm):
    b, l, h, p = xh.shape
    g, n = Bm.shape[-2], Bm.shape[-1]
    r = h // g
    q = SSD_CHUNK
    c = l // q
    X = (xh * dt[..., None]).reshape(b, c, q, g, r, p)
    Adt = (dt * A).reshape(b, c, q, g, r).transpose(0, 3, 4, 1, 2)
    A_cs = jnp.cumsum(Adt, axis=-1)
    Bc = Bm.reshape(b, c, q, g, n)
    Cc = Cm.reshape(b, c, q, g, n)
    Lmat = jnp.exp(_segsum(Adt))
    CB = jnp.einsum('bclgn,bcsgn->bgcls', Cc, Bc)
    y_diag = jnp.einsum('bgcls,bgrcls,bcsgrp->bclgrp', CB, Lmat, X)
    decay_states = jnp.exp(A_cs[..., -1:] - A_cs)
    states = jnp.einsum('bclgn,bgrcl,bclgrp->bcgrpn', Bc, decay_states, X)
    A_chunk = jnp.pad(A_cs[..., -1], ((0, 0), (0, 0), (0, 0), (1, 0)))
    decay_chunk = jnp.exp(_segsum(A_chunk))
    states = jnp.concatenate([jnp.zeros_like(states[:, :1]), states], axis=1)
    new_states = jnp.einsum('bgrzc,bcgrpn->bzgrpn', decay_chunk, states)
    prev_states = new_states[:, :-1]
    y_off = jnp.einsum('bclgn,bcgrpn,bgrcl->bclgrp', Cc, prev_states, jnp.exp(A_cs))
    return (y_diag + y_off).reshape(b, l, h, p)


def _complex_affine_combine(e_i, e_j):
    ai_re, ai_im, bi_re, bi_im = e_i
    aj_re, aj_im, bj_re, bj_im = e_j
    a_re = aj_re * ai_re - aj_im * ai_im
    a_im = aj_re * ai_im + aj_im * ai_re
    b_re = aj_re * bi_re - aj_im * bi_im + bj_re
    b_im = aj_re * bi_im + aj_im * bi_re + bj_im
    return (a_re, a_im, b_re, b_im)


def s5_bidirectional(u, a_re, a_im, log_step, b_re, b_im, c_re, c_im, d):
    bsz, l, _ = u.shape
    uf = u.astype(jnp.float32)
    ug = uf.reshape(bsz, l, S5_GROUPS, S5_GROUP_CH)
    br = b_re.astype(jnp.float32)
    bi = b_im.astype(jnp.float32)
    y = d.astype(jnp.float32) * uf
    for k, rev in ((0, False), (1, True)):
        lam_re = a_re[k].astype(jnp.float32)
        lam_im = a_im[k].astype(jnp.float32)
        step = jnp.exp(log_step[k].astype(jnp.float32))[:, None]
        mag = jnp.exp(lam_re * step)
        ang = lam_im * step
        lb_re = mag * jnp.cos(ang)
        lb_im = mag * jnp.sin(ang)
        den = lam_re * lam_re + lam_im * lam_im
        nr = lb_re - 1.0
        coef_re = (nr * lam_re + lb_im * lam_im) / den
        coef_im = (lb_im * lam_re - nr * lam_im) / den
        bb_re = coef_re[..., None] * br - coef_im[..., None] * bi
        bb_im = coef_re[..., None] * bi + coef_im[..., None] * br
        bu_re = jnp.einsum('blgh,gph->blgp', ug, bb_re)
        bu_im = jnp.einsum('blgh,gph->blgp', ug, bb_im)
        a_seq_re = jnp.broadcast_to(lb_re, (1, l) + lb_re.shape)
        a_seq_im = jnp.broadcast_to(lb_im, (1, l) + lb_im.shape)
        _, _, s_re, s_im = lax.associative_scan(
            _complex_affine_combine, (a_seq_re, a_seq_im, bu_re, bu_im), reverse=rev, axis=1)
        yk = (jnp.einsum('blgp,ghp->blgh', s_re, c_re[k].astype(jnp.float32))
              - jnp.einsum('blgp,ghp->blgh', s_im, c_im[k].astype(jnp.float32)))
        y = y + yk.reshape(bsz, l, S5_WIDTH)
    return y.astype(u.dtype)


def hybrid_mixer(h, w_in, conv_w, conv_b, dt_bias, a_log, ssd_d, ssd_norm_w,
                 s5_a_re, s5_a_im, s5_log_step, s5_b_re, s5_b_im, s5_c_re, s5_c_im, s5_d,
                 w_glu, b_glu, s5_norm_w, w_out):
    bsz, l, _ = h.shape
    proj = h @ w_in
    o1 = SSD_WIDTH
    o2 = o1 + CONV_CH
    o3 = o2 + 2 * SSD_HEADS
    z = proj[..., :o1]
    xbc = proj[..., o1:o2]
    dt_raw = proj[..., o2:o3]
    u = proj[..., o3:]
    xbc = jax.nn.silu(depthwise_conv_centred(xbc, conv_w, conv_b))
    gn = SSD_GROUPS * SSD_STATE
    xh = xbc[..., :SSD_WIDTH].reshape(bsz, l, SSD_HEADS, SSD_HEAD_DIM).astype(jnp.float32)
    Bm = xbc[..., SSD_WIDTH:SSD_WIDTH + gn].reshape(bsz, l, SSD_GROUPS, SSD_STATE).astype(jnp.float32)
    Cm = xbc[..., SSD_WIDTH + gn:].reshape(bsz, l, SSD_GROUPS, SSD_STATE).astype(jnp.float32)
    dt = jax.nn.softplus(dt_raw.reshape(bsz, l, 2, SSD_HEADS).astype(jnp.float32)
                         + dt_bias.astype(jnp.float32))
    A = -jnp.exp(a_log.astype(jnp.float32))
    flip = lambda t: jnp.flip(t, axis=1)
    y_fwd = ssd_chunked_scan(xh, dt[:, :, 0], A[0], Bm, Cm)
    y_bwd = flip(ssd_chunked_scan(flip(xh), flip(dt[:, :, 1]), A[1], flip(Bm), flip(Cm)))
    y = y_fwd + y_bwd + ssd_d.astype(jnp.float32)[:, None] * xh
    y = y.reshape(bsz, l, SSD_WIDTH) * jax.nn.silu(z.astype(jnp.float32))
    y_ssd = rms_norm(y, ssd_norm_w).astype(h.dtype)
    y5 = s5_bidirectional(u, s5_a_re, s5_a_im, s5_log_step, s5_b_re, s5_b_im, s5_c_re, s5_c_im, s5_d)
    g = jax.nn.gelu(y5)
    y_s5 = rms_norm(g * jax.nn.sigmoid(g @ w_glu + b_glu), s5_norm_w)
    return jnp.concatenate([y_ssd, y_s5], axis=-1) @ w_out


def squared_relu_mlp(h, w1, w2):
    return jnp.square(jax.nn.relu(h @ w1)) @ w2


def setup_inputs(seed: int = 0) -> dict:
    key = jax.random.key(seed)
    ks = jax.random.split(key, 32)
    f32 = jnp.float32
    nrm = lambda k, shape, s: jax.random.normal(k, shape, f32) * s
    L = DEPTH
    dt0 = jnp.exp(jax.random.uniform(ks[5], (L, 2, SSD_HEADS), f32, math.log(1e-3), math.log(1e-1)))
    n_idx = jnp.arange(S5_STATE, dtype=f32)
    return {
        'x': nrm(ks[0], (BATCH, SEQ, D_MODEL), 1.0),
        'ln_in_g': 1.0 + nrm(ks[1], (D_MODEL,), 0.02),
        'ln_in_b': nrm(ks[2], (D_MODEL,), 0.02),
        'w_in': nrm(ks[3], (L, D_MODEL, IN_PROJ), D_MODEL ** -0.5),
        'conv_w': nrm(ks[4], (L, CONV_WIDTH, CONV_CH), CONV_WIDTH ** -0.5),
        'conv_b': nrm(ks[6], (L, CONV_CH), 0.02),
        'dt_bias': dt0 + jnp.log(-jnp.expm1(-dt0)),
        'a_log': jnp.log(jax.random.uniform(ks[7], (L, 2, SSD_HEADS), f32, 1.0, 16.0)),
        'ssd_d': 1.0 + nrm(ks[8], (L, SSD_HEADS), 0.1),
        'ssd_norm_w': 1.0 + nrm(ks[9], (L, SSD_WIDTH), 0.02),
        's5_a_re': -0.5 + nrm(ks[10], (L, 2, S5_GROUPS, S5_STATE), 0.01),
        's5_a_im': jnp.pi * n_idx + nrm(ks[11], (L, 2, S5_GROUPS, S5_STATE), 0.01),
        's5_log_step': jax.random.uniform(ks[12], (L, 2, S5_GROUPS), f32, math.log(1e-3), math.log(1e-1)),
        's5_b_re': nrm(ks[13], (L, S5_GROUPS, S5_STATE, S5_GROUP_CH), (2 * S5_GROUP_CH) ** -0.5),
        's5_b_im': nrm(ks[14], (L, S5_GROUPS, S5_STATE, S5_GROUP_CH), (2 * S5_GROUP_CH) ** -0.5),
        's5_c_re': nrm(ks[15], (L, 2, S5_GROUPS, S5_GROUP_CH, S5_STATE), 0.5),
        's5_c_im': nrm(ks[16], (L, 2, S5_GROUPS, S5_GROUP_CH, S5_STATE), 0.5),
        's5_d': nrm(ks[17], (L, S5_WIDTH), 0.5),
        'w_glu': nrm(ks[18], (L, S5_WIDTH, S5_WIDTH), S5_WIDTH ** -0.5),
        'b_glu': nrm(ks[19], (L, S5_WIDTH), 0.02),
        's5_norm_w': 1.0 + nrm(ks[20], (L, S5_WIDTH), 0.02),
        'w_out': nrm(ks[21], (L, MIX_WIDTH, D_MODEL), MIX_WIDTH ** -0.5 * DEEPNORM_BETA),
        'ln1_g': 1.0 + nrm(ks[22], (L, D_MODEL), 0.02),
        'ln1_b': nrm(ks[23], (L, D_MODEL), 0.02),
        'w_mlp1': nrm(ks[24], (L, D_MODEL, D_FF), D_MODEL ** -0.5),
        'w_mlp2': nrm(ks[25], (L, D_FF, D_MODEL), D_FF ** -0.5 * DEEPNORM_BETA),
        'ln2_g': 1.0 + nrm(ks[26], (L, D_MODEL), 0.02),
        'ln2_b': nrm(ks[27], (L, D_MODEL), 0.02),
    }


def reference(x, ln_in_g, ln_in_b, w_in, conv_w, conv_b, dt_bias, a_log, ssd_d, ssd_norm_w,
              s5_a_re, s5_a_im, s5_log_step, s5_b_re, s5_b_im, s5_c_re, s5_c_im, s5_d,
              w_glu, b_glu, s5_norm_w, w_out, ln1_g, ln1_b, w_mlp1, w_mlp2, ln2_g, ln2_b):
    h = layer_norm(x, ln_in_g, ln_in_b)
    for i in range(DEPTH):
        mix = hybrid_mixer(h, w_in[i], conv_w[i], conv_b[i], dt_bias[i], a_log[i], ssd_d[i], ssd_norm_w[i],
                           s5_a_re[i], s5_a_im[i], s5_log_step[i], s5_b_re[i], s5_b_im[i],
                           s5_c_re[i], s5_c_im[i], s5_d[i], w_glu[i], b_glu[i], s5_norm_w[i], w_out[i])
        h = layer_norm(DEEPNORM_ALPHA * h + mix, ln1_g[i], ln1_b[i])
        h = layer_norm(DEEPNORM_ALPHA * h + squared_relu_mlp(h, w_mlp1[i], w_mlp2[i]), ln2_g[i], ln2_b[i])
    return h
```

```python
import numpy as np
import concourse.bass as bass
import concourse.mybir as mybir
from concourse.bass_utils import run_bass_kernel_spmd
from contextlib import ExitStack

F32 = mybir.dt.float32
BF16 = mybir.dt.bfloat16
I32 = mybir.dt.int32
AF = mybir.ActivationFunctionType
ALU = mybir.AluOpType
AX = mybir.AxisListType

P = 128
T = 4096
D = 1024
NT = T // P
DEPTH = 2
CONV_CH = 2048
IN_PROJ = 4128
D_FF = 4096
ALPHA = float((2 * DEPTH) ** 0.25)
EPS = 1e-5
O_Z, O_XBC, O_DT, O_U = 0, 1024, 3072, 3104


class Dep:
    __slots__ = ("w", "r", "x")

    def __init__(self, x=False):
        self.w = None
        self.r = []
        self.x = x


class Sched:
    def __init__(self, nc, stk, n_dma=40):
        self.nc = nc
        self.eng = {"pe": nc.tensor, "act": nc.scalar, "dve": nc.vector,
                    "pool": nc.gpsimd, "sp": nc.sync}
        self.sem = {e: stk.enter_context(nc.semaphore("s_" + e)) for e in self.eng}
        self.cnt = {e: 0 for e in self.eng}
        self.seen = {e: {} for e in self.eng}
        self.dsem = [stk.enter_context(nc.semaphore("d%d" % i)) for i in range(n_dma)]
        self.dval = [0] * n_dma
        n_sw = 12
        self.dpool = {"hw": list(range(0, n_dma - n_sw)), "sw": list(range(n_dma - n_sw, n_dma))}
        self.dnext = {"hw": 0, "sw": 0}

    def _semof(self, key):
        return self.dsem[key[1]] if isinstance(key, tuple) else self.sem[key]

    def _collect(self, e, reads, writes, extra=()):
        need = {}

        def add(tok):
            if tok is None:
                return
            k, v = tok
            if k == e and e == "pe":
                return
            if self.seen[e].get(k, 0) >= v:
                return
            if need.get(k, 0) < v:
                need[k] = v

        for d in reads:
            add(d.w)
            if d.x:
                for t in d.r:
                    if t[0] != e:
                        add(t)
        for d in writes:
            add(d.w)
            for t in d.r:
                add(t)
        for t in extra:
            add(t)
        for k, v in need.items():
            self.eng[e].wait_ge(self._semof(k), v)
            self.seen[e][k] = v

    def op(self, e, fn, reads=(), writes=()):
        self._collect(e, reads, writes)
        ins = fn(self.eng[e])
        self.cnt[e] += 1
        ins.then_inc(self.sem[e], 1)
        tok = (e, self.cnt[e])
        for d in reads:
            d.r.append(tok)
        for d in writes:
            d.w = tok
            d.r = []
        return tok

    def dma(self, q, out, in_, reads=(), writes=(), **kw):
        kind = "sw" if q == "pool" else "hw"
        pool = self.dpool[kind]
        k = pool[self.dnext[kind]]
        self.dnext[kind] = (self.dnext[kind] + 1) % len(pool)
        extra = []
        if self.dval[k] > 0:
            extra.append((("d", k), self.dval[k]))
        self._collect(q, reads, writes, extra)
        self.dval[k] += 16
        self.eng[q].dma_start(out=out, in_=in_, **kw).then_inc(self.dsem[k], 16)
        tok = (("d", k), self.dval[k])
        for d in reads:
            d.r.append(tok)
        for d in writes:
            d.w = tok
            d.r = []
        return tok

    def barrier(self):
        toks = [(e, self.cnt[e]) for e in self.eng if self.cnt[e] > 0]
        toks += [(("d", k), v) for k, v in enumerate(self.dval) if v > 0]
        for e in self.eng:
            self._collect(e, (), (), toks)

    def finish(self, q="sp"):
        toks = [(e, self.cnt[e]) for e in self.eng if self.cnt[e] > 0]
        toks += [(("d", k), v) for k, v in enumerate(self.dval) if v > 0]
        self._collect(q, (), (), toks)


_UID = [0]


def uname(name):
    _UID[0] += 1
    return "%s_%d" % (name, _UID[0])


class Ring:
    def __init__(self, nc, stk, name, n, shape, dtype, psum=False):
        alloc = nc.psum_tensor if psum else nc.sbuf_tensor
        self.t = [stk.enter_context(alloc(uname(name), shape, dtype)) for i in range(n)]
        self.d = [Dep(x=psum) for _ in range(n)]
        self.i = 0

    def next(self):
        i = self.i
        self.i = (i + 1) % len(self.t)
        return self.t[i], self.d[i]


PARAMS = [
    ("ln_in_g", [D]), ("ln_in_b", [D]), ("w_in", [DEPTH, D, IN_PROJ]),
    ("conv_w", [DEPTH, 5, CONV_CH]), ("conv_b", [DEPTH, CONV_CH]),
    ("dt_bias", [DEPTH, 2, 16]), ("a_log", [DEPTH, 2, 16]), ("ssd_d", [DEPTH, 16]),
    ("ssd_norm_w", [DEPTH, 1024]),
    ("s5_a_re", [DEPTH, 2, 64, 64]), ("s5_a_im", [DEPTH, 2, 64, 64]), ("s5_log_step", [DEPTH, 2, 64]),
    ("s5_b_re", [DEPTH, 64, 64, 16]), ("s5_b_im", [DEPTH, 64, 64, 16]),
    ("s5_c_re", [DEPTH, 2, 64, 16, 64]), ("s5_c_im", [DEPTH, 2, 64, 16, 64]),
    ("s5_d", [DEPTH, 1024]), ("w_glu", [DEPTH, 1024, 1024]), ("b_glu", [DEPTH, 1024]),
    ("s5_norm_w", [DEPTH, 1024]), ("w_out", [DEPTH, 2048, 1024]),
    ("ln1_g", [DEPTH, D]), ("ln1_b", [DEPTH, D]), ("w_mlp1", [DEPTH, D, D_FF]),
    ("w_mlp2", [DEPTH, D_FF, D]), ("ln2_g", [DEPTH, D]), ("ln2_b", [DEPTH, D]),
]


def bcast_row(ap_row, n):
    return ap_row.partition_broadcast(P).rearrange("p o n -> p (o n)")


class K:
    def __init__(self, dbg=None, nlayers=DEPTH):
        self.dbg = dbg
        self.nlayers = nlayers
        nc = self.nc = bass.Bass("TRN2", target_bir_lowering=False)
        self.x = nc.dram_tensor("x", [T, D], F32, kind="ExternalInput").ap()
        self.prm = {n: nc.dram_tensor(n, s, F32, kind="ExternalInput").ap() for n, s in PARAMS}
        self.out = nc.dram_tensor("out", [T, D], F32, kind="ExternalOutput").ap()
        self.scr = {}
        with ExitStack() as stk:
            self.gstk = stk
            self.S = Sched(nc, stk)
            self.build()
            self.S.finish("sp")

    def dram(self, name, shape, dtype):
        kind = "ExternalOutput" if self.dbg == name else "Internal"
        t = self.nc.dram_tensor(name, shape, dtype, kind=kind).ap()
        self.scr[name] = t
        return t

    def build(self):
        nc, S, stk = self.nc, self.S, self.gstk
        self.H32 = self.dram("H32", [T, D], F32)
        self.HT = self.dram("HT", [D, T], BF16)
        self.WB = {}
        for L_ in range(max(self.nlayers, 1)):
            self.WB[L_] = dict(WINB=self.dram("WINB%d" % L_, [D, IN_PROJ], BF16), WGLUB=self.dram("WGLUB%d" % L_, [D, D], BF16),
                               WOUTB=self.dram("WOUTB%d" % L_, [2 * D, D], BF16), W1B=self.dram("W1B%d" % L_, [D, D_FF], BF16),
                               W2B=self.dram("W2B%d" % L_, [D_FF, D], BF16))
        self.Z = self.dram("Z", [T, D], F32)
        self.DT = self.dram("DT", [T, 32], F32)
        self.U = self.dram("U", [T, D], BF16)
        self.XBCT = self.dram("XBCT", [CONV_CH, T], BF16)
        self.YCATT = self.dram("YCATT", [2 * D, T], BF16)
        self.ps = Ring(nc, stk, "ps", 6, [P, 512], F32, psum=True)
        self.psY = Ring(nc, stk, "psY", 2, [P, 512], F32, psum=True)
        ii = stk.enter_context(nc.sbuf_tensor(uname("c_ii"), [P, P], I32))
        self.ident = stk.enter_context(nc.sbuf_tensor(uname("c_id"), [P, P], F32))
        self.identb = stk.enter_context(nc.sbuf_tensor(uname("c_idb"), [P, P], BF16))
        self.dconst = Dep()
        dii = Dep()
        S.op("pool", lambda e: e.iota(ii[:], pattern=[[1, P]], base=0, channel_multiplier=-1), writes=[dii])
        S.op("dve", lambda e: e.tensor_single_scalar(out=self.ident[:], in_=ii[:], scalar=0, op=ALU.is_equal),
             reads=[dii], writes=[self.dconst])
        S.op("dve", lambda e: e.tensor_copy(out=self.identb[:], in_=self.ident[:]), reads=[self.dconst], writes=[self.dconst])
        self.ii, self.dii = ii, dii

        with ExitStack() as cst:
            steps = self.cast_steps(cst, 0) if self.nlayers > 0 else []
            self.ln_phase(self.x, steps, self.prm["ln_in_g"].rearrange("(o n) -> o n", o=1),
                          self.prm["ln_in_b"].rearrange("(o n) -> o n", o=1), self.H32)
        for L in range(self.nlayers):
            self.layer(L)
        with ExitStack() as st:
            r = Ring(nc, st, "fo", 3, [P, D], F32)
            for tt in range(NT):
                t_, d_ = r.next()
                S.dma("sp", t_[:], self.H32[tt * P:(tt + 1) * P, :], writes=[d_])
                S.dma("act", self.out[tt * P:(tt + 1) * P, :], t_[:], reads=[d_])
            S.barrier()

    def ln_core(self, st, xt, dx, g_t, b_t, dgb, tt, h32_out, tmp):
        nc, S = self.nc, self.S
        stats, mv, rstd, dst = tmp["stats"], tmp["mv"], tmp["rstd"], tmp["dst"]
        S.op("dve", lambda e: e.bn_stats(out=stats[:, 0:6], in_=xt[:, 0:512]), reads=[dx], writes=[dst])
        S.op("dve", lambda e: e.bn_stats(out=stats[:, 6:12], in_=xt[:, 512:1024]), reads=[dx], writes=[dst])
        S.op("dve", lambda e: e.bn_aggr(out=mv[:], in_=stats[:]), reads=[dst], writes=[dst])
        S.op("act", lambda e: e.activation(out=rstd[:], in_=mv[:, 1:2], func=AF.Sqrt, bias=EPS, scale=1.0),
             reads=[dst], writes=[dst])
        S.op("dve", lambda e: e.reciprocal(out=rstd[:], in_=rstd[:]), reads=[dst], writes=[dst])
        S.op("dve", lambda e: e.tensor_scalar(out=xt[:], in0=xt[:], scalar1=mv[:, 0:1], scalar2=rstd[:, 0:1],
                                              op0=ALU.subtract, op1=ALU.mult), reads=[dx, dst], writes=[dx])
        S.op("dve", lambda e: e.tensor_tensor(out=xt[:], in0=xt[:], in1=g_t[:], op=ALU.mult), reads=[dx, dgb], writes=[dx])
        S.op("dve", lambda e: e.tensor_tensor(out=xt[:], in0=xt[:], in1=b_t[:], op=ALU.add), reads=[dx, dgb], writes=[dx])
        S.dma("act", h32_out[tt * P:(tt + 1) * P, :], xt[:], reads=[dx])
        if tmp.get("defer") is not None:
            tmp["defer"].append((xt, dx, tt))
            return
        self.ln_T(xt, dx, tt, tmp)

    def ln_flush(self, tmp, keep=0):
        q = tmp.get("defer")
        while q is not None and len(q) > keep:
            xt, dx, tt = q.pop(0)
            self.ln_T(xt, dx, tt, tmp)

    def ln_T(self, xt, dx, tt, tmp):
        S = self.S
        hT, dhT = tmp["hT"].next()
        for half in range(2):
            pt, dpt = self.ps.next()
            for j in range(4):
                k = half * 4 + j
                S.op("pe", lambda e, k=k, j=j, pt=pt: e.transpose(out=pt[:, j * P:(j + 1) * P], in_=xt[:, k * P:(k + 1) * P],
                                                                  identity=self.ident[:]),
                     reads=[dx, self.dconst], writes=[dpt])
            S.op("act", lambda e, pt=pt, half=half: e.activation(
                out=hT[:, half * 4:(half + 1) * 4, :], in_=pt[:].rearrange("p (j t) -> p j t", j=4), func=AF.Copy),
                reads=[dpt], writes=[dhT])
        S.dma("act", self.HT.rearrange("(k p) t -> p k t", p=P)[:, :, tt * P:(tt + 1) * P], hT[:], reads=[dhT])

    def ln_tmp(self, st):
        nc = self.nc
        return {
            "stats": st.enter_context(nc.sbuf_tensor(uname("ln_stats"), [P, 12], F32)),
            "mv": st.enter_context(nc.sbuf_tensor(uname("ln_mv"), [P, 2], F32)),
            "rstd": st.enter_context(nc.sbuf_tensor(uname("ln_rstd"), [P, 1], F32)),
            "dst": Dep(),
            "defer": [],
            "hT": Ring(nc, st, "ln_hT", 2, [P, 8, P], BF16),
        }

    def load_gb(self, st, g_row, b_row):
        nc, S = self.nc, self.S
        g_t = st.enter_context(nc.sbuf_tensor(uname("ln_g"), [P, D], F32))
        b_t = st.enter_context(nc.sbuf_tensor(uname("ln_b"), [P, D], F32))
        dgb = Dep()
        S.dma("sp", g_t[:], bcast_row(g_row, D), writes=[dgb])
        S.dma("sp", b_t[:], bcast_row(b_row, D), writes=[dgb])
        return g_t, b_t, dgb

    def ln_phase(self, src, extra, g_row, b_row, h32_out):
        nc, S = self.nc, self.S
        with ExitStack() as st:
            g_t, b_t, dgb = self.load_gb(st, g_row, b_row)
            tmp = self.ln_tmp(st)
            r = Ring(nc, st, "ln_x", 5, [P, D], F32)
            extra = list(extra or [])
            per = (len(extra) + NT - 1) // NT
            pend = None
            for tt in range(NT):
                xt, dx = r.next()
                S.dma("sp", xt[:], src[tt * P:(tt + 1) * P, :], writes=[dx])
                for _ in range(per):
                    if extra:
                        extra.pop(0)()
                if pend is not None:
                    self.ln_core(st, *pend)
                    self.ln_flush(tmp, 1)
                pend = (xt, dx, g_t, b_t, dgb, tt, h32_out, tmp)
            self.ln_core(st, *pend)
            self.ln_flush(tmp, 0)
            while extra:
                extra.pop(0)()
            S.barrier()

    def cast_steps(self, st, L, CH=2048, depth=3):
        nc, S, p = self.nc, self.S, self.prm
        rf = Ring(nc, st, "cw_f", depth, [P, CH], F32)
        rb = Ring(nc, st, "cw_b", depth, [P, CH], BF16)
        steps = []
        cnt = [0]

        def mk(W, WB, kt, n0, nb):
            def step():
                f, df = rf.next()
                b, db = rb.next()
                S.dma("sp", f[:, :nb], W[kt * P:(kt + 1) * P, n0:n0 + nb], writes=[df])
                S.op("pool", lambda e: e.tensor_copy(out=b[:, :nb], in_=f[:, :nb]), reads=[df, db], writes=[db])
                S.dma("pool", WB[kt * P:(kt + 1) * P, n0:n0 + nb], b[:, :nb], reads=[db])
                cnt[0] += 1
            return step

        wb = self.WB[L]
        for W, WB in ((p["w_in"][L], wb["WINB"]), (p["w_glu"][L], wb["WGLUB"]), (p["w_out"][L], wb["WOUTB"]),
                      (p["w_mlp1"][L], wb["W1B"]), (p["w_mlp2"][L], wb["W2B"])):
            Kd, N = W.shape
            for kt in range(Kd // P):
                for n0 in range(0, N, CH):
                    steps.append(mk(W, WB, kt, n0, min(CH, N - n0)))
        return steps

    def cast_w(self, W, WB):
        nc, S = self.nc, self.S
        Kd, N = W.shape
        with ExitStack() as st:
            rf = Ring(nc, st, "cw_f", 3, [P, 2048], F32)
            rb = Ring(nc, st, "cw_b", 3, [P, 2048], BF16)
            i = 0
            for kt in range(Kd // P):
                for n0 in range(0, N, 2048):
                    nb = min(2048, N - n0)
                    f, df = rf.next()
                    b, db = rb.next()
                    S.dma("sp", f[:, :nb], W[kt * P:(kt + 1) * P, n0:n0 + nb], writes=[df])
                    eng = "pool" if i % 2 == 0 else "act"
                    if eng == "pool":
                        S.op("pool", lambda e, f=f, b=b, nb=nb: e.tensor_copy(out=b[:, :nb], in_=f[:, :nb]), reads=[df], writes=[db])
                    else:
                        S.op("act", lambda e, f=f, b=b, nb=nb: e.activation(out=b[:, :nb], in_=f[:, :nb], func=AF.Copy), reads=[df], writes=[db])
                    S.dma(eng, WB[kt * P:(kt + 1) * P, n0:n0 + nb], b[:, :nb], reads=[db])
                    i += 1
            S.barrier()

    def layer(self, L):
        p = self.prm
        last = (L == self.nlayers - 1)
        wb = self.WB[L]
        self.WINB, self.WGLUB, self.WOUTB, self.W1B, self.W2B = wb["WINB"], wb["WGLUB"], wb["WOUTB"], wb["W1B"], wb["W2B"]
        if L > 0 and not OVERLAP_CAST:
            self.cast_w(p["w_in"][L], self.WINB)
            self.cast_w(p["w_glu"][L], self.WGLUB)
            self.cast_w(p["w_out"][L], self.WOUTB)
            self.cast_w(p["w_mlp1"][L], self.W1B)
            self.cast_w(p["w_mlp2"][L], self.W2B)
        if last and STOP == "cast":
            return
        self.in_proj(L)
        if self.dbg in ("Z", "DT", "U", "XBCT") or (last and STOP == "in_proj"):
            return
        self.ssd(L)
        if SSD_ONLY or (last and STOP in ("ssd", "ssdA")):
            return
        if OVERLAP_CAST and L + 1 < self.nlayers:
            with ExitStack() as cst:
                self.s5(L, self.cast_steps(cst, L + 1, CH=1024, depth=2))
        else:
            self.s5(L, [])
        if self.dbg == "Y5" or (last and STOP in ("s5", "s5prep", "s5main")):
            return
        self.out_proj(L)
        if STOP == "out_proj" and last:
            return
        self.mlp(L)

    def in_proj(self, L):
        nc, S, p = self.nc, self.S, self.prm
        with ExitStack() as st:
            hT = st.enter_context(nc.sbuf_tensor(uname("ip_hT"), [P, 8, T], BF16))
            dh = Dep()
            for k in range(8):
                S.dma("sp", hT[:, k, :], self.HT[k * P:(k + 1) * P, :], writes=[dh])
            wr = Ring(nc, st, "ip_w", 2, [P, 8, 512], BF16)
            orr = Ring(nc, st, "ip_o", 3, [P, 512], F32)
            orb = Ring(nc, st, "ip_ob16", 3, [P, 512], BF16)
            WB3 = self.WINB.rearrange("(k p) n -> p k n", p=P)
            for (c0, dst, d0) in ((O_Z, self.Z, 0), (O_Z + 512, self.Z, 512), (O_U, self.U, 0), (O_U + 512, self.U, 512)):
                w, dw = wr.next()
                S.dma("sp", w[:], WB3[:, :, c0:c0 + 512], writes=[dw])
                for tt in range(NT):
                    pt, dpt = self.ps.next()
                    for k in range(8):
                        S.op("pe", lambda e, k=k, pt=pt, w=w, tt=tt: e.matmul(pt[:], lhsT=hT[:, k, tt * P:(tt + 1) * P], rhs=w[:, k, :],
                                                                        start=(k == 0), stop=(k == 7)),
                             reads=[dh, dw], writes=[dpt])
                    o, do = (orr if dst is self.Z else orb).next()
                    eng = "dve" if tt % 2 == 0 else "act"
                    if eng == "dve":
                        S.op("dve", lambda e, o=o, pt=pt: e.tensor_copy(out=o[:], in_=pt[:]), reads=[dpt], writes=[do])
                    else:
                        S.op("act", lambda e, o=o, pt=pt: e.activation(out=o[:], in_=pt[:], func=AF.Copy), reads=[dpt], writes=[do])
                    S.dma("pool", dst[tt * P:(tt + 1) * P, d0:d0 + 512], o[:], reads=[do])
            w, dw = wr.next()
            S.dma("sp", w[:, :, 0:32], WB3[:, :, O_DT:O_DT + 32], writes=[dw])
            dtb = st.enter_context(nc.sbuf_tensor(uname("ip_dtb"), [P, 32], F32))
            ddtb = Dep()
            S.dma("sp", dtb[:], bcast_row(p["dt_bias"][L].rearrange("(o a) h -> o (a h)", o=1), 32), writes=[ddtb])
            for tt in range(NT):
                pt, dpt = self.ps.next()
                for k in range(8):
                    S.op("pe", lambda e, k=k, pt=pt, w=w, tt=tt: e.matmul(pt[:, 0:32], lhsT=hT[:, k, tt * P:(tt + 1) * P], rhs=w[:, k, 0:32],
                                                                    start=(k == 0), stop=(k == 7)),
                         reads=[dh, dw], writes=[dpt])
                o, do = orr.next()
                S.op("dve", lambda e, o=o, pt=pt: e.tensor_tensor(out=o[:, 0:32], in0=pt[:, 0:32], in1=dtb[:], op=ALU.add),
                     reads=[dpt, ddtb], writes=[do])
                S.op("act", lambda e, o=o: e.activation(out=o[:, 0:32], in_=o[:, 0:32], func=AF.Exp), reads=[do], writes=[do])
                S.op("act", lambda e, o=o: e.activation(out=o[:, 0:32], in_=o[:, 0:32], func=AF.Ln, bias=1.0, scale=1.0), reads=[do], writes=[do])
                S.dma("pool", self.DT[tt * P:(tt + 1) * P, :], o[:, 0:32], reads=[do])
            cw = st.enter_context(nc.sbuf_tensor(uname("ip_cw"), [P, 5, 16], F32))
            cb = st.enter_context(nc.sbuf_tensor(uname("ip_cb"), [P, 16], F32))
            dcw = Dep()
            for kk in range(5):
                S.dma("sp", cw[:, kk, :], p["conv_w"][L][kk].rearrange("(f p) -> p f", p=P), writes=[dcw], allow_slow_non_contiguous=True)
            S.dma("sp", cb[:], p["conv_b"][L].rearrange("(f p) -> p f", p=P), writes=[dcw], allow_slow_non_contiguous=True)
            xr_ring = Ring(nc, st, "ip_xr", 2, [P, T + 4], BF16)
            ob_ring = Ring(nc, st, "ip_ob", 2, [P, T], BF16)
            for xr_t, xr_d in zip(xr_ring.t, xr_ring.d):
                S.op("pool", lambda e, t_=xr_t: e.memset(t_[:, 0:2], 0.0), writes=[xr_d])
                S.op("pool", lambda e, t_=xr_t: e.memset(t_[:, T + 2:T + 4], 0.0), writes=[xr_d])
            diagw = st.enter_context(nc.sbuf_tensor(uname("ip_diagw"), [P, 16, 5, P], BF16))
            ddg = Dep()
            for f in range(16):
                for kk in range(5):
                    S.op("dve", lambda e, f=f, kk=kk: e.tensor_scalar(out=diagw[:, f, kk, :], in0=self.ident[:], scalar1=cw[:, kk, f:f + 1], scalar2=None, op0=ALU.mult),
                         reads=[dcw, self.dconst], writes=[ddg])

            def conv_part(f, xr, dxr):
                ob, dob = ob_ring.next()
                for tb in range(8):
                    pc, dpc = self.ps.next()
                    for kk in range(5):
                        S.op("pe", lambda e, kk=kk, pc=pc, tb=tb: e.matmul(pc[:], lhsT=diagw[:, f, kk, :], rhs=xr[:, tb * 512 + kk:tb * 512 + kk + 512],
                                                                       start=(kk == 0), stop=(kk == 4)), reads=[ddg, dxr], writes=[dpc])
                    S.op("act", lambda e, pc=pc, tb=tb, ob=ob: e.activation(out=ob[:, tb * 512:(tb + 1) * 512], in_=pc[:], func=AF.Silu, bias=cb[:, f:f + 1], scale=1.0),
                         reads=[dpc, dcw, dob], writes=[dob])
                S.dma("act", self.XBCT[f * P:(f + 1) * P, :], ob[:], reads=[dob])

            pend_conv = None
            for fq in range(4):
                w, dw = wr.next()
                S.dma("sp", w[:], WB3[:, :, O_XBC + fq * 512:O_XBC + (fq + 1) * 512], writes=[dw])
                for fj in range(4):
                    f = fq * 4 + fj
                    xr, dxr = xr_ring.next()
                    for tb in range(8):
                        pt, dpt = self.ps.next()
                        for k in range(8):
                            S.op("pe", lambda e, k=k, pt=pt, w=w, tb=tb, fj=fj: e.matmul(
                                pt[:], lhsT=w[:, k, fj * P:(fj + 1) * P], rhs=hT[:, k, tb * 512:(tb + 1) * 512],
                                start=(k == 0), stop=(k == 7)), reads=[dh, dw], writes=[dpt])
                        if tb % 2 == 0:
                            S.op("dve", lambda e, xr=xr, pt=pt, tb=tb: e.tensor_copy(out=xr[:, 2 + tb * 512:2 + (tb + 1) * 512], in_=pt[:]),
                                 reads=[dpt, dxr], writes=[dxr])
                        else:
                            S.op("act", lambda e, xr=xr, pt=pt, tb=tb: e.activation(out=xr[:, 2 + tb * 512:2 + (tb + 1) * 512], in_=pt[:], func=AF.Copy),
                                 reads=[dpt, dxr], writes=[dxr])
                    if pend_conv is not None:
                        conv_part(*pend_conv)
                    pend_conv = (f, xr, dxr)
            conv_part(*pend_conv)
            S.barrier()

    def ssd_consts(self):
        nc, S, stk = self.nc, self.S, self.gstk
        if hasattr(self, "triU"):
            return
        sb = lambda n, shp, dt=F32: stk.enter_context(nc.sbuf_tensor(uname(n), shp, dt))
        self.triU, self.triL = sb("triU", [P, P]), sb("triL", [P, P])
        self.ones = sb("ones", [P, P])
        self.sel3 = sb("sel3", [96, 32, P], BF16)
        self.ones96 = sb("ones96", [96, P], BF16)
        d = self.dconst
        S.op("dve", lambda e: e.tensor_single_scalar(out=self.triU[:], in_=self.ii[:], scalar=0, op=ALU.is_ge), reads=[self.dii], writes=[d])
        S.op("dve", lambda e: e.tensor_single_scalar(out=self.triL[:], in_=self.ii[:], scalar=0, op=ALU.is_le), reads=[self.dii], writes=[d])
        S.op("dve", lambda e: e.memset(self.ones[:], 1.0), writes=[d])
        S.op("dve", lambda e: e.memset(self.ones96[:], 1.0), writes=[d])
        for q in range(3):
            S.op("dve", lambda e, q=q: e.tensor_copy(out=self.sel3[q * 32:(q + 1) * 32],
                                                     in_=self.ident[q * 32:(q + 1) * 32, q * 32:(q + 1) * 32].unsqueeze(2).to_broadcast([32, 32, P])),
                 reads=[d], writes=[d])

    def ssd(self, L):
        nc, S, p = self.nc, self.S, self.prm
        self.ssd_consts()
        dc = self.dconst
        RB = self.scr.get("RB")
        if RB is None:
            RB = self.dram("RB", [32, P, D], BF16)
        v3 = lambda ap: ap.rearrange("p (h q) -> p h q", q=64)
        with ExitStack() as st:
            sb = lambda n, shp, dt=F32: st.enter_context(nc.sbuf_tensor(uname(n), shp, dt))
            Abc, Dbc, normw = sb("Abc", [P, 32]), sb("Dbc", [P, 16]), sb("normw", [P, D])
            dpl = Dep()
            S.dma("sp", Abc[:], bcast_row(p["a_log"][L].rearrange("(o a) h -> o (a h)", o=1), 32), writes=[dpl])
            S.dma("sp", Dbc[:], bcast_row(p["ssd_d"][L:L + 1, :], 16), writes=[dpl])
            S.dma("sp", normw[:], bcast_row(p["ssd_norm_w"][L:L + 1, :], D), writes=[dpl])
            S.op("act", lambda e: e.activation(out=Abc[:], in_=Abc[:], func=AF.Exp), reads=[dpl], writes=[dpl])
            S.op("dve", lambda e: e.tensor_scalar(out=Abc[:], in0=Abc[:], scalar1=-1.0, scalar2=None, op0=ALU.mult), reads=[dpl], writes=[dpl])
            R = [sb("Rf", [P, D]), sb("Rb", [P, D])]
            Rh = [sb("Rfh", [P, D], BF16), sb("Rbh", [P, D], BF16)]
            dR = [Dep(), Dep()]
            dRh = [Dep(), Dep()]
            for i in range(2):
                S.op("pool", lambda e, i=i: e.memset(R[i][:], 0.0), writes=[dR[i]])
                S.op("pool", lambda e, i=i: e.memset(Rh[i][:], 0.0), writes=[dRh[i]])
            xin = Ring(nc, st, "sd_xin", 2, [P, 16, 512], BF16)
            X3 = self.XBCT.rearrange("(k p) t -> p k t", p=P)
            dtr = Ring(nc, st, "sd_dt", 3, [P, 32], F32)
            sm = {n: Ring(nc, st, "sd_" + n, 3, [P, 32], F32) for n in ("adt", "cs", "dec", "etot", "ecs")}
            xtok_r = Ring(nc, st, "sd_xtok", 3, [P, D], F32)
            btok_r = Ring(nc, st, "sd_btok", 3, [P, 512], BF16)
            X_r = [Ring(nc, st, "sd_X%d" % i, 3, [P, D], BF16) for i in range(2)]
            Xd_r = [Ring(nc, st, "sd_Xd%d" % i, 3, [P, D], BF16) for i in range(2)]
            xf_r = Ring(nc, st, "sd_xf32", 2, [P, D], F32)
            cs3_r = Ring(nc, st, "sd_cs3", 3, [P, 96], F32)
            csb_r = Ring(nc, st, "sd_csb", 2, [P, 64], BF16)
            csT_r = Ring(nc, st, "sd_csT", 3, [96, P], BF16)
            csel_r = Ring(nc, st, "sd_csel", 2, [96, 32, P], BF16)
            ncsT_r = Ring(nc, st, "sd_ncsT", 3, [96, P], BF16)
            cbm_r = [Ring(nc, st, "sd_cbm%d" % i, 3, [P, 4, P], BF16) for i in range(2)]
            lt_r = Ring(nc, st, "sd_lt", 4, [P, 4, P], BF16)
            mt_r = Ring(nc, st, "sd_mt", 5, [P, 4, P], BF16)
            y_r = Ring(nc, st, "sd_y", 3, [P, D], F32)
            t2_r = Ring(nc, st, "sd_t2", 2, [P, D], BF16)
            t_r = Ring(nc, st, "sd_t", 2, [P, D], F32)
            z_r = Ring(nc, st, "sd_z", 3, [P, D], F32)
            yb_r = Ring(nc, st, "sd_yb", 3, [P, D], BF16)
            yT_r = Ring(nc, st, "sd_yT", 2, [P, 8, P], BF16)
            ss_r = Ring(nc, st, "sd_ss", 2, [P, 2], F32)
            rbl_r = Ring(nc, st, "sd_rbl", 3, [P, D], BF16)
            cur = {}

            def load_block(blk):
                t_, d_ = xin.next()
                for q in range(2):
                    S.dma("sp", t_[:, q * 8:(q + 1) * 8, :], X3[:, q * 8:(q + 1) * 8, blk * 512:(blk + 1) * 512], writes=[d_])
                cur["blk"], cur["xin"], cur["dxin"] = blk, t_, d_

            def prefetch(c):
                if cur.get("blk") != c // 4:
                    load_block(c // 4)
                pre = dict(xin=cur["xin"], dxin=cur["dxin"])
                dt, ddt = dtr.next()
                S.dma("sp", dt[:], self.DT[c * P:(c + 1) * P, :], writes=[ddt])
                rbl, drbl = rbl_r.next()
                S.dma("sp", rbl[:], RB[c], writes=[drbl])
                zt, dz = z_r.next()
                S.dma("sp", zt[:], self.Z[c * P:(c + 1) * P, :], writes=[dz])
                pre.update(dt=(dt, ddt), rbl=(rbl, drbl), zt=(zt, dz))
                return pre

            def common(c, dirs, pre=None):
                if pre is None:
                    if cur.get("blk") != c // 4:
                        load_block(c // 4)
                    xi, dxi = cur["xin"], cur["dxin"]
                    dt, ddt = dtr.next()
                    S.dma("sp", dt[:], self.DT[c * P:(c + 1) * P, :], writes=[ddt])
                else:
                    xi, dxi = pre["xin"], pre["dxin"]
                    dt, ddt = pre["dt"]
                o = (c % 4) * P
                adt, dadt = sm["adt"].next()
                S.op("dve", lambda e: e.tensor_tensor(out=adt[:], in0=dt[:], in1=Abc[:], op=ALU.mult), reads=[ddt, dpl], writes=[dadt])
                pA, dpA = self.ps.next()
                S.op("pe", lambda e: e.matmul(pA[:, 0:16], lhsT=self.triU[:], rhs=adt[:, 0:16], start=True, stop=True), reads=[dc, dadt], writes=[dpA])
                S.op("pe", lambda e: e.matmul(pA[:, 16:32], lhsT=self.triL[:], rhs=adt[:, 16:32], start=True, stop=True), reads=[dc, dadt], writes=[dpA])
                S.op("pe", lambda e: e.matmul(pA[:, 32:64], lhsT=self.ones[:], rhs=adt[:, 0:32], start=True, stop=True), reads=[dc, dadt], writes=[dpA])
                cs, dcs = sm["cs"].next()
                dec, ddec = sm["dec"].next()
                etot, detot = sm["etot"].next()
                ecs, decs = sm["ecs"].next()
                S.op("dve", lambda e: e.tensor_copy(out=cs[:], in_=pA[:, 0:32]), reads=[dpA], writes=[dcs])
                S.op("dve", lambda e: e.tensor_tensor(out=dec[:], in0=pA[:, 32:64], in1=cs[:], op=ALU.subtract), reads=[dpA, dcs], writes=[ddec])
                S.op("act", lambda e: e.activation(out=dec[:], in_=dec[:], func=AF.Exp), reads=[ddec], writes=[ddec])
                S.op("act", lambda e: e.activation(out=etot[:], in_=pA[:, 32:64], func=AF.Exp), reads=[dpA], writes=[detot])
                S.op("act", lambda e: e.activation(out=ecs[:], in_=cs[:], func=AF.Exp), reads=[dcs], writes=[decs])
                px, dpx = self.ps.next()
                pxb = px[:].bitcast(BF16)
                for k in range(8):
                    S.op("pe", lambda e, k=k: e.transpose(out=pxb[:, k * P:(k + 1) * P], in_=xi[:, k, o:o + P], identity=self.identb[:]),
                         reads=[dxi, dc], writes=[dpx])
                xtok, dxt = xtok_r.next()
                S.op("act", lambda e: e.activation(out=xtok[:], in_=pxb, func=AF.Copy), reads=[dpx], writes=[dxt])
                pb, dpb = self.ps.next()
                pbb = pb[:].bitcast(BF16)
                for g in range(4):
                    S.op("pe", lambda e, g=g: e.transpose(out=pbb[:, g * P:(g + 1) * P], in_=xi[:, 8 + g, o:o + P], identity=self.identb[:]),
                         reads=[dxi, dc], writes=[dpb])
                btok, dbt = btok_r.next()
                S.op("act", lambda e: e.activation(out=btok[:], in_=pbb[:, 0:512], func=AF.Copy), reads=[dpb], writes=[dbt])
                Xs, Xds = {}, {}
                for di in dirs:
                    X, dX = X_r[di].next()
                    Xd, dXd = Xd_r[di].next()
                    xf32, dxf = xf_r.next()
                    S.op("dve", lambda e, di=di, xf32=xf32: e.tensor_tensor(out=v3(xf32[:]), in0=v3(xtok[:]),
                                                                            in1=dt[:, di * 16:(di + 1) * 16].unsqueeze(2).to_broadcast([P, 16, 64]), op=ALU.mult),
                         reads=[dxt, ddt, dxf], writes=[dxf])
                    S.op("act", lambda e, X=X, xf32=xf32: e.activation(out=X[:], in_=xf32[:], func=AF.Copy), reads=[dxf], writes=[dX])
                    S.op("dve", lambda e, di=di, Xd=Xd, xf32=xf32: e.tensor_tensor(out=v3(Xd[:]), in0=v3(xf32[:]),
                                                                                   in1=dec[:, di * 16:(di + 1) * 16].unsqueeze(2).to_broadcast([P, 16, 64]), op=ALU.mult),
                         reads=[dxf, ddec], writes=[dXd])
                    Xs[di], Xds[di] = (X, dX), (Xd, dXd)
                return dict(xi=xi, dxi=dxi, o=o, dt=(dt, ddt), cs=(cs, dcs), etot=(etot, detot), ecs=(ecs, decs),
                            xtok=(xtok, dxt), btok=(btok, dbt), X=Xs, Xd=Xds)

            def update_state(cm, di):
                btok, dbt = cm["btok"]
                Xd, dXd = cm["Xd"][di]
                etot, detot = cm["etot"]
                S.op("dve", lambda e: e.tensor_tensor(out=v3(R[di][:]), in0=v3(R[di][:]),
                                                      in1=etot[:, di * 16:(di + 1) * 16].unsqueeze(2).to_broadcast([P, 16, 64]), op=ALU.mult),
                     reads=[detot, dR[di]], writes=[dR[di]])
                for half in range(2):
                    pst, dps = self.ps.next()
                    for gg in range(2):
                        g = half * 2 + gg
                        S.op("pe", lambda e, g=g, gg=gg, pst=pst: e.matmul(pst[:, gg * 256:(gg + 1) * 256], lhsT=btok[:, g * P:(g + 1) * P],
                                                                   rhs=Xd[:, g * 256:(g + 1) * 256], start=True, stop=True),
                             reads=[dbt, dXd], writes=[dps])
                    S.op("dve", lambda e, half=half, pst=pst: e.tensor_tensor(out=R[di][:, half * 512:(half + 1) * 512],
                                                                              in0=R[di][:, half * 512:(half + 1) * 512], in1=pst[:], op=ALU.add),
                         reads=[dps, dR[di]], writes=[dR[di]])
                S.op("dve", lambda e: e.tensor_copy(out=Rh[di][:], in_=R[di][:]), reads=[dR[di]], writes=[dRh[di]])

            cms = {NT - 1: common(NT - 1, (1,)), NT - 2: common(NT - 2, (1,))}
            for c in range(NT - 1, -1, -1):
                if c - 2 >= 0:
                    cms[c - 2] = common(c - 2, (1,))
                S.dma("act", RB[c], Rh[1][:], reads=[dRh[1]])
                update_state(cms.pop(c), 1)
            S.barrier()
            cur.clear()
            if STOP == "ssdA":
                return

            def stage1(c, pre):
                cm = common(c, (0, 1), pre)
                xi, dxi, o = cm["xi"], cm["dxi"], cm["o"]
                cs, dcs = cm["cs"]
                rbl, drbl = pre["rbl"]
                zt, dz = pre["zt"]
                S.op("act", lambda e: e.activation(out=zt[:], in_=zt[:], func=AF.Silu), reads=[dz], writes=[dz])
                cs3, dcs3 = cs3_r.next()
                csb, dcsb = csb_r.next()
                S.op("dve", lambda e: e.tensor_copy(out=csb[:, 0:32], in_=cs[:]), reads=[dcs, dcsb], writes=[dcsb])
                S.op("dve", lambda e: e.tensor_copy(out=cs3[:, 0:32], in_=csb[:, 0:32]), reads=[dcsb, dcs3], writes=[dcs3])
                S.op("dve", lambda e: e.tensor_tensor(out=cs3[:, 64:96], in0=cs[:], in1=cs3[:, 0:32], op=ALU.subtract), reads=[dcs, dcs3], writes=[dcs3])
                S.op("dve", lambda e: e.tensor_copy(out=csb[:, 32:64], in_=cs3[:, 64:96]), reads=[dcs3, dcsb], writes=[dcsb])
                S.op("dve", lambda e: e.tensor_copy(out=cs3[:, 32:64], in_=csb[:, 32:64]), reads=[dcsb, dcs3], writes=[dcs3])
                S.op("dve", lambda e: e.tensor_tensor(out=cs3[:, 64:96], in0=cs3[:, 64:96], in1=cs3[:, 32:64], op=ALU.subtract), reads=[dcs3], writes=[dcs3])
                pcb, dpcb = self.ps.next()
                for g in range(4):
                    S.op("pe", lambda e, g=g: e.matmul(pcb[:, g * P:(g + 1) * P], lhsT=xi[:, 8 + g, o:o + P], rhs=xi[:, 12 + g, o:o + P],
                                                       start=True, stop=True), reads=[dxi], writes=[dpcb])
                cbm = []
                for di in range(2):
                    t_, d_ = cbm_r[di].next()
                    msk = self.triU if di == 0 else self.triL
                    S.op("dve", lambda e, t_=t_, msk=msk: e.tensor_tensor(out=t_[:], in0=pcb[:].rearrange("p (g l) -> p g l", g=4),
                                                                          in1=msk[:].unsqueeze(1).to_broadcast([P, 4, P]), op=ALU.mult),
                         reads=[dpcb, dc], writes=[d_])
                    cbm.append((t_, d_))
                cm.update(rbl=(rbl, drbl), zt=(zt, dz), cbm=cbm, cs3=(cs3, dcs3))
                return cm

            def stage1b(cm):
                cs3, dcs3 = cm["cs3"]
                pc, dpc = self.ps.next()
                S.op("pe", lambda e: e.transpose(out=pc[0:96, 0:P], in_=cs3[:], identity=self.ident[:]), reads=[dcs3, dc], writes=[dpc])
                csT, dcsT = csT_r.next()
                ncsT, dncsT = ncsT_r.next()
                S.op("dve", lambda e: e.tensor_copy(out=csT[:], in_=pc[0:96, 0:P]), reads=[dpc, dcsT], writes=[dcsT])
                S.op("act", lambda e: e.activation(out=ncsT[:], in_=pc[0:96, 0:P], func=AF.Copy, scale=-1.0), reads=[dpc, dncsT], writes=[dncsT])
                csel, dcsel = csel_r.next()
                S.op("dve", lambda e: e.tensor_tensor(out=csel[:], in0=self.sel3[:], in1=csT[:].unsqueeze(1).to_broadcast([96, 32, P]), op=ALU.mult),
                     reads=[dcsT, dc, dcsel], writes=[dcsel])
                cm.update(csT=(csT, dcsT), ncsT=(ncsT, dncsT), csel=(csel, dcsel))

            def stage2(c, cm):
                xi, dxi, o = cm["xi"], cm["dxi"], cm["o"]
                ecs, decs = cm["ecs"]
                rbl, drbl = cm["rbl"]
                zt, dz = cm["zt"]
                csT, dcsT = cm["csT"]
                ncsT, dncsT = cm["ncsT"]
                csel, dcsel = cm["csel"]
                pY = [self.psY.next(), self.psY.next()]

                def emit_L(di, g):
                    cbt, dcbt = cm["cbm"][di]
                    pL, dpL = self.ps.next()
                    h0 = di * 16 + g * 4
                    S.op("pe", lambda e: e.matmul(pL[:], lhsT=self.ones96[:], rhs=csel[:, h0:h0 + 4, :].rearrange("k h l -> k (h l)"),
                                                  start=True, stop=False), reads=[dcsel, dc], writes=[dpL])
                    S.op("pe", lambda e: e.matmul(pL[:], lhsT=ncsT[:], rhs=self.sel3[:, h0:h0 + 4, :].rearrange("k h l -> k (h l)"),
                                                  start=False, stop=True), reads=[dncsT, dc], writes=[dpL])
                    lt, dlt = lt_r.next()
                    S.op("act", lambda e: e.activation(out=lt[:], in_=pL[:].rearrange("p (h l) -> p h l", h=4), func=AF.Exp), reads=[dpL, dlt], writes=[dlt])
                    mt, dmt = mt_r.next()
                    S.op("dve", lambda e: e.scalar_tensor_tensor(out=mt[:], in0=lt[:], scalar=1.0, in1=cbt[:, g:g + 1, :].to_broadcast([P, 4, P]),
                                                                 op0=ALU.min, op1=ALU.mult), reads=[dlt, dcbt, dmt], writes=[dmt])
                    return (mt, dmt)

                def emit_Y(di, g, mtd):
                    mt, dmt = mtd
                    X, dX = cm["X"][di]
                    for hh in range(4):
                        j = g * 4 + hh
                        py, dpy = pY[j // 8]
                        jj = j % 8
                        S.op("pe", lambda e, hh=hh, j=j, jj=jj, py=py: e.matmul(py[:, jj * 64:(jj + 1) * 64], lhsT=mt[:, hh, :], rhs=X[:, j * 64:(j + 1) * 64],
                                                                        start=(di == 0 and jj == 0), stop=(di == 1), skip_group_check=True),
                             reads=[dmt, dX], writes=[dpy])

                units = [(di, g) for di in range(2) for g in range(4)]
                mts = {}
                for k in range(len(units) + 2):
                    if k < len(units):
                        mts[k] = emit_L(*units[k])
                    if 0 <= k - 2 < len(units):
                        emit_Y(*units[k - 2], mts.pop(k - 2))
                y, dy = y_r.next()
                tmpt, dtm = t_r.next()
                for di in range(2):
                    prev, dprev = (Rh[0], dRh[0]) if di == 0 else (rbl, drbl)
                    for half in range(2):
                        po, dpo = self.ps.next()
                        for gg in range(2):
                            g = half * 2 + gg
                            S.op("pe", lambda e, g=g, gg=gg, po=po, prev=prev: e.matmul(po[:, gg * 256:(gg + 1) * 256], lhsT=xi[:, 12 + g, o:o + P],
                                                                                rhs=prev[:, g * 256:(g + 1) * 256], start=True, stop=True),
                                 reads=[dxi, dprev], writes=[dpo])
                        dst = y if di == 0 else tmpt
                        ddst = dy if di == 0 else dtm
                        S.op("dve", lambda e, po=po, half=half, dst=dst, di=di: e.tensor_tensor(
                            out=dst[:, half * 512:(half + 1) * 512].rearrange("p (h q) -> p h q", q=64),
                            in0=po[:].rearrange("p (h q) -> p h q", q=64),
                            in1=ecs[:, di * 16 + half * 8:di * 16 + half * 8 + 8].unsqueeze(2).to_broadcast([P, 8, 64]), op=ALU.mult),
                            reads=[dpo, decs, ddst], writes=[ddst])
                S.op("dve", lambda e: e.tensor_tensor(out=y[:], in0=y[:], in1=tmpt[:], op=ALU.add), reads=[dy, dtm], writes=[dy])
                for half in range(2):
                    py, dpy = pY[half]
                    S.op("dve", lambda e, half=half, py=py: e.tensor_tensor(out=y[:, half * 512:(half + 1) * 512], in0=y[:, half * 512:(half + 1) * 512],
                                                                            in1=py[:], op=ALU.add), reads=[dpy, dy], writes=[dy])
                xtok, dxt = cm["xtok"]
                S.op("dve", lambda e: e.tensor_tensor(out=v3(tmpt[:]), in0=v3(xtok[:]), in1=Dbc[:].unsqueeze(2).to_broadcast([P, 16, 64]), op=ALU.mult),
                     reads=[dxt, dpl, dtm], writes=[dtm])
                S.op("dve", lambda e: e.tensor_tensor(out=y[:], in0=y[:], in1=tmpt[:], op=ALU.add), reads=[dy, dtm], writes=[dy])
                update_state(cm, 0)
                return (y, dy, zt, dz)

            def stage2b(c, st2):
                y, dy, zt, dz = st2
                tmpt, dtm = t2_r.next()
                S.op("dve", lambda e: e.tensor_tensor(out=y[:], in0=y[:], in1=zt[:], op=ALU.mult), reads=[dy, dz], writes=[dy])
                ss, dss = ss_r.next()
                S.op("act", lambda e: e.activation(out=tmpt[:], in_=y[:], func=AF.Square, accum_out=ss[:, 0:1]), reads=[dy, dtm], writes=[dtm, dss])
                S.op("act", lambda e: e.activation(out=ss[:, 1:2], in_=ss[:, 0:1], func=AF.Sqrt, bias=EPS, scale=1.0 / D), reads=[dss], writes=[dss])
                S.op("dve", lambda e: e.reciprocal(out=ss[:, 1:2], in_=ss[:, 1:2]), reads=[dss], writes=[dss])
                yb, dyb = yb_r.next()
                S.op("dve", lambda e: e.scalar_tensor_tensor(out=yb[:], in0=y[:], scalar=ss[:, 1:2], in1=normw[:], op0=ALU.mult, op1=ALU.mult),
                     reads=[dy, dss, dpl], writes=[dyb])
                return (yb, dyb)

            pres = {0: prefetch(0), 1: prefetch(1)}
            cms = {0: stage1(0, pres.pop(0))}
            stage1b(cms[0])
            st2s, ybs = {}, {}
            for c in range(NT + 2):
                if c + 1 < NT:
                    cms[c + 1] = stage1(c + 1, pres.pop(c + 1))

                if c < NT:
                    st2s[c] = stage2(c, cms.pop(c))
                if c + 1 < NT:
                    stage1b(cms[c + 1])
                if 0 <= c - 1 < NT:
                    ybs[c - 1] = stage2b(c - 1, st2s.pop(c - 1))
                if 0 <= c - 2 < NT:
                    yb, dyb = ybs.pop(c - 2)
                    self.store_T(yb, dyb, yT_r, 0, c - 2)
                if c + 2 < NT:
                    pres[c + 2] = prefetch(c + 2)
            S.barrier()

    def store_T(self, yb, dyb, yT_r, row0, c):
        S = self.S
        yT, dyT = yT_r.next()
        pt, dpt = self.ps.next()
        ptb = pt[:].bitcast(BF16)
        for k in range(8):
            S.op("pe", lambda e, k=k: e.transpose(out=ptb[:, k * P:(k + 1) * P], in_=yb[:, k * P:(k + 1) * P], identity=self.identb[:]),
                 reads=[dyb, self.dconst], writes=[dpt])
        S.op("act", lambda e: e.activation(out=yT[:], in_=ptb.rearrange("p (k t) -> p k t", k=8), func=AF.Copy), reads=[dpt], writes=[dyT])
        S.dma("act", self.YCATT[row0:row0 + D, :].rearrange("(k p) t -> p k t", p=P)[:, :, c * P:(c + 1) * P], yT[:], reads=[dyT])

    def s5(self, L, extra=()):
        nc, S, p = self.nc, self.S, self.prm
        dc = self.dconst
        TWO_PI = float(2 * np.pi)
        NQ = 32
        Y5 = self.scr.get("Y5")
        if Y5 is None:
            Y5 = self.dram("Y5", [T, D], F32)
        with ExitStack() as st:
            sb = lambda n, shp, dt=F32: st.enter_context(nc.sbuf_tensor(uname(n), shp, dt))
            dpp = Dep()

            def ew(eng, fn):
                S.op(eng, fn, reads=[dpp], writes=[dpp])

            tstk = [None]
            tb = lambda n, shp, dt=F32: tstk[0].enter_context(nc.sbuf_tensor(uname(n), shp, dt))

            def trig(x, shape, name):
                xi = tb(name + "_xi", shape, I32)
                fr = tb(name + "_fr", shape)
                sn = tb(name + "_sn", shape)
                cs_ = tb(name + "_cs", shape)
                ew("dve", lambda e: e.tensor_copy(out=xi[:], in_=x[:]))
                ew("dve", lambda e: e.tensor_copy(out=fr[:], in_=xi[:]))
                ew("dve", lambda e: e.tensor_tensor(out=fr[:], in0=x[:], in1=fr[:], op=ALU.subtract))
                ew("act", lambda e: e.activation(out=sn[:], in_=fr[:], func=AF.Sin, scale=TWO_PI))
                ew("act", lambda e: e.activation(out=fr[:], in_=fr[:], func=AF.Abs))
                ew("act", lambda e: e.activation(out=cs_[:], in_=fr[:], func=AF.Sin, scale=-TWO_PI, bias=float(np.pi / 2)))
                return sn, cs_

            sidx_i = sb("sidx_i", [P, NQ, 8], I32)
            sidx = sb("sidx", [P, NQ, 8])
            ew("pool", lambda e: e.iota(sidx_i[:], pattern=[[0, NQ], [1, 8]], base=0, channel_multiplier=0))
            ew("dve", lambda e: e.tensor_copy(out=sidx[:], in_=sidx_i[:]))
            cidx_i = sb("cidx_i", [P, 512], I32)
            cidx = sb("cidx", [P, 512])
            ew("pool", lambda e: e.iota(cidx_i[:], pattern=[[1, 512]], base=0, channel_multiplier=0))
            ew("dve", lambda e: e.tensor_copy(out=cidx[:], in_=cidx_i[:]))
            mF, mB = sb("mF", [P, 8, 16]), sb("mB", [P, 8, 16])
            mi = sb("mi", [P, 8, 16], I32)
            ew("pool", lambda e: e.iota(mi[:], pattern=[[16, 8], [0, 16]], base=15, channel_multiplier=-1))
            ew("dve", lambda e: e.tensor_single_scalar(out=mF[:], in_=mi[:], scalar=0, op=ALU.is_ge))
            ew("pool", lambda e: e.iota(mi[:], pattern=[[-16, 8], [0, 16]], base=0, channel_multiplier=1))
            ew("dve", lambda e: e.tensor_single_scalar(out=mB[:], in_=mi[:], scalar=0, op=ALU.is_ge))
            dcol = sb("dcol", [P, 64])
            for s_ in range(8):
                S.dma("sp", dcol[s_ * 16:(s_ + 1) * 16, :], p["s5_d"][L].rearrange("(g h) -> h g", h=16), writes=[dpp], allow_slow_non_contiguous=True)

            def do_T(raw, outt, n_inner, srcf):
                for j0 in range(0, n_inner, 8):
                    nj = min(8, n_inner - j0)
                    pt, dpt = self.ps.next()
                    for jj in range(nj):
                        S.op("pe", lambda e, jj=jj, src=srcf(j0 + jj), pt=pt: e.transpose(out=pt[:, jj * NQ:(jj + 1) * NQ], in_=src, identity=self.ident[0:NQ, 0:NQ]),
                             reads=[dpp, dc], writes=[dpt])
                    S.op("dve", lambda e, pt=pt, j0=j0, nj=nj: e.tensor_copy(
                        out=outt[:, :, j0:j0 + nj].rearrange("p q j -> p j q"), in_=pt[:, 0:nj * NQ].rearrange("p (j q) -> p j q", j=nj)),
                        reads=[dpt, dpp], writes=[dpp])

            pers = []
            for d_ in range(2):
                pers.append(dict(
                    Cre=sb("Cre%d" % d_, [P, NQ, 16]), Cim=sb("Cim%d" % d_, [P, NQ, 16]),
                    E=[sb("E%d_%d" % (i, d_), [P, NQ, 8]) for i in range(4)],
                    Bbr=sb("Bbr%d" % d_, [P, NQ, 16]), Bbi=sb("Bbi%d" % d_, [P, NQ, 16]),
                    f8=sb("f8_%d" % d_, [P, NQ]), rho8=sb("rho8_%d" % d_, [P, NQ])))
            bstk = ExitStack()
            tstk[0] = bstk
            Bt = []
            for nm in ("s5_b_re", "s5_b_im"):
                raw = tb(nm + "_raw", [NQ, 2048])
                outt = tb(nm + "_T", [P, NQ, 16])
                S.dma("sp", raw[:], p[nm][L].rearrange("(q a) p h -> q (a p h)", a=2), writes=[dpp])
                do_T(raw, outt, 16, lambda j, raw=raw: raw[:].rearrange("q (ap j) -> q j ap", j=16)[:, j, :])
                Bt.append(outt)
            Bre, Bim = Bt
            PR = []
            for d_ in range(2):
                dstk = ExitStack()
                tstk[0] = dstk
                lrli = []
                for nm in ("s5_a_re", "s5_a_im"):
                    raw = tb(nm + "_raw", [NQ, P])
                    outt = tb(nm + "_T", [P, NQ, 1])
                    S.dma("sp", raw[:], p[nm][L, d_].rearrange("(q a) p -> q (a p)", a=2), writes=[dpp])
                    do_T(raw, outt, 1, lambda j, raw=raw: raw[:, :])
                    lrli.append(outt)
                for nm, key in (("s5_c_re", "Cre"), ("s5_c_im", "Cim")):
                    raw = tb(nm + "_raw", [NQ, 16, 2, 64])
                    for a_ in range(2):
                        S.dma("sp", raw[:, :, a_, :], p[nm][L, d_][a_::2], writes=[dpp])
                    do_T(raw, pers[d_][key], 16, lambda j, raw=raw: raw[:, j, :, :].rearrange("q a p -> q (a p)"))
                Cre, Cim = pers[d_]["Cre"], pers[d_]["Cim"]
                lr2, li2 = lrli[0][:, :, 0], lrli[1][:, :, 0]
                stp = tb("stp%d" % d_, [P, NQ])
                lsrow = p["s5_log_step"][L, d_:d_ + 1, :]
                for a_ in range(2):
                    S.dma("sp", stp[a_ * 64:(a_ + 1) * 64, :], lsrow[:, a_::2].partition_broadcast(64).rearrange("p o n -> p (o n)"),
                          writes=[dpp], allow_slow_non_contiguous=True)
                ew("act", lambda e: e.activation(out=stp[:], in_=stp[:], func=AF.Exp))
                lrs, f1 = tb("lrs%d" % d_, [P, NQ]), tb("f1%d" % d_, [P, NQ])
                ew("dve", lambda e: e.tensor_tensor(out=lrs[:], in0=lr2, in1=stp[:], op=ALU.mult))
                ew("dve", lambda e: e.tensor_tensor(out=f1[:], in0=li2, in1=stp[:], op=ALU.mult))
                ew("dve", lambda e: e.tensor_scalar(out=f1[:], in0=f1[:], scalar1=1.0 / TWO_PI, scalar2=None, op0=ALU.mult))
                xs, ms = tb("xs%d" % d_, [P, NQ, 8]), tb("ms%d" % d_, [P, NQ, 8])
                ew("dve", lambda e: e.tensor_tensor(out=xs[:], in0=sidx[:], in1=f1[:].unsqueeze(2).to_broadcast([P, NQ, 8]), op=ALU.mult))
                ew("dve", lambda e: e.tensor_tensor(out=ms[:], in0=sidx[:], in1=lrs[:].unsqueeze(2).to_broadcast([P, NQ, 8]), op=ALU.mult))
                sn, cs_ = trig(xs, [P, NQ, 8], "tg%d" % d_)
                magp, magm = tb("magp%d" % d_, [P, NQ, 8]), tb("magm%d" % d_, [P, NQ, 8])
                ew("act", lambda e: e.activation(out=magp[:], in_=ms[:], func=AF.Exp))
                ew("act", lambda e: e.activation(out=magm[:], in_=ms[:], func=AF.Exp, scale=-1.0))
                Erp, Eip, Erm, Eim = pers[d_]["E"]
                ew("dve", lambda e: e.tensor_tensor(out=Erp[:], in0=magp[:], in1=cs_[:], op=ALU.mult))
                ew("dve", lambda e: e.tensor_tensor(out=Eip[:], in0=magp[:], in1=sn[:], op=ALU.mult))
                ew("dve", lambda e: e.tensor_tensor(out=Erm[:], in0=magm[:], in1=cs_[:], op=ALU.mult))
                ew("dve", lambda e: e.scalar_tensor_tensor(out=Eim[:], in0=magm[:], scalar=-1.0, in1=sn[:], op0=ALU.mult, op1=ALU.mult))
                nr, den, cre, cim, t0 = (tb("c%d_%d" % (i, d_), [P, NQ]) for i in range(5))
                lbr, lbi = Erp[:, :, 1], Eip[:, :, 1]
                ew("dve", lambda e: e.tensor_scalar(out=nr[:], in0=lbr, scalar1=-1.0, scalar2=None, op0=ALU.add))
                ew("dve", lambda e: e.tensor_tensor(out=den[:], in0=lr2, in1=lr2, op=ALU.mult))
                ew("dve", lambda e: e.tensor_tensor(out=t0[:], in0=li2, in1=li2, op=ALU.mult))
                ew("dve", lambda e: e.tensor_tensor(out=den[:], in0=den[:], in1=t0[:], op=ALU.add))
                ew("dve", lambda e: e.reciprocal(out=den[:], in_=den[:]))
                ew("dve", lambda e: e.tensor_tensor(out=cre[:], in0=nr[:], in1=lr2, op=ALU.mult))
                ew("dve", lambda e: e.tensor_tensor(out=t0[:], in0=lbi, in1=li2, op=ALU.mult))
                ew("dve", lambda e: e.tensor_tensor(out=cre[:], in0=cre[:], in1=t0[:], op=ALU.add))
                ew("dve", lambda e: e.tensor_tensor(out=cre[:], in0=cre[:], in1=den[:], op=ALU.mult))
                ew("dve", lambda e: e.tensor_tensor(out=cim[:], in0=lbi, in1=lr2, op=ALU.mult))
                ew("dve", lambda e: e.tensor_tensor(out=t0[:], in0=nr[:], in1=li2, op=ALU.mult))
                ew("dve", lambda e: e.tensor_tensor(out=cim[:], in0=cim[:], in1=t0[:], op=ALU.subtract))
                ew("dve", lambda e: e.tensor_tensor(out=cim[:], in0=cim[:], in1=den[:], op=ALU.mult))
                Bbr, Bbi, t1 = pers[d_]["Bbr"], pers[d_]["Bbi"], tb("t1_%d" % d_, [P, NQ, 16])
                bc = lambda t_: t_[:].unsqueeze(2).to_broadcast([P, NQ, 16])
                ew("dve", lambda e: e.tensor_tensor(out=Bbr[:], in0=Bre[:], in1=bc(cre), op=ALU.mult))
                ew("dve", lambda e: e.tensor_tensor(out=t1[:], in0=Bim[:], in1=bc(cim), op=ALU.mult))
                ew("dve", lambda e: e.tensor_tensor(out=Bbr[:], in0=Bbr[:], in1=t1[:], op=ALU.subtract))
                ew("dve", lambda e: e.tensor_tensor(out=Bbi[:], in0=Bim[:], in1=bc(cre), op=ALU.mult))
                ew("dve", lambda e: e.tensor_tensor(out=t1[:], in0=Bre[:], in1=bc(cim), op=ALU.mult))
                ew("dve", lambda e: e.tensor_tensor(out=Bbi[:], in0=Bbi[:], in1=t1[:], op=ALU.add))
                f8, rho8 = pers[d_]["f8"], pers[d_]["rho8"]
                f8i = tb("f8i_%d" % d_, [P, NQ], I32)
                ew("dve", lambda e: e.tensor_scalar(out=f8[:], in0=f1[:], scalar1=8.0, scalar2=None, op0=ALU.mult))
                ew("dve", lambda e: e.tensor_copy(out=f8i[:], in_=f8[:]))
                ew("dve", lambda e: e.tensor_copy(out=t0[:], in_=f8i[:]))
                ew("dve", lambda e: e.tensor_tensor(out=f8[:], in0=f8[:], in1=t0[:], op=ALU.subtract))
                ew("act", lambda e: e.activation(out=rho8[:], in_=lrs[:], func=AF.Exp, scale=8.0))
                Ex = (Erm, Eim) if d_ == 0 else (Erp, Eip)
                Ey = (Erp, Eip) if d_ == 0 else (Erm, Eim)
                PR.append(dict(Ex=Ex, Ey=Ey, Bbr=Bbr, Bbi=Bbi, Cre=Cre, Cim=Cim, f8=f8, rho8=rho8))
                S.barrier()
                dstk.close()
            S.barrier()
            bstk.close()

            if STOP == "s5prep":
                return
            Ug_r = Ring(nc, st, "s5_Ug", 2, [P, 8, 512], BF16)
            utr = Ring(nc, st, "s5_ut", 2, [P, 8, P], BF16)
            urr = Ring(nc, st, "s5_ur", 8, [P, 8, P], BF16)
            U4 = self.U.rearrange("(cb c l) ch -> cb c l ch", cb=4, c=P, l=8)
            mats = {}
            for nm in ("Xr", "Xi", "Yr", "Yi"):
                for d_ in range(2):
                    mats[nm, d_] = sb("m%s%d" % (nm, d_), [P, 4, 8, 16])
            dmat = Dep()
            dmatb = Dep()
            matsb2 = [{k: sb("mb%d%s%d" % ((par,) + k), [P, 4, 8, 16], BF16) for k in mats} for par in range(2)]
            tmpm = sb("tmpm", [P, 4, 8, 16])
            Yacc = sb("Yacc", [P, 4, 8, P])
            dYacc = Dep()
            Tg_r = Ring(nc, st, "s5_T", 8, [P, P], BF16)
            Tf_r = Ring(nc, st, "s5_Tf", 3, [P, P], F32)
            winz = [Ring(nc, st, "s5_winz%d" % a_, 2, [P, 2, P], BF16) for a_ in range(2)]
            for a_ in range(2):
                for r_ in (winz[a_],):
                    for t_, d_ in zip(r_.t, r_.d):
                        S.op("pool", lambda e, t_=t_: e.memset(t_[:], 0.0), writes=[d_])
            cos_r = Ring(nc, st, "s5_cos", 6, [P, 512], BF16)
            sin_r = Ring(nc, st, "s5_sin", 6, [P, 512], BF16)
            tr_r = Ring(nc, st, "s5_tr", 2, [P, 512], F32)
            tri_r = Ring(nc, st, "s5_tri", 2, [P, 512], I32)
            gt_r = Ring(nc, st, "s5_gt", 12, [P, 512], BF16)
            gb_r = Ring(nc, st, "s5_gb", 4, [P, 512], BF16)
            tt_r = Ring(nc, st, "s5_tt", 4, [P, 512], BF16)
            dd = [sb("s5_d%d" % i, [P, 512], BF16) for i in range(4)]
            ddd = [Dep() for _ in range(4)]
            for i in range(4):
                S.op("pool", lambda e, i=i: e.memset(dd[i][:], 0.0), writes=[ddd[i]])
            DD_r = Ring(nc, st, "s5_DD", 6, [P, 2, 512], BF16)
            yg_r = Ring(nc, st, "s5_yg", 2, [P, 512], F32)
            Y54 = Y5.rearrange("(cb c l) ch -> c cb l ch", cb=4, c=P, l=8)
            bstate = {}

            utq = {}

            def batch_load(gb):
                lst = []
                for cb in range(4):
                    ur, dur = urr.next()
                    S.dma("sp", ur[:], U4[cb][:, :, gb * P:(gb + 1) * P], writes=[dur])
                    lst.append((ur, dur))
                utq[gb] = lst

            def batch_prep(gb):
                Ug, dUg = Ug_r.next()
                uts = utq.pop(gb)
                if gb + 1 < 8:
                    batch_load(gb + 1)
                for cb in range(4):
                    ur, dur = uts[cb]
                    ut, dut = utr.next()
                    S.op("act", lambda e, ur=ur, ut=ut: e.activation(out=ut[:].rearrange("p g (l h) -> p g l h", h=16),
                                                                     in_=ur[:].rearrange("p l (g h) -> p g l h", h=16), func=AF.Copy),
                         reads=[dur, dut], writes=[dut])
                    pt, dpt = self.ps.next()
                    ptb = pt[:].bitcast(BF16)
                    for g_ in range(8):
                        S.op("pe", lambda e, g_=g_, ptb=ptb, ut=ut: e.transpose(out=ptb[:, g_ * P:(g_ + 1) * P],
                                                                            in_=ut[:].rearrange("p a b -> p (a b)")[:, g_ * P:(g_ + 1) * P],
                                                                            identity=self.identb[:]), reads=[dut, dc], writes=[dpt])
                    S.op("act", lambda e, ptb=ptb, cb=cb, Ug=Ug: e.activation(out=Ug[:, :, cb * P:(cb + 1) * P],
                                                                          in_=ptb.rearrange("p (g c) -> p g c", g=8), func=AF.Copy),
                         reads=[dpt, dUg], writes=[dUg])
                q0 = gb * 4
                for d_ in range(2):
                    pr = PR[d_]
                    bE = lambda t_: t_[:, q0:q0 + 4, :].unsqueeze(3).to_broadcast([P, 4, 8, 16])
                    bB = lambda t_: t_[:, q0:q0 + 4, :].unsqueeze(2).to_broadcast([P, 4, 8, 16])

                    def cplx(outr, outi, Er, Ei, Br, Bi, neg_im):
                        S.op("dve", lambda e: e.tensor_tensor(out=outr[:], in0=bE(Er), in1=bB(Br), op=ALU.mult), reads=[dpp, dmat], writes=[dmat])
                        S.op("dve", lambda e: e.tensor_tensor(out=tmpm[:], in0=bE(Ei), in1=bB(Bi), op=ALU.mult), reads=[dpp, dmat], writes=[dmat])
                        S.op("dve", lambda e: e.tensor_tensor(out=outr[:], in0=outr[:], in1=tmpm[:], op=ALU.subtract), reads=[dmat], writes=[dmat])
                        S.op("dve", lambda e: e.tensor_tensor(out=outi[:], in0=bE(Er), in1=bB(Bi), op=ALU.mult), reads=[dpp, dmat], writes=[dmat])
                        S.op("dve", lambda e: e.tensor_tensor(out=tmpm[:], in0=bE(Ei), in1=bB(Br), op=ALU.mult), reads=[dpp, dmat], writes=[dmat])
                        if neg_im:
                            S.op("dve", lambda e: e.scalar_tensor_tensor(out=outi[:], in0=outi[:], scalar=-1.0, in1=tmpm[:], op0=ALU.mult, op1=ALU.subtract),
                                 reads=[dmat], writes=[dmat])
                        else:
                            S.op("dve", lambda e: e.tensor_tensor(out=outi[:], in0=outi[:], in1=tmpm[:], op=ALU.add), reads=[dmat], writes=[dmat])

                    cplx(mats["Xr", d_], mats["Xi", d_], pr["Ex"][0], pr["Ex"][1], pr["Bbr"], pr["Bbi"], False)
                    cplx(mats["Yr", d_], mats["Yi", d_], pr["Ey"][0], pr["Ey"][1], pr["Cre"], pr["Cim"], True)
                matsb = matsb2[gb % 2]
                bstate["matsb"] = matsb
                for nm in ("Xr", "Xi", "Yr", "Yi"):
                    for d_ in range(2):
                        S.op("act", lambda e, nm=nm, d_=d_: e.activation(out=matsb[nm, d_][:], in_=mats[nm, d_][:], func=AF.Copy), reads=[dmat, dmatb], writes=[dmatb])
                bstate["Ug"], bstate["dUg"] = Ug, dUg

            def stageA(q):
                gb, qq = q // 4, q % 4
                if qq == 0:
                    batch_prep(gb)
                Ug, dUg = bstate["Ug"], bstate["dUg"]
                m2 = lambda nm, d_: mats[nm, d_][:, qq, :, :].rearrange("p s h -> p (s h)")
                matsb = bstate["matsb"]
                m2b = lambda nm, d_: matsb[nm, d_][:, qq, :, :].rearrange("p s h -> p (s h)")
                h = dict(Ug=Ug, dUg=dUg, gb=gb, qq=qq, T=[], mod=[], matsb=matsb)
                wzs = []
                for d_ in range(2):
                    pw, dpw = self.ps.next()
                    pwb = pw[:].bitcast(BF16)
                    S.op("pe", lambda e, d_=d_, pwb=pwb: e.transpose(out=pwb[:, 0:P], in_=m2b("Xr", d_), identity=self.identb[:]), reads=[dmatb, dc], writes=[dpw])
                    S.op("pe", lambda e, d_=d_, pwb=pwb: e.transpose(out=pwb[:, P:2 * P], in_=m2b("Xi", d_), identity=self.identb[:]), reads=[dmatb, dc], writes=[dpw])
                    wz = []
                    for a_ in range(2):
                        w_, dw_ = winz[a_].next()
                        S.op("act", lambda e, pwb=pwb, w_=w_, a_=a_: e.activation(out=w_[:, :, a_ * 64:(a_ + 1) * 64],
                                                                             in_=pwb[:, 0:2 * P].rearrange("p (r c) -> p r c", r=2)[:, :, a_ * 64:(a_ + 1) * 64], func=AF.Copy),
                             reads=[dpw, dw_], writes=[dw_])
                        wz.append((w_, dw_))
                    wzs.append(wz)
                for a_ in range(2):
                    g = 2 * q + a_
                    sl = slice(a_ * 64, (a_ + 1) * 64)
                    Tf, dTf = Tf_r.next()
                    for d_ in range(2):
                        pt, dpt = self.ps.next()
                        S.op("pe", lambda e, d_=d_, pt=pt: e.matmul(pt[:, 0:P], lhsT=m2("Xr", d_)[sl, :], rhs=m2("Yr", d_)[sl, :], start=True, stop=False), reads=[dmat], writes=[dpt])
                        S.op("pe", lambda e, d_=d_, pt=pt: e.matmul(pt[:, 0:P], lhsT=m2("Xi", d_)[sl, :], rhs=m2("Yi", d_)[sl, :], start=False, stop=True), reads=[dmat], writes=[dpt])
                        msk = mF if d_ == 0 else mB
                        if d_ == 0:
                            S.op("dve", lambda e, pt=pt, msk=msk, Tf=Tf: e.tensor_tensor(out=Tf[:], in0=pt[:, 0:P], in1=msk[:].rearrange("p l h -> p (l h)"), op=ALU.mult),
                                 reads=[dpt, dpp, dTf], writes=[dTf])
                        else:
                            tq, dtq = Tf_r.next()
                            S.op("dve", lambda e, pt=pt, msk=msk, tq=tq: e.tensor_tensor(out=tq[:], in0=pt[:, 0:P], in1=msk[:].rearrange("p l h -> p (l h)"), op=ALU.mult),
                                 reads=[dpt, dpp, dtq], writes=[dtq])
                            S.op("pool", lambda e, tq=tq, Tf=Tf: e.tensor_tensor(out=Tf[:], in0=Tf[:], in1=tq[:], op=ALU.add), reads=[dtq, dTf], writes=[dTf])
                    Tg, dTg = Tg_r.next()
                    S.op("dve", lambda e, g=g, Tg=Tg, Tf=Tf: e.scalar_tensor_tensor(out=Tg[:], in0=self.ident[:], scalar=dcol[:, g:g + 1], in1=Tf[:], op0=ALU.mult, op1=ALU.add),
                         reads=[dTf, dpp, dc, dTg], writes=[dTg])
                    h["T"].append((Tg, dTg))
                for d_ in range(2):
                    pr = PR[d_]
                    wz = wzs[d_]
                    pgr, dpgr = self.ps.next()
                    pgi, dpgi = self.ps.next()
                    for a_ in range(2):
                        w_, dw_ = wz[a_]
                        S.op("pe", lambda e, w_=w_, a_=a_: e.matmul(pgr[:], lhsT=w_[:, 0, :], rhs=Ug[:, 2 * qq + a_, :], start=(a_ == 0), stop=(a_ == 1)), reads=[dw_, dUg], writes=[dpgr])
                    for a_ in range(2):
                        w_, dw_ = wz[a_]
                        S.op("pe", lambda e, w_=w_, a_=a_: e.matmul(pgi[:], lhsT=w_[:, 1, :], rhs=Ug[:, 2 * qq + a_, :], start=(a_ == 0), stop=(a_ == 1)), reads=[dw_, dUg], writes=[dpgi])
                    tr, dtr_ = tr_r.next()
                    tri, dtri = tri_r.next()
                    cs_, dcs_ = cos_r.next()
                    sn, dsn = sin_r.next()
                    f8c = pr["f8"][:, q:q + 1]
                    S.op("act", lambda e: e.activation(out=tr[:], in_=cidx[:], func=AF.Copy, scale=f8c), reads=[dpp, dtr_], writes=[dtr_])
                    S.op("dve", lambda e: e.tensor_copy(out=tri[:], in_=tr[:]), reads=[dtr_, dtri], writes=[dtri])
                    S.op("dve", lambda e: e.tensor_tensor(out=tr[:], in0=tr[:], in1=tri[:], op=ALU.subtract), reads=[dtr_, dtri], writes=[dtr_])
                    S.op("act", lambda e: e.activation(out=sn[:], in_=tr[:], func=AF.Sin, scale=TWO_PI), reads=[dtr_, dsn], writes=[dsn])
                    S.op("act", lambda e: e.activation(out=tr[:], in_=tr[:], func=AF.Abs), reads=[dtr_, dsn], writes=[dtr_])
                    S.op("act", lambda e: e.activation(out=cs_[:], in_=tr[:], func=AF.Sin, scale=-TWO_PI, bias=float(np.pi / 2)), reads=[dtr_, dcs_], writes=[dcs_])
                    rv = (lambda ap: ap) if d_ == 0 else (lambda ap: ap[:, ::-1])
                    grb, dgrb = gb_r.next()
                    gib, dgib = gb_r.next()
                    S.op("act", lambda e: e.activation(out=grb[:], in_=pgr[:], func=AF.Copy), reads=[dpgr, dgrb], writes=[dgrb])
                    S.op("act", lambda e: e.activation(out=gib[:], in_=pgi[:], func=AF.Copy), reads=[dpgi, dgib], writes=[dgib])
                    gre, dgre = gt_r.next()
                    gim, dgim = gt_r.next()
                    ta, dta = tt_r.next()
                    tb_, dtb = tt_r.next()
                    S.op("dve", lambda e: e.tensor_tensor(out=gre[:], in0=rv(grb[:]), in1=cs_[:], op=ALU.mult), reads=[dgrb, dcs_, dgre], writes=[dgre])
                    S.op("dve", lambda e: e.tensor_tensor(out=ta[:], in0=rv(gib[:]), in1=sn[:], op=ALU.mult), reads=[dgib, dsn, dta], writes=[dta])
                    S.op("dve", lambda e: e.tensor_tensor(out=gre[:], in0=gre[:], in1=ta[:], op=ALU.add), reads=[dgre, dta], writes=[dgre])
                    S.op("dve", lambda e: e.tensor_tensor(out=gim[:], in0=rv(gib[:]), in1=cs_[:], op=ALU.mult), reads=[dgib, dcs_, dgim], writes=[dgim])
                    S.op("dve", lambda e: e.tensor_tensor(out=tb_[:], in0=rv(grb[:]), in1=sn[:], op=ALU.mult), reads=[dgrb, dsn, dtb], writes=[dtb])
                    S.op("dve", lambda e: e.tensor_tensor(out=gim[:], in0=gim[:], in1=tb_[:], op=ALU.subtract), reads=[dgim, dtb], writes=[dgim])
                    h["mod"].append(dict(gre=(gre, dgre), gim=(gim, dgim), cs=(cs_, dcs_), sn=(sn, dsn), rv=rv))
                return h

            def stageB(q, h):
                gb, qq = h["gb"], h["qq"]
                Ug, dUg = h["Ug"], h["dUg"]
                matsb = h["matsb"]
                DDs = []
                for d_ in range(2):
                    pr = PR[d_]
                    md = h["mod"][d_]
                    gre, dgre = md["gre"]
                    gim, dgim = md["gim"]
                    cs_, dcs_ = md["cs"]
                    sn, dsn = md["sn"]
                    rv = md["rv"]
                    rho = pr["rho8"][:, q:q + 1].to_broadcast([P, 511])
                    dre, dim_ = dd[d_ * 2], dd[d_ * 2 + 1]
                    S.op("dve", lambda e: e.tensor_tensor_scan(out=dre[:, 1:512], data0=gre[:, 0:511], data1=rho, initial=0.0, op0=ALU.add, op1=ALU.mult),
                         reads=[dgre, dpp, ddd[d_ * 2]], writes=[ddd[d_ * 2]])
                    S.op("dve", lambda e: e.tensor_tensor_scan(out=dim_[:, 1:512], data0=gim[:, 0:511], data1=rho, initial=0.0, op0=ALU.add, op1=ALU.mult),
                         reads=[dgim, dpp, ddd[d_ * 2 + 1]], writes=[ddd[d_ * 2 + 1]])
                    DD, dDD = DD_r.next()
                    ta2, dta2 = tt_r.next()
                    tb2, dtb2 = tt_r.next()
                    S.op("dve", lambda e: e.tensor_tensor(out=ta2[:], in0=dre[:], in1=cs_[:], op=ALU.mult), reads=[ddd[d_ * 2], dcs_, dta2], writes=[dta2])
                    S.op("dve", lambda e: e.tensor_tensor(out=tb2[:], in0=dim_[:], in1=sn[:], op=ALU.mult), reads=[ddd[d_ * 2 + 1], dsn, dtb2], writes=[dtb2])
                    S.op("dve", lambda e: e.tensor_tensor(out=rv(DD[:, 0, :]), in0=ta2[:], in1=tb2[:], op=ALU.subtract), reads=[dta2, dtb2, dDD], writes=[dDD])
                    S.op("dve", lambda e: e.tensor_tensor(out=ta2[:], in0=dim_[:], in1=cs_[:], op=ALU.mult), reads=[ddd[d_ * 2 + 1], dcs_, dta2], writes=[dta2])
                    S.op("dve", lambda e: e.tensor_tensor(out=tb2[:], in0=dre[:], in1=sn[:], op=ALU.mult), reads=[ddd[d_ * 2], dsn, dtb2], writes=[dtb2])
                    S.op("dve", lambda e: e.tensor_tensor(out=rv(DD[:, 1, :]), in0=ta2[:], in1=tb2[:], op=ALU.add), reads=[dta2, dtb2, dDD], writes=[dDD])
                    DDs.append((DD, dDD))
                h["DDs"] = DDs

            def stageC(q, h):
                gb, qq = h["gb"], h["qq"]
                Ug, dUg = h["Ug"], h["dUg"]
                matsb = h["matsb"]
                DDs = h["DDs"]
                pys = []
                for a_ in range(2):
                    gg = 2 * qq + a_
                    Tg, dTg = h["T"][a_]
                    py, dpy = self.ps.next()
                    S.op("pe", lambda e, gg=gg, Tg=Tg: e.matmul(py[:], lhsT=Tg[:], rhs=Ug[:, gg, :], start=True, stop=False), reads=[dTg, dUg], writes=[dpy])
                    sl = slice(a_ * 64, (a_ + 1) * 64)
                    for d_ in range(2):
                        DD, dDD = DDs[d_]
                        S.op("pe", lambda e, DD=DD, d_=d_: e.matmul(py[:], lhsT=matsb["Yr", d_][sl, qq, :, :].rearrange("p s h -> p (s h)"), rhs=DD[sl, 0, :],
                                                                start=False, stop=False), reads=[dDD, dmatb], writes=[dpy])
                        S.op("pe", lambda e, DD=DD, d_=d_: e.matmul(py[:], lhsT=matsb["Yi", d_][sl, qq, :, :].rearrange("p s h -> p (s h)"), rhs=DD[sl, 1, :],
                                                                start=False, stop=(d_ == 1)), reads=[dDD, dmatb], writes=[dpy])
                    pys.append((py, dpy))
                for a_ in range(2):
                    gg = 2 * qq + a_
                    py, dpy = pys[a_]
                    yg, dyg = yg_r.next()
                    S.op("act", lambda e: e.activation(out=yg[:], in_=py[:], func=AF.Copy), reads=[dpy, dyg], writes=[dyg])
                    pb, dpb = self.ps.next()
                    for cb in range(4):
                        S.op("pe", lambda e, cb=cb: e.transpose(out=pb[:, cb * P:(cb + 1) * P], in_=yg[:, cb * P:(cb + 1) * P], identity=self.ident[:]),
                             reads=[dyg, dc], writes=[dpb])
                    S.op("dve", lambda e, gg=gg: e.tensor_copy(out=Yacc[:, :, :, gg * 16:(gg + 1) * 16],
                                                               in_=pb[:].rearrange("p (cb l h) -> p cb l h", cb=4, l=8)), reads=[dpb, dYacc], writes=[dYacc])
                if qq == 3:
                    for cb in range(4):
                        S.dma("act", Y54[:, cb, :, gb * P:(gb + 1) * P], Yacc[:, cb, :, :], reads=[dYacc])

            batch_load(0)
            hs = [stageA(0), stageA(1)]
            extra = list(extra)
            per = (len(extra) + NQ - 1) // NQ
            for q in range(NQ + 1):
                for _ in range(per):
                    if extra:
                        extra.pop(0)()
                if q + 2 < NQ:
                    hs.append(stageA(q + 2))
                if q < NQ:
                    stageB(q, hs[q])
                if q >= 1:
                    stageC(q - 1, hs[q - 1])
                    hs[q - 1] = None
            S.barrier()
        if self.dbg == "Y5" or STOP == "s5main":
            return
        self.glu(L, Y5)

    def glu(self, L, Y5):
        nc, S, p = self.nc, self.S, self.prm
        with ExitStack() as st:
            sb = lambda n, shp, dt=F32: st.enter_context(nc.sbuf_tensor(uname(n), shp, dt))
            W = sb("gl_w", [P, 8, D], BF16)
            dW = Dep()
            W3 = self.WGLUB.rearrange("(k p) n -> p k n", p=P)
            for q in range(2):
                S.dma("sp", W[:, q * 4:(q + 1) * 4, :], W3[:, q * 4:(q + 1) * 4, :], writes=[dW])
            bg, nw = sb("gl_b", [P, D]), sb("gl_nw", [P, D])
            dpl = Dep()
            S.dma("sp", bg[:], bcast_row(p["b_glu"][L:L + 1, :], D), writes=[dpl])
            S.dma("sp", nw[:], bcast_row(p["s5_norm_w"][L:L + 1, :], D), writes=[dpl])
            yr = Ring(nc, st, "gl_y", 6, [P, D], F32)
            gbr = Ring(nc, st, "gl_gb", 3, [P, D], BF16)
            gTr = Ring(nc, st, "gl_gT", 3, [P, 8, P], BF16)
            sr = Ring(nc, st, "gl_s", 2, [P, D], F32)
            ss_r = Ring(nc, st, "gl_ss", 2, [P, 2], F32)
            yb_r = Ring(nc, st, "gl_yb", 3, [P, D], BF16)
            yT_r = Ring(nc, st, "gl_yT", 2, [P, 8, P], BF16)
            def gA(tt):
                y, dy = yr.next()
                S.dma("sp", y[:], Y5[tt * P:(tt + 1) * P, :], writes=[dy])
                return (y, dy)

            def gB(tt, hA):
                y, dy = hA
                S.op("act", lambda e: e.activation(out=y[:], in_=y[:], func=AF.Gelu), reads=[dy], writes=[dy])
                gb_, dgb_ = gbr.next()
                S.op("dve", lambda e: e.tensor_copy(out=gb_[:], in_=y[:]), reads=[dy, dgb_], writes=[dgb_])
                return (y, dy, gb_, dgb_)

            def gC(tt, hB):
                y, dy, gb_, dgb_ = hB
                pt, dpt = self.ps.next()
                ptb = pt[:].bitcast(BF16)
                for k in range(8):
                    S.op("pe", lambda e, k=k: e.transpose(out=ptb[:, k * P:(k + 1) * P], in_=gb_[:, k * P:(k + 1) * P], identity=self.identb[:]),
                         reads=[dgb_, self.dconst], writes=[dpt])
                gT, dgT = gTr.next()
                S.op("act", lambda e: e.activation(out=gT[:], in_=ptb.rearrange("p (k t) -> p k t", k=8), func=AF.Copy), reads=[dpt, dgT], writes=[dgT])
                return (y, dy, gT, dgT)

            def g2(tt, h1):
                y, dy, gT, dgT = h1
                sg, dsg = sr.next()
                for nb in range(2):
                    pm, dpm = self.ps.next()
                    for k in range(8):
                        S.op("pe", lambda e, k=k, pm=pm, nb=nb: e.matmul(pm[:], lhsT=gT[:, k, :], rhs=W[:, k, nb * 512:(nb + 1) * 512],
                                                                   start=(k == 0), stop=(k == 7)), reads=[dgT, dW], writes=[dpm])
                    S.op("dve", lambda e, pm=pm, nb=nb: e.tensor_tensor(out=sg[:, nb * 512:(nb + 1) * 512], in0=pm[:], in1=bg[:, nb * 512:(nb + 1) * 512], op=ALU.add),
                         reads=[dpm, dpl, dsg], writes=[dsg])
                S.op("act", lambda e: e.activation(out=sg[:], in_=sg[:], func=AF.Tanh, scale=0.5), reads=[dsg], writes=[dsg])
                S.op("dve", lambda e: e.scalar_tensor_tensor(out=y[:], in0=sg[:], scalar=1.0, in1=y[:], op0=ALU.add, op1=ALU.mult), reads=[dy, dsg], writes=[dy])
                ss, dss = ss_r.next()
                S.op("act", lambda e: e.activation(out=sg[:], in_=y[:], func=AF.Square, accum_out=ss[:, 0:1]), reads=[dy, dsg], writes=[dsg, dss])
                S.op("act", lambda e: e.activation(out=ss[:, 1:2], in_=ss[:, 0:1], func=AF.Sqrt, bias=4.0 * EPS, scale=1.0 / D), reads=[dss], writes=[dss])
                S.op("dve", lambda e: e.reciprocal(out=ss[:, 1:2], in_=ss[:, 1:2]), reads=[dss], writes=[dss])
                yb, dyb = yb_r.next()
                S.op("dve", lambda e: e.scalar_tensor_tensor(out=yb[:], in0=y[:], scalar=ss[:, 1:2], in1=nw[:], op0=ALU.mult, op1=ALU.mult),
                     reads=[dy, dss, dpl, dyb], writes=[dyb])
                return (yb, dyb)

            hq = {}
            for i in range(NT + 5):
                if i < NT:
                    hq["A", i] = gA(i)
                if 0 <= i - 1 < NT:
                    hq["B", i - 1] = gB(i - 1, hq.pop(("A", i - 1)))
                if 0 <= i - 2 < NT:
                    hq["C", i - 2] = gC(i - 2, hq.pop(("B", i - 2)))
                if 0 <= i - 3 < NT:
                    hq["2", i - 3] = g2(i - 3, hq.pop(("C", i - 3)))
                if 0 <= i - 4 < NT:
                    yb, dyb = hq.pop(("2", i - 4))
                    self.store_T(yb, dyb, yT_r, D, i - 4)
            S.barrier()

    def out_proj(self, L):
        nc, S, p = self.nc, self.S, self.prm
        with ExitStack() as st:
            W = st.enter_context(nc.sbuf_tensor(uname("op_w"), [P, 16, D], BF16))
            dW = Dep()
            W3 = self.WOUTB.rearrange("(k p) n -> p k n", p=P)
            for q in range(4):
                S.dma("sp", W[:, q * 4:(q + 1) * 4, :], W3[:, q * 4:(q + 1) * 4, :], writes=[dW])
            g_t, b_t, dgb = self.load_gb(st, p["ln1_g"][L:L + 1, :], p["ln1_b"][L:L + 1, :])
            tmp = self.ln_tmp(st)
            ar = Ring(nc, st, "op_a", 2, [P, 16, 512], BF16)
            hr = Ring(nc, st, "op_h", 3, [P, D], F32)
            xr = Ring(nc, st, "op_x", 5, [P, D], F32)
            A3 = self.YCATT.rearrange("(k p) t -> p k t", p=P)
            pend = None
            for tb in range(8):
                a, da = ar.next()
                for q in range(2):
                    S.dma("sp", a[:, q * 8:(q + 1) * 8, :], A3[:, q * 8:(q + 1) * 8, tb * 512:(tb + 1) * 512], writes=[da])
                for j in range(4):
                    tt = tb * 4 + j
                    hres, dhr = hr.next()
                    S.dma("sp", hres[:], self.H32[tt * P:(tt + 1) * P, :], writes=[dhr])
                    xt, dx = xr.next()
                    for nb in range(2):
                        pt, dpt = self.ps.next()
                        for k in range(16):
                            S.op("pe", lambda e, k=k, pt=pt, a=a, j=j, nb=nb: e.matmul(
                                pt[:], lhsT=a[:, k, j * P:(j + 1) * P], rhs=W[:, k, nb * 512:(nb + 1) * 512],
                                start=(k == 0), stop=(k == 15)), reads=[da, dW], writes=[dpt])
                        S.op("dve", lambda e, xt=xt, hres=hres, pt=pt, nb=nb: e.scalar_tensor_tensor(
                            out=xt[:, nb * 512:(nb + 1) * 512], in0=hres[:, nb * 512:(nb + 1) * 512], scalar=ALPHA, in1=pt[:],
                            op0=ALU.mult, op1=ALU.add), reads=[dhr, dpt, dx], writes=[dx])
                    if pend is not None:
                        self.ln_core(st, *pend)
                        self.ln_flush(tmp, 1)
                    pend = (xt, dx, g_t, b_t, dgb, tt, self.H32, tmp)
            self.ln_core(st, *pend)
            self.ln_flush(tmp, 0)
            S.barrier()

    def mlp(self, L):
        nc, S, p = self.nc, self.S, self.prm
        with ExitStack() as st:
            hT = st.enter_context(nc.sbuf_tensor(uname("ml_hT"), [P, 8, T], BF16))
            dh = Dep()
            for k in range(8):
                S.dma("sp", hT[:, k, :], self.HT[k * P:(k + 1) * P, :], writes=[dh])
            S.barrier()
            g_t, b_t, dgb = self.load_gb(st, p["ln2_g"][L:L + 1, :], p["ln2_b"][L:L + 1, :])
            tmp = self.ln_tmp(st)
            h1T = st.enter_context(nc.sbuf_tensor(uname("ml_h1T"), [P, 32, 512], BF16))
            dh1 = [Dep() for _ in range(32)]
            w1r = Ring(nc, st, "ml_w1", 2, [P, 8, 512], BF16)
            w2r = Ring(nc, st, "ml_w2", 3, [P, 8, 512], BF16)
            rr = Ring(nc, st, "ml_r", 3, [P, 512], BF16)
            hr = Ring(nc, st, "ml_h", 3, [P, 512], F32)
            xr = Ring(nc, st, "ml_x", 9, [P, D], F32)
            W13 = self.W1B.rearrange("(k p) n -> p k n", p=P)
            W23 = self.W2B.rearrange("(k p) n -> p k n", p=P)
            pend_ln = []
            for tb in range(8):
                for jq in range(8):
                    w1, dw1 = w1r.next()
                    S.dma("sp", w1[:], W13[:, :, jq * 512:(jq + 1) * 512], writes=[dw1])
                    for jj in range(4):
                        j = jq * 4 + jj
                        pt, dpt = self.ps.next()
                        for k in range(8):
                            S.op("pe", lambda e, k=k, pt=pt, w1=w1, jj=jj, tb=tb: e.matmul(
                                pt[:], lhsT=w1[:, k, jj * P:(jj + 1) * P], rhs=hT[:, k, tb * 512:(tb + 1) * 512],
                                start=(k == 0), stop=(k == 7)), reads=[dh, dw1], writes=[dpt])
                        r, dr = rr.next()
                        S.op("act", lambda e, r=r, pt=pt: e.activation(out=r[:], in_=pt[:], func=AF.Relu), reads=[dpt], writes=[dr])
                        eng = "pool" if j % 2 == 0 else "dve"
                        S.op(eng, lambda e, r=r, j=j: e.tensor_tensor(out=h1T[:, j, :], in0=r[:], in1=r[:], op=ALU.mult),
                             reads=[dr], writes=[dh1[j]])
                for a_ in pend_ln:
                    self.ln_core(st, *a_)
                    self.ln_flush(tmp, 1)
                self.ln_flush(tmp, 0)
                pend_ln = []
                xts = []
                for tj in range(4):
                    hres, dhr = hr.next() if False else (None, None)
                    xts.append(xr.next())
                for nb in range(2):
                    pts = [self.ps.next() for _ in range(4)]
                    for jg in range(4):
                        w2, dw2 = w2r.next()
                        S.dma("sp", w2[:], W23[:, jg * 8:(jg + 1) * 8, nb * 512:(nb + 1) * 512], writes=[dw2])
                        for tj in range(4):
                            pt, dpt = pts[tj]
                            for k in range(8):
                                kk = jg * 8 + k
                                S.op("pe", lambda e, k=k, kk=kk, pt=pt, w2=w2, tj=tj: e.matmul(
                                    pt[:], lhsT=h1T[:, kk, tj * P:(tj + 1) * P], rhs=w2[:, k, :],
                                    start=(kk == 0), stop=(kk == 31)), reads=[dh1[kk], dw2], writes=[dpt])
                    for tj in range(4):
                        tt = tb * 4 + tj
                        pt, dpt = pts[tj]
                        xt, dx = xts[tj]
                        hres, dhr = hr.next()
                        S.dma("sp", hres[:, 0:512], self.H32[tt * P:(tt + 1) * P, nb * 512:(nb + 1) * 512], writes=[dhr])
                        S.op("dve", lambda e, xt=xt, hres=hres, pt=pt, nb=nb: e.scalar_tensor_tensor(
                            out=xt[:, nb * 512:(nb + 1) * 512], in0=hres[:, 0:512], scalar=ALPHA, in1=pt[:],
                            op0=ALU.mult, op1=ALU.add), reads=[dhr, dpt, dx], writes=[dx])
                pend_ln = [(xts[tj][0], xts[tj][1], g_t, b_t, dgb, tb * 4 + tj, self.H32, tmp) for tj in range(4)]
            for a_ in pend_ln:
                self.ln_core(st, *a_)
                self.ln_flush(tmp, 1)
            self.ln_flush(tmp, 0)
            S.barrier()


SSD_ONLY = False
OVERLAP_CAST = True
STOP = None
_CACHE = {}


def get_nc(dbg=None, nlayers=DEPTH):
    key = (dbg, nlayers)
    if key not in _CACHE:
        _CACHE[key] = K(dbg, nlayers).nc
    return _CACHE[key]


def kernel(**inputs):
    nc = get_nc()
    x = np.ascontiguousarray(inputs["x"], dtype=np.float32)
    in_maps = []
    for c in range(8):
        m = {"x": x[c]}
        for n, s in PARAMS:
            m[n] = np.ascontiguousarray(inputs[n], dtype=np.float32)
        in_maps.append(m)
    res = run_bass_kernel_spmd(nc, in_maps, core_ids=list(range(8)))
    return np.stack([r["out"] for r in res.results], axis=0)
```

```python
import numpy as np
import concourse.bass as bass
import concourse.mybir as mybir
from concourse.bass_utils import run_bass_kernel_spmd
from contextlib import ExitStack

F32 = mybir.dt.float32
BF16 = mybir.dt.bfloat16
I32 = mybir.dt.int32
AF = mybir.ActivationFunctionType
ALU = mybir.AluOpType
AX = mybir.AxisListType

P = 128
T = 4096
D = 1024
NT = T // P
DEPTH = 2
CONV_CH = 2048
IN_PROJ = 4128
D_FF = 4096
ALPHA = float((2 * DEPTH) ** 0.25)
EPS = 1e-5
O_Z, O_XBC, O_DT, O_U = 0, 1024, 3072, 3104


class Dep:
    __slots__ = ("w", "r", "x")

    def __init__(self, x=False):
        self.w = None
        self.r = []
        self.x = x


class Sched:
    def __init__(self, nc, stk, n_dma=40):
        self.nc = nc
        self.eng = {"pe": nc.tensor, "act": nc.scalar, "dve": nc.vector,
                    "pool": nc.gpsimd, "sp": nc.sync}
        self.sem = {e: stk.enter_context(nc.semaphore("s_" + e)) for e in self.eng}
        self.cnt = {e: 0 for e in self.eng}
        self.seen = {e: {} for e in self.eng}
        self.dsem = [stk.enter_context(nc.semaphore("d%d" % i)) for i in range(n_dma)]
        self.dval = [0] * n_dma
        n_sw = 12
        self.dpool = {"hw": list(range(0, n_dma - n_sw)), "sw": list(range(n_dma - n_sw, n_dma))}
        self.dnext = {"hw": 0, "sw": 0}

    def _semof(self, key):
        return self.dsem[key[1]] if isinstance(key, tuple) else self.sem[key]

    def _collect(self, e, reads, writes, extra=()):
        need = {}

        def add(tok):
            if tok is None:
                return
            k, v = tok
            if k == e and e == "pe":
                return
            if self.seen[e].get(k, 0) >= v:
                return
            if need.get(k, 0) < v:
                need[k] = v

        for d in reads:
            add(d.w)
            if d.x:
                for t in d.r:
                    if t[0] != e:
                        add(t)
        for d in writes:
            add(d.w)
            for t in d.r:
                add(t)
        for t in extra:
            add(t)
        for k, v in need.items():
            self.eng[e].wait_ge(self._semof(k), v)
            self.seen[e][k] = v

    def op(self, e, fn, reads=(), writes=()):
        self._collect(e, reads, writes)
        ins = fn(self.eng[e])
        self.cnt[e] += 1
        ins.then_inc(self.sem[e], 1)
        tok = (e, self.cnt[e])
        for d in reads:
            d.r.append(tok)
        for d in writes:
            d.w = tok
            d.r = []
        return tok

    def dma(self, q, out, in_, reads=(), writes=(), **kw):
        kind = "sw" if q == "pool" else "hw"
        pool = self.dpool[kind]
        k = pool[self.dnext[kind]]
        self.dnext[kind] = (self.dnext[kind] + 1) % len(pool)
        extra = []
        if self.dval[k] > 0:
            extra.append((("d", k), self.dval[k]))
        self._collect(q, reads, writes, extra)
        self.dval[k] += 16
        self.eng[q].dma_start(out=out, in_=in_, **kw).then_inc(self.dsem[k], 16)
        tok = (("d", k), self.dval[k])
        for d in reads:
            d.r.append(tok)
        for d in writes:
            d.w = tok
            d.r = []
        return tok

    def barrier(self):
        toks = [(e, self.cnt[e]) for e in self.eng if self.cnt[e] > 0]
        toks += [(("d", k), v) for k, v in enumerate(self.dval) if v > 0]
        for e in self.eng:
            self._collect(e, (), (), toks)

    def finish(self, q="sp"):
        toks = [(e, self.cnt[e]) for e in self.eng if self.cnt[e] > 0]
        toks += [(("d", k), v) for k, v in enumerate(self.dval) if v > 0]
        self._collect(q, (), (), toks)


_UID = [0]


def uname(name):
    _UID[0] += 1
    return "%s_%d" % (name, _UID[0])


class Ring:
    def __init__(self, nc, stk, name, n, shape, dtype, psum=False):
        alloc = nc.psum_tensor if psum else nc.sbuf_tensor
        self.t = [stk.enter_context(alloc(uname(name), shape, dtype)) for i in range(n)]
        self.d = [Dep(x=psum) for _ in range(n)]
        self.i = 0

    def next(self):
        i = self.i
        self.i = (i + 1) % len(self.t)
        return self.t[i], self.d[i]


PARAMS = [
    ("ln_in_g", [D]), ("ln_in_b", [D]), ("w_in", [DEPTH, D, IN_PROJ]),
    ("conv_w", [DEPTH, 5, CONV_CH]), ("conv_b", [DEPTH, CONV_CH]),
    ("dt_bias", [DEPTH, 2, 16]), ("a_log", [DEPTH, 2, 16]), ("ssd_d", [DEPTH, 16]),
    ("ssd_norm_w", [DEPTH, 1024]),
    ("s5_a_re", [DEPTH, 2, 64, 64]), ("s5_a_im", [DEPTH, 2, 64, 64]), ("s5_log_step", [DEPTH, 2, 64]),
    ("s5_b_re", [DEPTH, 64, 64, 16]), ("s5_b_im", [DEPTH, 64, 64, 16]),
    ("s5_c_re", [DEPTH, 2, 64, 16, 64]), ("s5_c_im", [DEPTH, 2, 64, 16, 64]),
    ("s5_d", [DEPTH, 1024]), ("w_glu", [DEPTH, 1024, 1024]), ("b_glu", [DEPTH, 1024]),
    ("s5_norm_w", [DEPTH, 1024]), ("w_out", [DEPTH, 2048, 1024]),
    ("ln1_g", [DEPTH, D]), ("ln1_b", [DEPTH, D]), ("w_mlp1", [DEPTH, D, D_FF]),
    ("w_mlp2", [DEPTH, D_FF, D]), ("ln2_g", [DEPTH, D]), ("ln2_b", [DEPTH, D]),
]


def bcast_row(ap_row, n):
    return ap_row.partition_broadcast(P).rearrange("p o n -> p (o n)")


class K:
    def __init__(self, dbg=None, nlayers=DEPTH):
        self.dbg = dbg
        self.nlayers = nlayers
        nc = self.nc = bass.Bass("TRN2", target_bir_lowering=False)
        self.x = nc.dram_tensor("x", [T, D], F32, kind="ExternalInput").ap()
        self.prm = {n: nc.dram_tensor(n, s, F32, kind="ExternalInput").ap() for n, s in PARAMS}
        self.out = nc.dram_tensor("out", [T, D], F32, kind="ExternalOutput").ap()
        self.scr = {}
        with ExitStack() as stk:
            self.gstk = stk
            self.S = Sched(nc, stk)
            self.build()
            self.S.finish("sp")

    def dram(self, name, shape, dtype):
        kind = "ExternalOutput" if self.dbg == name else "Internal"
        t = self.nc.dram_tensor(name, shape, dtype, kind=kind).ap()
        self.scr[name] = t
        return t

    def build(self):
        nc, S, stk = self.nc, self.S, self.gstk
        self.H32 = self.dram("H32", [T, D], F32)
        self.HT = self.dram("HT", [D, T], BF16)
        self.WB = {}
        for L_ in range(max(self.nlayers, 1)):
            self.WB[L_] = dict(WINB=self.dram("WINB%d" % L_, [D, IN_PROJ], BF16), WGLUB=self.dram("WGLUB%d" % L_, [D, D], BF16),
                               WOUTB=self.dram("WOUTB%d" % L_, [2 * D, D], BF16), W1B=self.dram("W1B%d" % L_, [D, D_FF], BF16),
                               W2B=self.dram("W2B%d" % L_, [D_FF, D], BF16))
        self.Z = self.dram("Z", [T, D], F32)
        self.DT = self.dram("DT", [T, 32], F32)
        self.U = self.dram("U", [T, D], BF16)
        self.XBCT = self.dram("XBCT", [CONV_CH, T], BF16)
        self.YCATT = self.dram("YCATT", [2 * D, T], BF16)
        self.ps = Ring(nc, stk, "ps", 6, [P, 512], F32, psum=True)
        self.psY = Ring(nc, stk, "psY", 2, [P, 512], F32, psum=True)
        ii = stk.enter_context(nc.sbuf_tensor(uname("c_ii"), [P, P], I32))
        self.ident = stk.enter_context(nc.sbuf_tensor(uname("c_id"), [P, P], F32))
        self.identb = stk.enter_context(nc.sbuf_tensor(uname("c_idb"), [P, P], BF16))
        self.dconst = Dep()
        dii = Dep()
        S.op("pool", lambda e: e.iota(ii[:], pattern=[[1, P]], base=0, channel_multiplier=-1), writes=[dii])
        S.op("dve", lambda e: e.tensor_single_scalar(out=self.ident[:], in_=ii[:], scalar=0, op=ALU.is_equal),
             reads=[dii], writes=[self.dconst])
        S.op("dve", lambda e: e.tensor_copy(out=self.identb[:], in_=self.ident[:]), reads=[self.dconst], writes=[self.dconst])
        self.ii, self.dii = ii, dii

        with ExitStack() as cst:
            steps = self.cast_steps(cst, 0) if self.nlayers > 0 else []
            self.ln_phase(self.x, steps, self.prm["ln_in_g"].rearrange("(o n) -> o n", o=1),
                          self.prm["ln_in_b"].rearrange("(o n) -> o n", o=1), self.H32)
        for L in range(self.nlayers):
            self.layer(L)
        with ExitStack() as st:
            r = Ring(nc, st, "fo", 3, [P, D], F32)
            for tt in range(NT):
                t_, d_ = r.next()
                S.dma("sp", t_[:], self.H32[tt * P:(tt + 1) * P, :], writes=[d_])
                S.dma("act", self.out[tt * P:(tt + 1) * P, :], t_[:], reads=[d_])
            S.barrier()

    def ln_core(self, st, xt, dx, g_t, b_t, dgb, tt, h32_out, tmp):
        nc, S = self.nc, self.S
        stats, mv, rstd, dst = tmp["stats"], tmp["mv"], tmp["rstd"], tmp["dst"]
        S.op("dve", lambda e: e.bn_stats(out=stats[:, 0:6], in_=xt[:, 0:512]), reads=[dx], writes=[dst])
        S.op("dve", lambda e: e.bn_stats(out=stats[:, 6:12], in_=xt[:, 512:1024]), reads=[dx], writes=[dst])
        S.op("dve", lambda e: e.bn_aggr(out=mv[:], in_=stats[:]), reads=[dst], writes=[dst])
        S.op("act", lambda e: e.activation(out=rstd[:], in_=mv[:, 1:2], func=AF.Sqrt, bias=EPS, scale=1.0),
             reads=[dst], writes=[dst])
        S.op("dve", lambda e: e.reciprocal(out=rstd[:], in_=rstd[:]), reads=[dst], writes=[dst])
        S.op("dve", lambda e: e.tensor_scalar(out=xt[:], in0=xt[:], scalar1=mv[:, 0:1], scalar2=rstd[:, 0:1],
                                              op0=ALU.subtract, op1=ALU.mult), reads=[dx, dst], writes=[dx])
        S.op("dve", lambda e: e.tensor_tensor(out=xt[:], in0=xt[:], in1=g_t[:], op=ALU.mult), reads=[dx, dgb], writes=[dx])
        S.op("dve", lambda e: e.tensor_tensor(out=xt[:], in0=xt[:], in1=b_t[:], op=ALU.add), reads=[dx, dgb], writes=[dx])
        S.dma("act", h32_out[tt * P:(tt + 1) * P, :], xt[:], reads=[dx])
        if tmp.get("defer") is not None:
            tmp["defer"].append((xt, dx, tt))
            return
        self.ln_T(xt, dx, tt, tmp)

    def ln_flush(self, tmp, keep=0):
        q = tmp.get("defer")
        while q is not None and len(q) > keep:
            xt, dx, tt = q.pop(0)
            self.ln_T(xt, dx, tt, tmp)

    def ln_T(self, xt, dx, tt, tmp):
        S = self.S
        hT, dhT = tmp["hT"].next()
        for half in range(2):
            pt, dpt = self.ps.next()
            for j in range(4):
                k = half * 4 + j
                S.op("pe", lambda e, k=k, j=j, pt=pt: e.transpose(out=pt[:, j * P:(j + 1) * P], in_=xt[:, k * P:(k + 1) * P],
                                                                  identity=self.ident[:]),
                     reads=[dx, self.dconst], writes=[dpt])
            S.op("act", lambda e, pt=pt, half=half: e.activation(
                out=hT[:, half * 4:(half + 1) * 4, :], in_=pt[:].rearrange("p (j t) -> p j t", j=4), func=AF.Copy),
                reads=[dpt], writes=[dhT])
        S.dma("act", self.HT.rearrange("(k p) t -> p k t", p=P)[:, :, tt * P:(tt + 1) * P], hT[:], reads=[dhT])

    def ln_tmp(self, st):
        nc = self.nc
        return {
            "stats": st.enter_context(nc.sbuf_tensor(uname("ln_stats"), [P, 12], F32)),
            "mv": st.enter_context(nc.sbuf_tensor(uname("ln_mv"), [P, 2], F32)),
            "rstd": st.enter_context(nc.sbuf_tensor(uname("ln_rstd"), [P, 1], F32)),
            "dst": Dep(),
            "defer": [],
            "hT": Ring(nc, st, "ln_hT", 2, [P, 8, P], BF16),
        }

    def load_gb(self, st, g_row, b_row):
        nc, S = self.nc, self.S
        g_t = st.enter_context(nc.sbuf_tensor(uname("ln_g"), [P, D], F32))
        b_t = st.enter_context(nc.sbuf_tensor(uname("ln_b"), [P, D], F32))
        dgb = Dep()
        S.dma("sp", g_t[:], bcast_row(g_row, D), writes=[dgb])
        S.dma("sp", b_t[:], bcast_row(b_row, D), writes=[dgb])
        return g_t, b_t, dgb

    def ln_phase(self, src, extra, g_row, b_row, h32_out):
        nc, S = self.nc, self.S
        with ExitStack() as st:
            g_t, b_t, dgb = self.load_gb(st, g_row, b_row)
            tmp = self.ln_tmp(st)
            r = Ring(nc, st, "ln_x", 5, [P, D], F32)
            extra = list(extra or [])
            per = (len(extra) + NT - 1) // NT
            pend = None
            for tt in range(NT):
                xt, dx = r.next()
                S.dma("sp", xt[:], src[tt * P:(tt + 1) * P, :], writes=[dx])
                for _ in range(per):
                    if extra:
                        extra.pop(0)()
                if pend is not None:
                    self.ln_core(st, *pend)
                    self.ln_flush(tmp, 1)
                pend = (xt, dx, g_t, b_t, dgb, tt, h32_out, tmp)
            self.ln_core(st, *pend)
            self.ln_flush(tmp, 0)
            while extra:
                extra.pop(0)()
            S.barrier()

    def cast_steps(self, st, L, CH=2048, depth=3):
        nc, S, p = self.nc, self.S, self.prm
        rf = Ring(nc, st, "cw_f", depth, [P, CH], F32)
        rb = Ring(nc, st, "cw_b", depth, [P, CH], BF16)
        steps = []
        cnt = [0]

        def mk(W, WB, kt, n0, nb):
            def step():
                f, df = rf.next()
                b, db = rb.next()
                S.dma("sp", f[:, :nb], W[kt * P:(kt + 1) * P, n0:n0 + nb], writes=[df])
                S.op("pool", lambda e: e.tensor_copy(out=b[:, :nb], in_=f[:, :nb]), reads=[df, db], writes=[db])
                S.dma("pool", WB[kt * P:(kt + 1) * P, n0:n0 + nb], b[:, :nb], reads=[db])
                cnt[0] += 1
            return step

        wb = self.WB[L]
        for W, WB in ((p["w_in"][L], wb["WINB"]), (p["w_glu"][L], wb["WGLUB"]), (p["w_out"][L], wb["WOUTB"]),
                      (p["w_mlp1"][L], wb["W1B"]), (p["w_mlp2"][L], wb["W2B"])):
            Kd, N = W.shape
            for kt in range(Kd // P):
                for n0 in range(0, N, CH):
                    steps.append(mk(W, WB, kt, n0, min(CH, N - n0)))
        return steps

    def cast_w(self, W, WB):
        nc, S = self.nc, self.S
        Kd, N = W.shape
        with ExitStack() as st:
            rf = Ring(nc, st, "cw_f", 3, [P, 2048], F32)
            rb = Ring(nc, st, "cw_b", 3, [P, 2048], BF16)
            i = 0
            for kt in range(Kd // P):
                for n0 in range(0, N, 2048):
                    nb = min(2048, N - n0)
                    f, df = rf.next()
                    b, db = rb.next()
                    S.dma("sp", f[:, :nb], W[kt * P:(kt + 1) * P, n0:n0 + nb], writes=[df])
                    eng = "pool" if i % 2 == 0 else "act"
                    if eng == "pool":
                        S.op("pool", lambda e, f=f, b=b, nb=nb: e.tensor_copy(out=b[:, :nb], in_=f[:, :nb]), reads=[df], writes=[db])
                    else:
                        S.op("act", lambda e, f=f, b=b, nb=nb: e.activation(out=b[:, :nb], in_=f[:, :nb], func=AF.Copy), reads=[df], writes=[db])
                    S.dma(eng, WB[kt * P:(kt + 1) * P, n0:n0 + nb], b[:, :nb], reads=[db])
                    i += 1
            S.barrier()

    def layer(self, L):
        p = self.prm
        last = (L == self.nlayers - 1)
        wb = self.WB[L]
        self.WINB, self.WGLUB, self.WOUTB, self.W1B, self.W2B = wb["WINB"], wb["WGLUB"], wb["WOUTB"], wb["W1B"], wb["W2B"]
        if L > 0 and not OVERLAP_CAST:
            self.cast_w(p["w_in"][L], self.WINB)
            self.cast_w(p["w_glu"][L], self.WGLUB)
            self.cast_w(p["w_out"][L], self.WOUTB)
            self.cast_w(p["w_mlp1"][L], self.W1B)
            self.cast_w(p["w_mlp2"][L], self.W2B)
        if last and STOP == "cast":
            return
        self.in_proj(L)
        if self.dbg in ("Z", "DT", "U", "XBCT") or (last and STOP == "in_proj"):
            return
        self.ssd(L)
        if SSD_ONLY or (last and STOP in ("ssd", "ssdA")):
            return
        if OVERLAP_CAST and L + 1 < self.nlayers:
            with ExitStack() as cst:
                self.s5(L, self.cast_steps(cst, L + 1, CH=1024, depth=2))
        else:
            self.s5(L, [])
        if self.dbg == "Y5" or (last and STOP in ("s5", "s5prep", "s5main")):
            return
        self.out_proj(L)
        if STOP == "out_proj" and last:
            return
        self.mlp(L)

    def in_proj(self, L):
        nc, S, p = self.nc, self.S, self.prm
        with ExitStack() as st:
            hT = st.enter_context(nc.sbuf_tensor(uname("ip_hT"), [P, 8, T], BF16))
            dh = Dep()
            for k in range(8):
                S.dma("sp", hT[:, k, :], self.HT[k * P:(k + 1) * P, :], writes=[dh])
            wr = Ring(nc, st, "ip_w", 2, [P, 8, 512], BF16)
            orr = Ring(nc, st, "ip_o", 3, [P, 512], F32)
            orb = Ring(nc, st, "ip_ob16", 3, [P, 512], BF16)
            WB3 = self.WINB.rearrange("(k p) n -> p k n", p=P)
            for (c0, dst, d0) in ((O_Z, self.Z, 0), (O_Z + 512, self.Z, 512), (O_U, self.U, 0), (O_U + 512, self.U, 512)):
                w, dw = wr.next()
                S.dma("sp", w[:], WB3[:, :, c0:c0 + 512], writes=[dw])
                for tt in range(NT):
                    pt, dpt = self.ps.next()
                    for k in range(8):
                        S.op("pe", lambda e, k=k, pt=pt, w=w, tt=tt: e.matmul(pt[:], lhsT=hT[:, k, tt * P:(tt + 1) * P], rhs=w[:, k, :],
                                                                        start=(k == 0), stop=(k == 7)),
                             reads=[dh, dw], writes=[dpt])
                    o, do = (orr if dst is self.Z else orb).next()
                    eng = "dve" if tt % 2 == 0 else "act"
                    if eng == "dve":
                        S.op("dve", lambda e, o=o, pt=pt: e.tensor_copy(out=o[:], in_=pt[:]), reads=[dpt], writes=[do])
                    else:
                        S.op("act", lambda e, o=o, pt=pt: e.activation(out=o[:], in_=pt[:], func=AF.Copy), reads=[dpt], writes=[do])
                    S.dma("pool", dst[tt * P:(tt + 1) * P, d0:d0 + 512], o[:], reads=[do])
            w, dw = wr.next()
            S.dma("sp", w[:, :, 0:32], WB3[:, :, O_DT:O_DT + 32], writes=[dw])
            dtb = st.enter_context(nc.sbuf_tensor(uname("ip_dtb"), [P, 32], F32))
            ddtb = Dep()
            S.dma("sp", dtb[:], bcast_row(p["dt_bias"][L].rearrange("(o a) h -> o (a h)", o=1), 32), writes=[ddtb])
            for tt in range(NT):
                pt, dpt = self.ps.next()
                for k in range(8):
                    S.op("pe", lambda e, k=k, pt=pt, w=w, tt=tt: e.matmul(pt[:, 0:32], lhsT=hT[:, k, tt * P:(tt + 1) * P], rhs=w[:, k, 0:32],
                                                                    start=(k == 0), stop=(k == 7)),
                         reads=[dh, dw], writes=[dpt])
                o, do = orr.next()
                S.op("dve", lambda e, o=o, pt=pt: e.tensor_tensor(out=o[:, 0:32], in0=pt[:, 0:32], in1=dtb[:], op=ALU.add),
                     reads=[dpt, ddtb], writes=[do])
                S.op("act", lambda e, o=o: e.activation(out=o[:, 0:32], in_=o[:, 0:32], func=AF.Exp), reads=[do], writes=[do])
                S.op("act", lambda e, o=o: e.activation(out=o[:, 0:32], in_=o[:, 0:32], func=AF.Ln, bias=1.0, scale=1.0), reads=[do], writes=[do])
                S.dma("pool", self.DT[tt * P:(tt + 1) * P, :], o[:, 0:32], reads=[do])
            cw = st.enter_context(nc.sbuf_tensor(uname("ip_cw"), [P, 5, 16], F32))
            cb = st.enter_context(nc.sbuf_tensor(uname("ip_cb"), [P, 16], F32))
            dcw = Dep()
            for kk in range(5):
                S.dma("sp", cw[:, kk, :], p["conv_w"][L][kk].rearrange("(f p) -> p f", p=P), writes=[dcw], allow_slow_non_contiguous=True)
            S.dma("sp", cb[:], p["conv_b"][L].rearrange("(f p) -> p f", p=P), writes=[dcw], allow_slow_non_contiguous=True)
            xr_ring = Ring(nc, st, "ip_xr", 2, [P, T + 4], BF16)
            ob_ring = Ring(nc, st, "ip_ob", 2, [P, T], BF16)
            for xr_t, xr_d in zip(xr_ring.t, xr_ring.d):
                S.op("pool", lambda e, t_=xr_t: e.memset(t_[:, 0:2], 0.0), writes=[xr_d])
                S.op("pool", lambda e, t_=xr_t: e.memset(t_[:, T + 2:T + 4], 0.0), writes=[xr_d])
            diagw = st.enter_context(nc.sbuf_tensor(uname("ip_diagw"), [P, 16, 5, P], BF16))
            ddg = Dep()
            for f in range(16):
                for kk in range(5):
                    S.op("dve", lambda e, f=f, kk=kk: e.tensor_scalar(out=diagw[:, f, kk, :], in0=self.ident[:], scalar1=cw[:, kk, f:f + 1], scalar2=None, op0=ALU.mult),
                         reads=[dcw, self.dconst], writes=[ddg])

            def conv_part(f, xr, dxr):
                ob, dob = ob_ring.next()
                for tb in range(8):
                    pc, dpc = self.ps.next()
                    for kk in range(5):
                        S.op("pe", lambda e, kk=kk, pc=pc, tb=tb: e.matmul(pc[:], lhsT=diagw[:, f, kk, :], rhs=xr[:, tb * 512 + kk:tb * 512 + kk + 512],
                                                                       start=(kk == 0), stop=(kk == 4)), reads=[ddg, dxr], writes=[dpc])
                    S.op("act", lambda e, pc=pc, tb=tb, ob=ob: e.activation(out=ob[:, tb * 512:(tb + 1) * 512], in_=pc[:], func=AF.Silu, bias=cb[:, f:f + 1], scale=1.0),
                         reads=[dpc, dcw, dob], writes=[dob])
                S.dma("act", self.XBCT[f * P:(f + 1) * P, :], ob[:], reads=[dob])

            pend_conv = None
            for fq in range(4):
                w, dw = wr.next()
                S.dma("sp", w[:], WB3[:, :, O_XBC + fq * 512:O_XBC + (fq + 1) * 512], writes=[dw])
                for fj in range(4):
                    f = fq * 4 + fj
                    xr, dxr = xr_ring.next()
                    for tb in range(8):
                        pt, dpt = self.ps.next()
                        for k in range(8):
                            S.op("pe", lambda e, k=k, pt=pt, w=w, tb=tb, fj=fj: e.matmul(
                                pt[:], lhsT=w[:, k, fj * P:(fj + 1) * P], rhs=hT[:, k, tb * 512:(tb + 1) * 512],
                                start=(k == 0), stop=(k == 7)), reads=[dh, dw], writes=[dpt])
                        if tb % 2 == 0:
                            S.op("dve", lambda e, xr=xr, pt=pt, tb=tb: e.tensor_copy(out=xr[:, 2 + tb * 512:2 + (tb + 1) * 512], in_=pt[:]),
                                 reads=[dpt, dxr], writes=[dxr])
                        else:
                            S.op("act", lambda e, xr=xr, pt=pt, tb=tb: e.activation(out=xr[:, 2 + tb * 512:2 + (tb + 1) * 512], in_=pt[:], func=AF.Copy),
                                 reads=[dpt, dxr], writes=[dxr])
                    if pend_conv is not None:
                        conv_part(*pend_conv)
                    pend_conv = (f, xr, dxr)
            conv_part(*pend_conv)
            S.barrier()

    def ssd_consts(self):
        nc, S, stk = self.nc, self.S, self.gstk
        if hasattr(self, "triU"):
            return
        sb = lambda n, shp, dt=F32: stk.enter_context(nc.sbuf_tensor(uname(n), shp, dt))
        self.triU, self.triL = sb("triU", [P, P]), sb("triL", [P, P])
        self.ones = sb("ones", [P, P])
        self.sel3 = sb("sel3", [96, 32, P], BF16)
        self.ones96 = sb("ones96", [96, P], BF16)
        d = self.dconst
        S.op("dve", lambda e: e.tensor_single_scalar(out=self.triU[:], in_=self.ii[:], scalar=0, op=ALU.is_ge), reads=[self.dii], writes=[d])
        S.op("dve", lambda e: e.tensor_single_scalar(out=self.triL[:], in_=self.ii[:], scalar=0, op=ALU.is_le), reads=[self.dii], writes=[d])
        S.op("dve", lambda e: e.memset(self.ones[:], 1.0), writes=[d])
        S.op("dve", lambda e: e.memset(self.ones96[:], 1.0), writes=[d])
        for q in range(3):
            S.op("dve", lambda e, q=q: e.tensor_copy(out=self.sel3[q * 32:(q + 1) * 32],
                                                     in_=self.ident[q * 32:(q + 1) * 32, q * 32:(q + 1) * 32].unsqueeze(2).to_broadcast([32, 32, P])),
                 reads=[d], writes=[d])

    def ssd(self, L):
        nc, S, p = self.nc, self.S, self.prm
        self.ssd_consts()
        dc = self.dconst
        RB = self.scr.get("RB")
        if RB is None:
            RB = self.dram("RB", [32, P, D], BF16)
        v3 = lambda ap: ap.rearrange("p (h q) -> p h q", q=64)
        with ExitStack() as st:
            sb = lambda n, shp, dt=F32: st.enter_context(nc.sbuf_tensor(uname(n), shp, dt))
            Abc, Dbc, normw = sb("Abc", [P, 32]), sb("Dbc", [P, 16]), sb("normw", [P, D])
            dpl = Dep()
            S.dma("sp", Abc[:], bcast_row(p["a_log"][L].rearrange("(o a) h -> o (a h)", o=1), 32), writes=[dpl])
            S.dma("sp", Dbc[:], bcast_row(p["ssd_d"][L:L + 1, :], 16), writes=[dpl])
            S.dma("sp", normw[:], bcast_row(p["ssd_norm_w"][L:L + 1, :], D), writes=[dpl])
            S.op("act", lambda e: e.activation(out=Abc[:], in_=Abc[:], func=AF.Exp), reads=[dpl], writes=[dpl])
            S.op("dve", lambda e: e.tensor_scalar(out=Abc[:], in0=Abc[:], scalar1=-1.0, scalar2=None, op0=ALU.mult), reads=[dpl], writes=[dpl])
            R = [sb("Rf", [P, D]), sb("Rb", [P, D])]
            Rh = [sb("Rfh", [P, D], BF16), sb("Rbh", [P, D], BF16)]
            dR = [Dep(), Dep()]
            dRh = [Dep(), Dep()]
            for i in range(2):
                S.op("pool", lambda e, i=i: e.memset(R[i][:], 0.0), writes=[dR[i]])
                S.op("pool", lambda e, i=i: e.memset(Rh[i][:], 0.0), writes=[dRh[i]])
            xin = Ring(nc, st, "sd_xin", 2, [P, 16, 512], BF16)
            X3 = self.XBCT.rearrange("(k p) t -> p k t", p=P)
            dtr = Ring(nc, st, "sd_dt", 3, [P, 32], F32)
            sm = {n: Ring(nc, st, "sd_" + n, 3, [P, 32], F32) for n in ("adt", "cs", "dec", "etot", "ecs")}
            xtok_r = Ring(nc, st, "sd_xtok", 3, [P, D], F32)
            btok_r = Ring(nc, st, "sd_btok", 3, [P, 512], BF16)
            X_r = [Ring(nc, st, "sd_X%d" % i, 3, [P, D], BF16) for i in range(2)]
            Xd_r = [Ring(nc, st, "sd_Xd%d" % i, 3, [P, D], BF16) for i in range(2)]
            xf_r = Ring(nc, st, "sd_xf32", 2, [P, D], F32)
            cs3_r = Ring(nc, st, "sd_cs3", 3, [P, 96], F32)
            csb_r = Ring(nc, st, "sd_csb", 2, [P, 64], BF16)
            csT_r = Ring(nc, st, "sd_csT", 3, [96, P], BF16)
            csel_r = Ring(nc, st, "sd_csel", 2, [96, 32, P], BF16)
            ncsT_r = Ring(nc, st, "sd_ncsT", 3, [96, P], BF16)
            cbm_r = [Ring(nc, st, "sd_cbm%d" % i, 3, [P, 4, P], BF16) for i in range(2)]
            lt_r = Ring(nc, st, "sd_lt", 4, [P, 4, P], BF16)
            mt_r = Ring(nc, st, "sd_mt", 5, [P, 4, P], BF16)
            y_r = Ring(nc, st, "sd_y", 3, [P, D], F32)
            t2_r = Ring(nc, st, "sd_t2", 2, [P, D], BF16)
            t_r = Ring(nc, st, "sd_t", 2, [P, D], F32)
            z_r = Ring(nc, st, "sd_z", 3, [P, D], F32)
            yb_r = Ring(nc, st, "sd_yb", 3, [P, D], BF16)
            yT_r = Ring(nc, st, "sd_yT", 2, [P, 8, P], BF16)
            ss_r = Ring(nc, st, "sd_ss", 2, [P, 2], F32)
            rbl_r = Ring(nc, st, "sd_rbl", 3, [P, D], BF16)
            cur = {}

            def load_block(blk):
                t_, d_ = xin.next()
                for q in range(2):
                    S.dma("sp", t_[:, q * 8:(q + 1) * 8, :], X3[:, q * 8:(q + 1) * 8, blk * 512:(blk + 1) * 512], writes=[d_])
                cur["blk"], cur["xin"], cur["dxin"] = blk, t_, d_

            def prefetch(c):
                if cur.get("blk") != c // 4:
                    load_block(c // 4)
                pre = dict(xin=cur["xin"], dxin=cur["dxin"])
                dt, ddt = dtr.next()
                S.dma("sp", dt[:], self.DT[c * P:(c + 1) * P, :], writes=[ddt])
                rbl, drbl = rbl_r.next()
                S.dma("sp", rbl[:], RB[c], writes=[drbl])
                zt, dz = z_r.next()
                S.dma("sp", zt[:], self.Z[c * P:(c + 1) * P, :], writes=[dz])
                pre.update(dt=(dt, ddt), rbl=(rbl, drbl), zt=(zt, dz))
                return pre

            def common(c, dirs, pre=None):
                if pre is None:
                    if cur.get("blk") != c // 4:
                        load_block(c // 4)
                    xi, dxi = cur["xin"], cur["dxin"]
                    dt, ddt = dtr.next()
                    S.dma("sp", dt[:], self.DT[c * P:(c + 1) * P, :], writes=[ddt])
                else:
                    xi, dxi = pre["xin"], pre["dxin"]
                    dt, ddt = pre["dt"]
                o = (c % 4) * P
                adt, dadt = sm["adt"].next()
                S.op("dve", lambda e: e.tensor_tensor(out=adt[:], in0=dt[:], in1=Abc[:], op=ALU.mult), reads=[ddt, dpl], writes=[dadt])
                pA, dpA = self.ps.next()
                S.op("pe", lambda e: e.matmul(pA[:, 0:16], lhsT=self.triU[:], rhs=adt[:, 0:16], start=True, stop=True), reads=[dc, dadt], writes=[dpA])
                S.op("pe", lambda e: e.matmul(pA[:, 16:32], lhsT=self.triL[:], rhs=adt[:, 16:32], start=True, stop=True), reads=[dc, dadt], writes=[dpA])
                S.op("pe", lambda e: e.matmul(pA[:, 32:64], lhsT=self.ones[:], rhs=adt[:, 0:32], start=True, stop=True), reads=[dc, dadt], writes=[dpA])
                cs, dcs = sm["cs"].next()
                dec, ddec = sm["dec"].next()
                etot, detot = sm["etot"].next()
                ecs, decs = sm["ecs"].next()
                S.op("dve", lambda e: e.tensor_copy(out=cs[:], in_=pA[:, 0:32]), reads=[dpA], writes=[dcs])
                S.op("dve", lambda e: e.tensor_tensor(out=dec[:], in0=pA[:, 32:64], in1=cs[:], op=ALU.subtract), reads=[dpA, dcs], writes=[ddec])
                S.op("act", lambda e: e.activation(out=dec[:], in_=dec[:], func=AF.Exp), reads=[ddec], writes=[ddec])
                S.op("act", lambda e: e.activation(out=etot[:], in_=pA[:, 32:64], func=AF.Exp), reads=[dpA], writes=[detot])
                S.op("act", lambda e: e.activation(out=ecs[:], in_=cs[:], func=AF.Exp), reads=[dcs], writes=[decs])
                px, dpx = self.ps.next()
                pxb = px[:].bitcast(BF16)
                for k in range(8):
                    S.op("pe", lambda e, k=k: e.transpose(out=pxb[:, k * P:(k + 1) * P], in_=xi[:, k, o:o + P], identity=self.identb[:]),
                         reads=[dxi, dc], writes=[dpx])
                xtok, dxt = xtok_r.next()
                S.op("act", lambda e: e.activation(out=xtok[:], in_=pxb, func=AF.Copy), reads=[dpx], writes=[dxt])
                pb, dpb = self.ps.next()
                pbb = pb[:].bitcast(BF16)
                for g in range(4):
                    S.op("pe", lambda e, g=g: e.transpose(out=pbb[:, g * P:(g + 1) * P], in_=xi[:, 8 + g, o:o + P], identity=self.identb[:]),
                         reads=[dxi, dc], writes=[dpb])
                btok, dbt = btok_r.next()
                S.op("act", lambda e: e.activation(out=btok[:], in_=pbb[:, 0:512], func=AF.Copy), reads=[dpb], writes=[dbt])
                Xs, Xds = {}, {}
                for di in dirs:
                    X, dX = X_r[di].next()
                    Xd, dXd = Xd_r[di].next()
                    xf32, dxf = xf_r.next()
                    S.op("dve", lambda e, di=di, xf32=xf32: e.tensor_tensor(out=v3(xf32[:]), in0=v3(xtok[:]),
                                                                            in1=dt[:, di * 16:(di + 1) * 16].unsqueeze(2).to_broadcast([P, 16, 64]), op=ALU.mult),
                         reads=[dxt, ddt, dxf], writes=[dxf])
                    S.op("act", lambda e, X=X, xf32=xf32: e.activation(out=X[:], in_=xf32[:], func=AF.Copy), reads=[dxf], writes=[dX])
                    S.op("dve", lambda e, di=di, Xd=Xd, xf32=xf32: e.tensor_tensor(out=v3(Xd[:]), in0=v3(xf32[:]),
                                                                                   in1=dec[:, di * 16:(di + 1) * 16].unsqueeze(2).to_broadcast([P, 16, 64]), op=ALU.mult),
                         reads=[dxf, ddec], writes=[dXd])
                    Xs[di], Xds[di] = (X, dX), (Xd, dXd)
                return dict(xi=xi, dxi=dxi, o=o, dt=(dt, ddt), cs=(cs, dcs), etot=(etot, detot), ecs=(ecs, decs),
                            xtok=(xtok, dxt), btok=(btok, dbt), X=Xs, Xd=Xds)

            def update_state(cm, di):
                btok, dbt = cm["btok"]
                Xd, dXd = cm["Xd"][di]
                etot, detot = cm["etot"]
                S.op("dve", lambda e: e.tensor_tensor(out=v3(R[di][:]), in0=v3(R[di][:]),
                                                      in1=etot[:, di * 16:(di + 1) * 16].unsqueeze(2).to_broadcast([P, 16, 64]), op=ALU.mult),
                     reads=[detot, dR[di]], writes=[dR[di]])
                for half in range(2):
                    pst, dps = self.ps.next()
                    for gg in range(2):
                        g = half * 2 + gg
                        S.op("pe", lambda e, g=g, gg=gg, pst=pst: e.matmul(pst[:, gg * 256:(gg + 1) * 256], lhsT=btok[:, g * P:(g + 1) * P],
                                                                   rhs=Xd[:, g * 256:(g + 1) * 256], start=True, stop=True),
                             reads=[dbt, dXd], writes=[dps])
                    S.op("dve", lambda e, half=half, pst=pst: e.tensor_tensor(out=R[di][:, half * 512:(half + 1) * 512],
                                                                              in0=R[di][:, half * 512:(half + 1) * 512], in1=pst[:], op=ALU.add),
                         reads=[dps, dR[di]], writes=[dR[di]])
                S.op("dve", lambda e: e.tensor_copy(out=Rh[di][:], in_=R[di][:]), reads=[dR[di]], writes=[dRh[di]])

            cms = {NT - 1: common(NT - 1, (1,)), NT - 2: common(NT - 2, (1,))}
            for c in range(NT - 1, -1, -1):
                if c - 2 >= 0:
                    cms[c - 2] = common(c - 2, (1,))
                S.dma("act", RB[c], Rh[1][:], reads=[dRh[1]])
                update_state(cms.pop(c), 1)
            S.barrier()
            cur.clear()
            if STOP == "ssdA":
                return

            def stage1(c, pre):
                cm = common(c, (0, 1), pre)
                xi, dxi, o = cm["xi"], cm["dxi"], cm["o"]
                cs, dcs = cm["cs"]
                rbl, drbl = pre["rbl"]
                zt, dz = pre["zt"]
                S.op("act", lambda e: e.activation(out=zt[:], in_=zt[:], func=AF.Silu), reads=[dz], writes=[dz])
                cs3, dcs3 = cs3_r.next()
                csb, dcsb = csb_r.next()
                S.op("dve", lambda e: e.tensor_copy(out=csb[:, 0:32], in_=cs[:]), reads=[dcs, dcsb], writes=[dcsb])
                S.op("dve", lambda e: e.tensor_copy(out=cs3[:, 0:32], in_=csb[:, 0:32]), reads=[dcsb, dcs3], writes=[dcs3])
                S.op("dve", lambda e: e.tensor_tensor(out=cs3[:, 64:96], in0=cs[:], in1=cs3[:, 0:32], op=ALU.subtract), reads=[dcs, dcs3], writes=[dcs3])
                S.op("dve", lambda e: e.tensor_copy(out=csb[:, 32:64], in_=cs3[:, 64:96]), reads=[dcs3, dcsb], writes=[dcsb])
                S.op("dve", lambda e: e.tensor_copy(out=cs3[:, 32:64], in_=csb[:, 32:64]), reads=[dcsb, dcs3], writes=[dcs3])
                S.op("dve", lambda e: e.tensor_tensor(out=cs3[:, 64:96], in0=cs3[:, 64:96], in1=cs3[:, 32:64], op=ALU.subtract), reads=[dcs3], writes=[dcs3])
                pcb, dpcb = self.ps.next()
                for g in range(4):
                    S.op("pe", lambda e, g=g: e.matmul(pcb[:, g * P:(g + 1) * P], lhsT=xi[:, 8 + g, o:o + P], rhs=xi[:, 12 + g, o:o + P],
                                                       start=True, stop=True), reads=[dxi], writes=[dpcb])
                cbm = []
                for di in range(2):
                    t_, d_ = cbm_r[di].next()
                    msk = self.triU if di == 0 else self.triL
                    S.op("dve", lambda e, t_=t_, msk=msk: e.tensor_tensor(out=t_[:], in0=pcb[:].rearrange("p (g l) -> p g l", g=4),
                                                                          in1=msk[:].unsqueeze(1).to_broadcast([P, 4, P]), op=ALU.mult),
                         reads=[dpcb, dc], writes=[d_])
                    cbm.append((t_, d_))
                cm.update(rbl=(rbl, drbl), zt=(zt, dz), cbm=cbm, cs3=(cs3, dcs3))
                return cm

            def stage1b(cm):
                cs3, dcs3 = cm["cs3"]
                pc, dpc = self.ps.next()
                S.op("pe", lambda e: e.transpose(out=pc[0:96, 0:P], in_=cs3[:], identity=self.ident[:]), reads=[dcs3, dc], writes=[dpc])
                csT, dcsT = csT_r.next()
                ncsT, dncsT = ncsT_r.next()
                S.op("dve", lambda e: e.tensor_copy(out=csT[:], in_=pc[0:96, 0:P]), reads=[dpc, dcsT], writes=[dcsT])
                S.op("act", lambda e: e.activation(out=ncsT[:], in_=pc[0:96, 0:P], func=AF.Copy, scale=-1.0), reads=[dpc, dncsT], writes=[dncsT])
                csel, dcsel = csel_r.next()
                S.op("dve", lambda e: e.tensor_tensor(out=csel[:], in0=self.sel3[:], in1=csT[:].unsqueeze(1).to_broadcast([96, 32, P]), op=ALU.mult),
                     reads=[dcsT, dc, dcsel], writes=[dcsel])
                cm.update(csT=(csT, dcsT), ncsT=(ncsT, dncsT), csel=(csel, dcsel))

            def stage2(c, cm):
                xi, dxi, o = cm["xi"], cm["dxi"], cm["o"]
                ecs, decs = cm["ecs"]
                rbl, drbl = cm["rbl"]
                zt, dz = cm["zt"]
                csT, dcsT = cm["csT"]
                ncsT, dncsT = cm["ncsT"]
                csel, dcsel = cm["csel"]
                pY = [self.psY.next(), self.psY.next()]

                def emit_L(di, g):
                    cbt, dcbt = cm["cbm"][di]
                    pL, dpL = self.ps.next()
                    h0 = di * 16 + g * 4
                    S.op("pe", lambda e: e.matmul(pL[:], lhsT=self.ones96[:], rhs=csel[:, h0:h0 + 4, :].rearrange("k h l -> k (h l)"),
                                                  start=True, stop=False), reads=[dcsel, dc], writes=[dpL])
                    S.op("pe", lambda e: e.matmul(pL[:], lhsT=ncsT[:], rhs=self.sel3[:, h0:h0 + 4, :].rearrange("k h l -> k (h l)"),
                                                  start=False, stop=True), reads=[dncsT, dc], writes=[dpL])
                    lt, dlt = lt_r.next()
                    S.op("act", lambda e: e.activation(out=lt[:], in_=pL[:].rearrange("p (h l) -> p h l", h=4), func=AF.Exp), reads=[dpL, dlt], writes=[dlt])
                    mt, dmt = mt_r.next()
                    S.op("dve", lambda e: e.scalar_tensor_tensor(out=mt[:], in0=lt[:], scalar=1.0, in1=cbt[:, g:g + 1, :].to_broadcast([P, 4, P]),
                                                                 op0=ALU.min, op1=ALU.mult), reads=[dlt, dcbt, dmt], writes=[dmt])
                    return (mt, dmt)

                def emit_Y(di, g, mtd):
                    mt, dmt = mtd
                    X, dX = cm["X"][di]
                    for hh in range(4):
                        j = g * 4 + hh
                        py, dpy = pY[j // 8]
                        jj = j % 8
                        S.op("pe", lambda e, hh=hh, j=j, jj=jj, py=py: e.matmul(py[:, jj * 64:(jj + 1) * 64], lhsT=mt[:, hh, :], rhs=X[:, j * 64:(j + 1) * 64],
                                                                        start=(di == 0 and jj == 0), stop=(di == 1), skip_group_check=True),
                             reads=[dmt, dX], writes=[dpy])

                units = [(di, g) for di in range(2) for g in range(4)]
                mts = {}
                for k in range(len(units) + 2):
                    if k < len(units):
                        mts[k] = emit_L(*units[k])
                    if 0 <= k - 2 < len(units):
                        emit_Y(*units[k - 2], mts.pop(k - 2))
                y, dy = y_r.next()
                tmpt, dtm = t_r.next()
                for di in range(2):
                    prev, dprev = (Rh[0], dRh[0]) if di == 0 else (rbl, drbl)
                    for half in range(2):
                        po, dpo = self.ps.next()
                        for gg in range(2):
                            g = half * 2 + gg
                            S.op("pe", lambda e, g=g, gg=gg, po=po, prev=prev: e.matmul(po[:, gg * 256:(gg + 1) * 256], lhsT=xi[:, 12 + g, o:o + P],
                                                                                rhs=prev[:, g * 256:(g + 1) * 256], start=True, stop=True),
                                 reads=[dxi, dprev], writes=[dpo])
                        dst = y if di == 0 else tmpt
                        ddst = dy if di == 0 else dtm
                        S.op("dve", lambda e, po=po, half=half, dst=dst, di=di: e.tensor_tensor(
                            out=dst[:, half * 512:(half + 1) * 512].rearrange("p (h q) -> p h q", q=64),
                            in0=po[:].rearrange("p (h q) -> p h q", q=64),
                            in1=ecs[:, di * 16 + half * 8:di * 16 + half * 8 + 8].unsqueeze(2).to_broadcast([P, 8, 64]), op=ALU.mult),
                            reads=[dpo, decs, ddst], writes=[ddst])
                S.op("dve", lambda e: e.tensor_tensor(out=y[:], in0=y[:], in1=tmpt[:], op=ALU.add), reads=[dy, dtm], writes=[dy])
                for half in range(2):
                    py, dpy = pY[half]
                    S.op("dve", lambda e, half=half, py=py: e.tensor_tensor(out=y[:, half * 512:(half + 1) * 512], in0=y[:, half * 512:(half + 1) * 512],
                                                                            in1=py[:], op=ALU.add), reads=[dpy, dy], writes=[dy])
                xtok, dxt = cm["xtok"]
                S.op("dve", lambda e: e.tensor_tensor(out=v3(tmpt[:]), in0=v3(xtok[:]), in1=Dbc[:].unsqueeze(2).to_broadcast([P, 16, 64]), op=ALU.mult),
                     reads=[dxt, dpl, dtm], writes=[dtm])
                S.op("dve", lambda e: e.tensor_tensor(out=y[:], in0=y[:], in1=tmpt[:], op=ALU.add), reads=[dy, dtm], writes=[dy])
                update_state(cm, 0)
                return (y, dy, zt, dz)

            def stage2b(c, st2):
                y, dy, zt, dz = st2
                tmpt, dtm = t2_r.next()
                S.op("dve", lambda e: e.tensor_tensor(out=y[:], in0=y[:], in1=zt[:], op=ALU.mult), reads=[dy, dz], writes=[dy])
                ss, dss = ss_r.next()
                S.op("act", lambda e: e.activation(out=tmpt[:], in_=y[:], func=AF.Square, accum_out=ss[:, 0:1]), reads=[dy, dtm], writes=[dtm, dss])
                S.op("act", lambda e: e.activation(out=ss[:, 1:2], in_=ss[:, 0:1], func=AF.Sqrt, bias=EPS, scale=1.0 / D), reads=[dss], writes=[dss])
                S.op("dve", lambda e: e.reciprocal(out=ss[:, 1:2], in_=ss[:, 1:2]), reads=[dss], writes=[dss])
                yb, dyb = yb_r.next()
                S.op("dve", lambda e: e.scalar_tensor_tensor(out=yb[:], in0=y[:], scalar=ss[:, 1:2], in1=normw[:], op0=ALU.mult, op1=ALU.mult),
                     reads=[dy, dss, dpl], writes=[dyb])
                return (yb, dyb)

            pres = {0: prefetch(0), 1: prefetch(1)}
            cms = {0: stage1(0, pres.pop(0))}
            stage1b(cms[0])
            st2s, ybs = {}, {}
            for c in range(NT + 2):
                if c + 1 < NT:
                    cms[c + 1] = stage1(c + 1, pres.pop(c + 1))

                if c < NT:
                    st2s[c] = stage2(c, cms.pop(c))
                if c + 1 < NT:
                    stage1b(cms[c + 1])
                if 0 <= c - 1 < NT:
                    ybs[c - 1] = stage2b(c - 1, st2s.pop(c - 1))
                if 0 <= c - 2 < NT:
                    yb, dyb = ybs.pop(c - 2)
                    self.store_T(yb, dyb, yT_r, 0, c - 2)
                if c + 2 < NT:
                    pres[c + 2] = prefetch(c + 2)
            S.barrier()

    def store_T(self, yb, dyb, yT_r, row0, c):
        S = self.S
        yT, dyT = yT_r.next()
        pt, dpt = self.ps.next()
        ptb = pt[:].bitcast(BF16)
        for k in range(8):
            S.op("pe", lambda e, k=k: e.transpose(out=ptb[:, k * P:(k + 1) * P], in_=yb[:, k * P:(k + 1) * P], identity=self.identb[:]),
                 reads=[dyb, self.dconst], writes=[dpt])
        S.op("act", lambda e: e.activation(out=yT[:], in_=ptb.rearrange("p (k t) -> p k t", k=8), func=AF.Copy), reads=[dpt], writes=[dyT])
        S.dma("act", self.YCATT[row0:row0 + D, :].rearrange("(k p) t -> p k t", p=P)[:, :, c * P:(c + 1) * P], yT[:], reads=[dyT])

    def s5(self, L, extra=()):
        nc, S, p = self.nc, self.S, self.prm
        dc = self.dconst
        TWO_PI = float(2 * np.pi)
        NQ = 32
        Y5 = self.scr.get("Y5")
        if Y5 is None:
            Y5 = self.dram("Y5", [T, D], F32)
        with ExitStack() as st:
            sb = lambda n, shp, dt=F32: st.enter_context(nc.sbuf_tensor(uname(n), shp, dt))
            dpp = Dep()

            def ew(eng, fn):
                S.op(eng, fn, reads=[dpp], writes=[dpp])

            tstk = [None]
            tb = lambda n, shp, dt=F32: tstk[0].enter_context(nc.sbuf_tensor(uname(n), shp, dt))

            def trig(x, shape, name):
                xi = tb(name + "_xi", shape, I32)
                fr = tb(name + "_fr", shape)
                sn = tb(name + "_sn", shape)
                cs_ = tb(name + "_cs", shape)
                ew("dve", lambda e: e.tensor_copy(out=xi[:], in_=x[:]))
                ew("dve", lambda e: e.tensor_copy(out=fr[:], in_=xi[:]))
                ew("dve", lambda e: e.tensor_tensor(out=fr[:], in0=x[:], in1=fr[:], op=ALU.subtract))
                ew("act", lambda e: e.activation(out=sn[:], in_=fr[:], func=AF.Sin, scale=TWO_PI))
                ew("act", lambda e: e.activation(out=fr[:], in_=fr[:], func=AF.Abs))
                ew("act", lambda e: e.activation(out=cs_[:], in_=fr[:], func=AF.Sin, scale=-TWO_PI, bias=float(np.pi / 2)))
                return sn, cs_

            sidx_i = sb("sidx_i", [P, NQ, 8], I32)
            sidx = sb("sidx", [P, NQ, 8])
            ew("pool", lambda e: e.iota(sidx_i[:], pattern=[[0, NQ], [1, 8]], base=0, channel_multiplier=0))
            ew("dve", lambda e: e.tensor_copy(out=sidx[:], in_=sidx_i[:]))
            cidx_i = sb("cidx_i", [P, 512], I32)
            cidx = sb("cidx", [P, 512])
            ew("pool", lambda e: e.iota(cidx_i[:], pattern=[[1, 512]], base=0, channel_multiplier=0))
            ew("dve", lambda e: e.tensor_copy(out=cidx[:], in_=cidx_i[:]))
            mF, mB = sb("mF", [P, 8, 16]), sb("mB", [P, 8, 16])
            mi = sb("mi", [P, 8, 16], I32)
            ew("pool", lambda e: e.iota(mi[:], pattern=[[16, 8], [0, 16]], base=15, channel_multiplier=-1))
            ew("dve", lambda e: e.tensor_single_scalar(out=mF[:], in_=mi[:], scalar=0, op=ALU.is_ge))
            ew("pool", lambda e: e.iota(mi[:], pattern=[[-16, 8], [0, 16]], base=0, channel_multiplier=1))
            ew("dve", lambda e: e.tensor_single_scalar(out=mB[:], in_=mi[:], scalar=0, op=ALU.is_ge))
            dcol = sb("dcol", [P, 64])
            for s_ in range(8):
                S.dma("sp", dcol[s_ * 16:(s_ + 1) * 16, :], p["s5_d"][L].rearrange("(g h) -> h g", h=16), writes=[dpp], allow_slow_non_contiguous=True)

            def do_T(raw, outt, n_inner, srcf):
                for j0 in range(0, n_inner, 8):
                    nj = min(8, n_inner - j0)
                    pt, dpt = self.ps.next()
                    for jj in range(nj):
                        S.op("pe", lambda e, jj=jj, src=srcf(j0 + jj), pt=pt: e.transpose(out=pt[:, jj * NQ:(jj + 1) * NQ], in_=src, identity=self.ident[0:NQ, 0:NQ]),
                             reads=[dpp, dc], writes=[dpt])
                    S.op("dve", lambda e, pt=pt, j0=j0, nj=nj: e.tensor_copy(
                        out=outt[:, :, j0:j0 + nj].rearrange("p q j -> p j q"), in_=pt[:, 0:nj * NQ].rearrange("p (j q) -> p j q", j=nj)),
                        reads=[dpt, dpp], writes=[dpp])

            pers = []
            for d_ in range(2):
                pers.append(dict(
                    Cre=sb("Cre%d" % d_, [P, NQ, 16]), Cim=sb("Cim%d" % d_, [P, NQ, 16]),
                    E=[sb("E%d_%d" % (i, d_), [P, NQ, 8]) for i in range(4)],
                    Bbr=sb("Bbr%d" % d_, [P, NQ, 16]), Bbi=sb("Bbi%d" % d_, [P, NQ, 16]),
                    f8=sb("f8_%d" % d_, [P, NQ]), rho8=sb("rho8_%d" % d_, [P, NQ])))
            bstk = ExitStack()
            tstk[0] = bstk
            Bt = []
            for nm in ("s5_b_re", "s5_b_im"):
                raw = tb(nm + "_raw", [NQ, 2048])
                outt = tb(nm + "_T", [P, NQ, 16])
                S.dma("sp", raw[:], p[nm][L].rearrange("(q a) p h -> q (a p h)", a=2), writes=[dpp])
                do_T(raw, outt, 16, lambda j, raw=raw: raw[:].rearrange("q (ap j) -> q j ap", j=16)[:, j, :])
                Bt.append(outt)
            Bre, Bim = Bt
            PR = []
            for d_ in range(2):
                dstk = ExitStack()
                tstk[0] = dstk
                lrli = []
                for nm in ("s5_a_re", "s5_a_im"):
                    raw = tb(nm + "_raw", [NQ, P])
                    outt = tb(nm + "_T", [P, NQ, 1])
                    S.dma("sp", raw[:], p[nm][L, d_].rearrange("(q a) p -> q (a p)", a=2), writes=[dpp])
                    do_T(raw, outt, 1, lambda j, raw=raw: raw[:, :])
                    lrli.append(outt)
                for nm, key in (("s5_c_re", "Cre"), ("s5_c_im", "Cim")):
                    raw = tb(nm + "_raw", [NQ, 16, 2, 64])
                    for a_ in range(2):
                        S.dma("sp", raw[:, :, a_, :], p[nm][L, d_][a_::2], writes=[dpp])
                    do_T(raw, pers[d_][key], 16, lambda j, raw=raw: raw[:, j, :, :].rearrange("q a p -> q (a p)"))
                Cre, Cim = pers[d_]["Cre"], pers[d_]["Cim"]
                lr2, li2 = lrli[0][:, :, 0], lrli[1][:, :, 0]
                stp = tb("stp%d" % d_, [P, NQ])
                lsrow = p["s5_log_step"][L, d_:d_ + 1, :]
                for a_ in range(2):
                    S.dma("sp", stp[a_ * 64:(a_ + 1) * 64, :], lsrow[:, a_::2].partition_broadcast(64).rearrange("p o n -> p (o n)"),
                          writes=[dpp], allow_slow_non_contiguous=True)
                ew("act", lambda e: e.activation(out=stp[:], in_=stp[:], func=AF.Exp))
                lrs, f1 = tb("lrs%d" % d_, [P, NQ]), tb("f1%d" % d_, [P, NQ])
                ew("dve", lambda e: e.tensor_tensor(out=lrs[:], in0=lr2, in1=stp[:], op=ALU.mult))
                ew("dve", lambda e: e.tensor_tensor(out=f1[:], in0=li2, in1=stp[:], op=ALU.mult))
                ew("dve", lambda e: e.tensor_scalar(out=f1[:], in0=f1[:], scalar1=1.0 / TWO_PI, scalar2=None, op0=ALU.mult))
                xs, ms = tb("xs%d" % d_, [P, NQ, 8]), tb("ms%d" % d_, [P, NQ, 8])
                ew("dve", lambda e: e.tensor_tensor(out=xs[:], in0=sidx[:], in1=f1[:].unsqueeze(2).to_broadcast([P, NQ, 8]), op=ALU.mult))
                ew("dve", lambda e: e.tensor_tensor(out=ms[:], in0=sidx[:], in1=lrs[:].unsqueeze(2).to_broadcast([P, NQ, 8]), op=ALU.mult))
                sn, cs_ = trig(xs, [P, NQ, 8], "tg%d" % d_)
                magp, magm = tb("magp%d" % d_, [P, NQ, 8]), tb("magm%d" % d_, [P, NQ, 8])
                ew("act", lambda e: e.activation(out=magp[:], in_=ms[:], func=AF.Exp))
                ew("act", lambda e: e.activation(out=magm[:], in_=ms[:], func=AF.Exp, scale=-1.0))
                Erp, Eip, Erm, Eim = pers[d_]["E"]
                ew("dve", lambda e: e.tensor_tensor(out=Erp[:], in0=magp[:], in1=cs_[:], op=ALU.mult))
                ew("dve", lambda e: e.tensor_tensor(out=Eip[:], in0=magp[:], in1=sn[:], op=ALU.mult))
                ew("dve", lambda e: e.tensor_tensor(out=Erm[:], in0=magm[:], in1=cs_[:], op=ALU.mult))
                ew("dve", lambda e: e.scalar_tensor_tensor(out=Eim[:], in0=magm[:], scalar=-1.0, in1=sn[:], op0=ALU.mult, op1=ALU.mult))
                nr, den, cre, cim, t0 = (tb("c%d_%d" % (i, d_), [P, NQ]) for i in range(5))
                lbr, lbi = Erp[:, :, 1], Eip[:, :, 1]
                ew("dve", lambda e: e.tensor_scalar(out=nr[:], in0=lbr, scalar1=-1.0, scalar2=None, op0=ALU.add))
                ew("dve", lambda e: e.tensor_tensor(out=den[:], in0=lr2, in1=lr2, op=ALU.mult))
                ew("dve", lambda e: e.tensor_tensor(out=t0[:], in0=li2, in1=li2, op=ALU.mult))
                ew("dve", lambda e: e.tensor_tensor(out=den[:], in0=den[:], in1=t0[:], op=ALU.add))
                ew("dve", lambda e: e.reciprocal(out=den[:], in_=den[:]))
                ew("dve", lambda e: e.tensor_tensor(out=cre[:], in0=nr[:], in1=lr2, op=ALU.mult))
                ew("dve", lambda e: e.tensor_tensor(out=t0[:], in0=lbi, in1=li2, op=ALU.mult))
                ew("dve", lambda e: e.tensor_tensor(out=cre[:], in0=cre[:], in1=t0[:], op=ALU.add))
                ew("dve", lambda e: e.tensor_tensor(out=cre[:], in0=cre[:], in1=den[:], op=ALU.mult))
                ew("dve", lambda e: e.tensor_tensor(out=cim[:], in0=lbi, in1=lr2, op=ALU.mult))
                ew("dve", lambda e: e.tensor_tensor(out=t0[:], in0=nr[:], in1=li2, op=ALU.mult))
                ew("dve", lambda e: e.tensor_tensor(out=cim[:], in0=cim[:], in1=t0[:], op=ALU.subtract))
                ew("dve", lambda e: e.tensor_tensor(out=cim[:], in0=cim[:], in1=den[:], op=ALU.mult))
                Bbr, Bbi, t1 = pers[d_]["Bbr"], pers[d_]["Bbi"], tb("t1_%d" % d_, [P, NQ, 16])
                bc = lambda t_: t_[:].unsqueeze(2).to_broadcast([P, NQ, 16])
                ew("dve", lambda e: e.tensor_tensor(out=Bbr[:], in0=Bre[:], in1=bc(cre), op=ALU.mult))
                ew("dve", lambda e: e.tensor_tensor(out=t1[:], in0=Bim[:], in1=bc(cim), op=ALU.mult))
                ew("dve", lambda e: e.tensor_tensor(out=Bbr[:], in0=Bbr[:], in1=t1[:], op=ALU.subtract))
                ew("dve", lambda e: e.tensor_tensor(out=Bbi[:], in0=Bim[:], in1=bc(cre), op=ALU.mult))
                ew("dve", lambda e: e.tensor_tensor(out=t1[:], in0=Bre[:], in1=bc(cim), op=ALU.mult))
                ew("dve", lambda e: e.tensor_tensor(out=Bbi[:], in0=Bbi[:], in1=t1[:], op=ALU.add))
                f8, rho8 = pers[d_]["f8"], pers[d_]["rho8"]
                f8i = tb("f8i_%d" % d_, [P, NQ], I32)
                ew("dve", lambda e: e.tensor_scalar(out=f8[:], in0=f1[:], scalar1=8.0, scalar2=None, op0=ALU.mult))
                ew("dve", lambda e: e.tensor_copy(out=f8i[:], in_=f8[:]))
                ew("dve", lambda e: e.tensor_copy(out=t0[:], in_=f8i[:]))
                ew("dve", lambda e: e.tensor_tensor(out=f8[:], in0=f8[:], in1=t0[:], op=ALU.subtract))
                ew("act", lambda e: e.activation(out=rho8[:], in_=lrs[:], func=AF.Exp, scale=8.0))
                Ex = (Erm, Eim) if d_ == 0 else (Erp, Eip)
                Ey = (Erp, Eip) if d_ == 0 else (Erm, Eim)
                PR.append(dict(Ex=Ex, Ey=Ey, Bbr=Bbr, Bbi=Bbi, Cre=Cre, Cim=Cim, f8=f8, rho8=rho8))
                S.barrier()
                dstk.close()
            S.barrier()
            bstk.close()

            if STOP == "s5prep":
                return
            Ug_r = Ring(nc, st, "s5_Ug", 2, [P, 8, 512], BF16)
            utr = Ring(nc, st, "s5_ut", 2, [P, 8, P], BF16)
            urr = Ring(nc, st, "s5_ur", 8, [P, 8, P], BF16)
            U4 = self.U.rearrange("(cb c l) ch -> cb c l ch", cb=4, c=P, l=8)
            mats = {}
            for nm in ("Xr", "Xi", "Yr", "Yi"):
                for d_ in range(2):
                    mats[nm, d_] = sb("m%s%d" % (nm, d_), [P, 4, 8, 16])
            dmat = Dep()
            dmatb = Dep()
            matsb2 = [{k: sb("mb%d%s%d" % ((par,) + k), [P, 4, 8, 16], BF16) for k in mats} for par in range(2)]
            tmpm = sb("tmpm", [P, 4, 8, 16])
            Yacc = sb("Yacc", [P, 4, 8, P])
            dYacc = Dep()
            Tg_r = Ring(nc, st, "s5_T", 8, [P, P], BF16)
            Tf_r = Ring(nc, st, "s5_Tf", 3, [P, P], F32)
            winz = [Ring(nc, st, "s5_winz%d" % a_, 2, [P, 2, P], BF16) for a_ in range(2)]
            for a_ in range(2):
                for r_ in (winz[a_],):
                    for t_, d_ in zip(r_.t, r_.d):
                        S.op("pool", lambda e, t_=t_: e.memset(t_[:], 0.0), writes=[d_])
            cos_r = Ring(nc, st, "s5_cos", 6, [P, 512], BF16)
            sin_r = Ring(nc, st, "s5_sin", 6, [P, 512], BF16)
            tr_r = Ring(nc, st, "s5_tr", 2, [P, 512], F32)
            tri_r = Ring(nc, st, "s5_tri", 2, [P, 512], I32)
            gt_r = Ring(nc, st, "s5_gt", 12, [P, 512], BF16)
            gb_r = Ring(nc, st, "s5_gb", 4, [P, 512], BF16)
            tt_r = Ring(nc, st, "s5_tt", 4, [P, 512], BF16)
            dd = [sb("s5_d%d" % i, [P, 512], BF16) for i in range(4)]
            ddd = [Dep() for _ in range(4)]
            for i in range(4):
                S.op("pool", lambda e, i=i: e.memset(dd[i][:], 0.0), writes=[ddd[i]])
            DD_r = Ring(nc, st, "s5_DD", 6, [P, 2, 512], BF16)
            yg_r = Ring(nc, st, "s5_yg", 2, [P, 512], F32)
            Y54 = Y5.rearrange("(cb c l) ch -> c cb l ch", cb=4, c=P, l=8)
            bstate = {}

            utq = {}

            def batch_load(gb):
                lst = []
                for cb in range(4):
                    ur, dur = urr.next()
                    S.dma("sp", ur[:], U4[cb][:, :, gb * P:(gb + 1) * P], writes=[dur])
                    lst.append((ur, dur))
                utq[gb] = lst

            def batch_prep(gb):
                Ug, dUg = Ug_r.next()
                uts = utq.pop(gb)
                if gb + 1 < 8:
                    batch_load(gb + 1)
                for cb in range(4):
                    ur, dur = uts[cb]
                    ut, dut = utr.next()
                    S.op("act", lambda e, ur=ur, ut=ut: e.activation(out=ut[:].rearrange("p g (l h) -> p g l h", h=16),
                                                                     in_=ur[:].rearrange("p l (g h) -> p g l h", h=16), func=AF.Copy),
                         reads=[dur, dut], writes=[dut])
                    pt, dpt = self.ps.next()
                    ptb = pt[:].bitcast(BF16)
                    for g_ in range(8):
                        S.op("pe", lambda e, g_=g_, ptb=ptb, ut=ut: e.transpose(out=ptb[:, g_ * P:(g_ + 1) * P],
                                                                            in_=ut[:].rearrange("p a b -> p (a b)")[:, g_ * P:(g_ + 1) * P],
                                                                            identity=self.identb[:]), reads=[dut, dc], writes=[dpt])
                    S.op("act", lambda e, ptb=ptb, cb=cb, Ug=Ug: e.activation(out=Ug[:, :, cb * P:(cb + 1) * P],
                                                                          in_=ptb.rearrange("p (g c) -> p g c", g=8), func=AF.Copy),
                         reads=[dpt, dUg], writes=[dUg])
                q0 = gb * 4
                for d_ in range(2):
                    pr = PR[d_]
                    bE = lambda t_: t_[:, q0:q0 + 4, :].unsqueeze(3).to_broadcast([P, 4, 8, 16])
                    bB = lambda t_: t_[:, q0:q0 + 4, :].unsqueeze(2).to_broadcast([P, 4, 8, 16])

                    def cplx(outr, outi, Er, Ei, Br, Bi, neg_im):
                        S.op("dve", lambda e: e.tensor_tensor(out=outr[:], in0=bE(Er), in1=bB(Br), op=ALU.mult), reads=[dpp, dmat], writes=[dmat])
                        S.op("dve", lambda e: e.tensor_tensor(out=tmpm[:], in0=bE(Ei), in1=bB(Bi), op=ALU.mult), reads=[dpp, dmat], writes=[dmat])
                        S.op("dve", lambda e: e.tensor_tensor(out=outr[:], in0=outr[:], in1=tmpm[:], op=ALU.subtract), reads=[dmat], writes=[dmat])
                        S.op("dve", lambda e: e.tensor_tensor(out=outi[:], in0=bE(Er), in1=bB(Bi), op=ALU.mult), reads=[dpp, dmat], writes=[dmat])
                        S.op("dve", lambda e: e.tensor_tensor(out=tmpm[:], in0=bE(Ei), in1=bB(Br), op=ALU.mult), reads=[dpp, dmat], writes=[dmat])
                        if neg_im:
                            S.op("dve", lambda e: e.scalar_tensor_tensor(out=outi[:], in0=outi[:], scalar=-1.0, in1=tmpm[:], op0=ALU.mult, op1=ALU.subtract),
                                 reads=[dmat], writes=[dmat])
                        else:
                            S.op("dve", lambda e: e.tensor_tensor(out=outi[:], in0=outi[:], in1=tmpm[:], op=ALU.add), reads=[dmat], writes=[dmat])

                    cplx(mats["Xr", d_], mats["Xi", d_], pr["Ex"][0], pr["Ex"][1], pr["Bbr"], pr["Bbi"], False)
                    cplx(mats["Yr", d_], mats["Yi", d_], pr["Ey"][0], pr["Ey"][1], pr["Cre"], pr["Cim"], True)
                matsb = matsb2[gb % 2]
                bstate["matsb"] = matsb
                for nm in ("Xr", "Xi", "Yr", "Yi"):
                    for d_ in range(2):
                        S.op("act", lambda e, nm=nm, d_=d_: e.activation(out=matsb[nm, d_][:], in_=mats[nm, d_][:], func=AF.Copy), reads=[dmat, dmatb], writes=[dmatb])
                bstate["Ug"], bstate["dUg"] = Ug, dUg

            def stageA(q):
                gb, qq = q // 4, q % 4
                if qq == 0:
                    batch_prep(gb)
                Ug, dUg = bstate["Ug"], bstate["dUg"]
                m2 = lambda nm, d_: mats[nm, d_][:, qq, :, :].rearrange("p s h -> p (s h)")
                matsb = bstate["matsb"]
                m2b = lambda nm, d_: matsb[nm, d_][:, qq, :, :].rearrange("p s h -> p (s h)")
                h = dict(Ug=Ug, dUg=dUg, gb=gb, qq=qq, T=[], mod=[], matsb=matsb)
                wzs = []
                for d_ in range(2):
                    pw, dpw = self.ps.next()
                    pwb = pw[:].bitcast(BF16)
                    S.op("pe", lambda e, d_=d_, pwb=pwb: e.transpose(out=pwb[:, 0:P], in_=m2b("Xr", d_), identity=self.identb[:]), reads=[dmatb, dc], writes=[dpw])
                    S.op("pe", lambda e, d_=d_, pwb=pwb: e.transpose(out=pwb[:, P:2 * P], in_=m2b("Xi", d_), identity=self.identb[:]), reads=[dmatb, dc], writes=[dpw])
                    wz = []
                    for a_ in range(2):
                        w_, dw_ = winz[a_].next()
                        S.op("act", lambda e, pwb=pwb, w_=w_, a_=a_: e.activation(out=w_[:, :, a_ * 64:(a_ + 1) * 64],
                                                                             in_=pwb[:, 0:2 * P].rearrange("p (r c) -> p r c", r=2)[:, :, a_ * 64:(a_ + 1) * 64], func=AF.Copy),
                             reads=[dpw, dw_], writes=[dw_])
                        wz.append((w_, dw_))
                    wzs.append(wz)
                for a_ in range(2):
                    g = 2 * q + a_
                    sl = slice(a_ * 64, (a_ + 1) * 64)
                    Tf, dTf = Tf_r.next()
                    for d_ in range(2):
                        pt, dpt = self.ps.next()
                        S.op("pe", lambda e, d_=d_, pt=pt: e.matmul(pt[:, 0:P], lhsT=m2("Xr", d_)[sl, :], rhs=m2("Yr", d_)[sl, :], start=True, stop=False), reads=[dmat], writes=[dpt])
                        S.op("pe", lambda e, d_=d_, pt=pt: e.matmul(pt[:, 0:P], lhsT=m2("Xi", d_)[sl, :], rhs=m2("Yi", d_)[sl, :], start=False, stop=True), reads=[dmat], writes=[dpt])
                        msk = mF if d_ == 0 else mB
                        if d_ == 0:
                            S.op("dve", lambda e, pt=pt, msk=msk, Tf=Tf: e.tensor_tensor(out=Tf[:], in0=pt[:, 0:P], in1=msk[:].rearrange("p l h -> p (l h)"), op=ALU.mult),
                                 reads=[dpt, dpp, dTf], writes=[dTf])
                        else:
                            tq, dtq = Tf_r.next()
                            S.op("dve", lambda e, pt=pt, msk=msk, tq=tq: e.tensor_tensor(out=tq[:], in0=pt[:, 0:P], in1=msk[:].rearrange("p l h -> p (l h)"), op=ALU.mult),
                                 reads=[dpt, dpp, dtq], writes=[dtq])
                            S.op("pool", lambda e, tq=tq, Tf=Tf: e.tensor_tensor(out=Tf[:], in0=Tf[:], in1=tq[:], op=ALU.add), reads=[dtq, dTf], writes=[dTf])
                    Tg, dTg = Tg_r.next()
                    S.op("dve", lambda e, g=g, Tg=Tg, Tf=Tf: e.scalar_tensor_tensor(out=Tg[:], in0=self.ident[:], scalar=dcol[:, g:g + 1], in1=Tf[:], op0=ALU.mult, op1=ALU.add),
                         reads=[dTf, dpp, dc, dTg], writes=[dTg])
                    h["T"].append((Tg, dTg))
                for d_ in range(2):
                    pr = PR[d_]
                    wz = wzs[d_]
                    pgr, dpgr = self.ps.next()
                    pgi, dpgi = self.ps.next()
                    for a_ in range(2):
                        w_, dw_ = wz[a_]
                        S.op("pe", lambda e, w_=w_, a_=a_: e.matmul(pgr[:], lhsT=w_[:, 0, :], rhs=Ug[:, 2 * qq + a_, :], start=(a_ == 0), stop=(a_ == 1)), reads=[dw_, dUg], writes=[dpgr])
                    for a_ in range(2):
                        w_, dw_ = wz[a_]
                        S.op("pe", lambda e, w_=w_, a_=a_: e.matmul(pgi[:], lhsT=w_[:, 1, :], rhs=Ug[:, 2 * qq + a_, :], start=(a_ == 0), stop=(a_ == 1)), reads=[dw_, dUg], writes=[dpgi])
                    tr, dtr_ = tr_r.next()
                    tri, dtri = tri_r.next()
                    cs_, dcs_ = cos_r.next()
                    sn, dsn = sin_r.next()
                    f8c = pr["f8"][:, q:q + 1]
                    S.op("act", lambda e: e.activation(out=tr[:], in_=cidx[:], func=AF.Copy, scale=f8c), reads=[dpp, dtr_], writes=[dtr_])
                    S.op("dve", lambda e: e.tensor_copy(out=tri[:], in_=tr[:]), reads=[dtr_, dtri], writes=[dtri])
                    S.op("dve", lambda e: e.tensor_tensor(out=tr[:], in0=tr[:], in1=tri[:], op=ALU.subtract), reads=[dtr_, dtri], writes=[dtr_])
                    S.op("act", lambda e: e.activation(out=sn[:], in_=tr[:], func=AF.Sin, scale=TWO_PI), reads=[dtr_, dsn], writes=[dsn])
                    S.op("act", lambda e: e.activation(out=tr[:], in_=tr[:], func=AF.Abs), reads=[dtr_, dsn], writes=[dtr_])
                    S.op("act", lambda e: e.activation(out=cs_[:], in_=tr[:], func=AF.Sin, scale=-TWO_PI, bias=float(np.pi / 2)), reads=[dtr_, dcs_], writes=[dcs_])
                    rv = (lambda ap: ap) if d_ == 0 else (lambda ap: ap[:, ::-1])
                    grb, dgrb = gb_r.next()
                    gib, dgib = gb_r.next()
                    S.op("act", lambda e: e.activation(out=grb[:], in_=pgr[:], func=AF.Copy), reads=[dpgr, dgrb], writes=[dgrb])
                    S.op("act", lambda e: e.activation(out=gib[:], in_=pgi[:], func=AF.Copy), reads=[dpgi, dgib], writes=[dgib])
                    gre, dgre = gt_r.next()
                    gim, dgim = gt_r.next()
                    ta, dta = tt_r.next()
                    tb_, dtb = tt_r.next()
                    S.op("dve", lambda e: e.tensor_tensor(out=gre[:], in0=rv(grb[:]), in1=cs_[:], op=ALU.mult), reads=[dgrb, dcs_, dgre], writes=[dgre])
                    S.op("dve", lambda e: e.tensor_tensor(out=ta[:], in0=rv(gib[:]), in1=sn[:], op=ALU.mult), reads=[dgib, dsn, dta], writes=[dta])
                    S.op("dve", lambda e: e.tensor_tensor(out=gre[:], in0=gre[:], in1=ta[:], op=ALU.add), reads=[dgre, dta], writes=[dgre])
                    S.op("dve", lambda e: e.tensor_tensor(out=gim[:], in0=rv(gib[:]), in1=cs_[:], op=ALU.mult), reads=[dgib, dcs_, dgim], writes=[dgim])
                    S.op("dve", lambda e: e.tensor_tensor(out=tb_[:], in0=rv(grb[:]), in1=sn[:], op=ALU.mult), reads=[dgrb, dsn, dtb], writes=[dtb])
                    S.op("dve", lambda e: e.tensor_tensor(out=gim[:], in0=gim[:], in1=tb_[:], op=ALU.subtract), reads=[dgim, dtb], writes=[dgim])
                    h["mod"].append(dict(gre=(gre, dgre), gim=(gim, dgim), cs=(cs_, dcs_), sn=(sn, dsn), rv=rv))
                return h

            def stageB(q, h):
                gb, qq = h["gb"], h["qq"]
                Ug, dUg = h["Ug"], h["dUg"]
                matsb = h["matsb"]
                DDs = []
                for d_ in range(2):
                    pr = PR[d_]
                    md = h["mod"][d_]
                    gre, dgre = md["gre"]
                    gim, dgim = md["gim"]
                    cs_, dcs_ = md["cs"]
                    sn, dsn = md["sn"]
                    rv = md["rv"]
                    rho = pr["rho8"][:, q:q + 1].to_broadcast([P, 511])
                    dre, dim_ = dd[d_ * 2], dd[d_ * 2 + 1]
                    S.op("dve", lambda e: e.tensor_tensor_scan(out=dre[:, 1:512], data0=gre[:, 0:511], data1=rho, initial=0.0, op0=ALU.add, op1=ALU.mult),
                         reads=[dgre, dpp, ddd[d_ * 2]], writes=[ddd[d_ * 2]])
                    S.op("dve", lambda e: e.tensor_tensor_scan(out=dim_[:, 1:512], data0=gim[:, 0:511], data1=rho, initial=0.0, op0=ALU.add, op1=ALU.mult),
                         reads=[dgim, dpp, ddd[d_ * 2 + 1]], writes=[ddd[d_ * 2 + 1]])
                    DD, dDD = DD_r.next()
                    ta2, dta2 = tt_r.next()
                    tb2, dtb2 = tt_r.next()
                    S.op("dve", lambda e: e.tensor_tensor(out=ta2[:], in0=dre[:], in1=cs_[:], op=ALU.mult), reads=[ddd[d_ * 2], dcs_, dta2], writes=[dta2])
                    S.op("dve", lambda e: e.tensor_tensor(out=tb2[:], in0=dim_[:], in1=sn[:], op=ALU.mult), reads=[ddd[d_ * 2 + 1], dsn, dtb2], writes=[dtb2])
                    S.op("dve", lambda e: e.tensor_tensor(out=rv(DD[:, 0, :]), in0=ta2[:], in1=tb2[:], op=ALU.subtract), reads=[dta2, dtb2, dDD], writes=[dDD])
                    S.op("dve", lambda e: e.tensor_tensor(out=ta2[:], in0=dim_[:], in1=cs_[:], op=ALU.mult), reads=[ddd[d_ * 2 + 1], dcs_, dta2], writes=[dta2])
                    S.op("dve", lambda e: e.tensor_tensor(out=tb2[:], in0=dre[:], in1=sn[:], op=ALU.mult), reads=[ddd[d_ * 2], dsn, dtb2], writes=[dtb2])
                    S.op("dve", lambda e: e.tensor_tensor(out=rv(DD[:, 1, :]), in0=ta2[:], in1=tb2[:], op=ALU.add), reads=[dta2, dtb2, dDD], writes=[dDD])
                    DDs.append((DD, dDD))
                h["DDs"] = DDs

            def stageC(q, h):
                gb, qq = h["gb"], h["qq"]
                Ug, dUg = h["Ug"], h["dUg"]
                matsb = h["matsb"]
                DDs = h["DDs"]
                pys = []
                for a_ in range(2):
                    gg = 2 * qq + a_
                    Tg, dTg = h["T"][a_]
                    py, dpy = self.ps.next()
                    S.op("pe", lambda e, gg=gg, Tg=Tg: e.matmul(py[:], lhsT=Tg[:], rhs=Ug[:, gg, :], start=True, stop=False), reads=[dTg, dUg], writes=[dpy])
                    sl = slice(a_ * 64, (a_ + 1) * 64)
                    for d_ in range(2):
                        DD, dDD = DDs[d_]
                        S.op("pe", lambda e, DD=DD, d_=d_: e.matmul(py[:], lhsT=matsb["Yr", d_][sl, qq, :, :].rearrange("p s h -> p (s h)"), rhs=DD[sl, 0, :],
                                                                start=False, stop=False), reads=[dDD, dmatb], writes=[dpy])
                        S.op("pe", lambda e, DD=DD, d_=d_: e.matmul(py[:], lhsT=matsb["Yi", d_][sl, qq, :, :].rearrange("p s h -> p (s h)"), rhs=DD[sl, 1, :],
                                                                start=False, stop=(d_ == 1)), reads=[dDD, dmatb], writes=[dpy])
                    pys.append((py, dpy))
                for a_ in range(2):
                    gg = 2 * qq + a_
                    py, dpy = pys[a_]
                    yg, dyg = yg_r.next()
                    S.op("act", lambda e: e.activation(out=yg[:], in_=py[:], func=AF.Copy), reads=[dpy, dyg], writes=[dyg])
                    pb, dpb = self.ps.next()
                    for cb in range(4):
                        S.op("pe", lambda e, cb=cb: e.transpose(out=pb[:, cb * P:(cb + 1) * P], in_=yg[:, cb * P:(cb + 1) * P], identity=self.ident[:]),
                             reads=[dyg, dc], writes=[dpb])
                    S.op("dve", lambda e, gg=gg: e.tensor_copy(out=Yacc[:, :, :, gg * 16:(gg + 1) * 16],
                                                               in_=pb[:].rearrange("p (cb l h) -> p cb l h", cb=4, l=8)), reads=[dpb, dYacc], writes=[dYacc])
                if qq == 3:
                    for cb in range(4):
                        S.dma("act", Y54[:, cb, :, gb * P:(gb + 1) * P], Yacc[:, cb, :, :], reads=[dYacc])

            batch_load(0)
            hs = [stageA(0), stageA(1)]
            extra = list(extra)
            per = (len(extra) + NQ - 1) // NQ
            for q in range(NQ + 1):
                for _ in range(per):
                    if extra:
                        extra.pop(0)()
                if q + 2 < NQ:
                    hs.append(stageA(q + 2))
                if q < NQ:
                    stageB(q, hs[q])
                if q >= 1:
                    stageC(q - 1, hs[q - 1])
                    hs[q - 1] = None
            S.barrier()
        if self.dbg == "Y5" or STOP == "s5main":
            return
        self.glu(L, Y5)

    def glu(self, L, Y5):
        nc, S, p = self.nc, self.S, self.prm
        with ExitStack() as st:
            sb = lambda n, shp, dt=F32: st.enter_context(nc.sbuf_tensor(uname(n), shp, dt))
            W = sb("gl_w", [P, 8, D], BF16)
            dW = Dep()
            W3 = self.WGLUB.rearrange("(k p) n -> p k n", p=P)
            for q in range(2):
                S.dma("sp", W[:, q * 4:(q + 1) * 4, :], W3[:, q * 4:(q + 1) * 4, :], writes=[dW])
            bg, nw = sb("gl_b", [P, D]), sb("gl_nw", [P, D])
            dpl = Dep()
            S.dma("sp", bg[:], bcast_row(p["b_glu"][L:L + 1, :], D), writes=[dpl])
            S.dma("sp", nw[:], bcast_row(p["s5_norm_w"][L:L + 1, :], D), writes=[dpl])
            yr = Ring(nc, st, "gl_y", 6, [P, D], F32)
            gbr = Ring(nc, st, "gl_gb", 3, [P, D], BF16)
            gTr = Ring(nc, st, "gl_gT", 3, [P, 8, P], BF16)
            sr = Ring(nc, st, "gl_s", 2, [P, D], F32)
            ss_r = Ring(nc, st, "gl_ss", 2, [P, 2], F32)
            yb_r = Ring(nc, st, "gl_yb", 3, [P, D], BF16)
            yT_r = Ring(nc, st, "gl_yT", 2, [P, 8, P], BF16)
            def gA(tt):
                y, dy = yr.next()
                S.dma("sp", y[:], Y5[tt * P:(tt + 1) * P, :], writes=[dy])
                return (y, dy)

            def gB(tt, hA):
                y, dy = hA
                S.op("act", lambda e: e.activation(out=y[:], in_=y[:], func=AF.Gelu), reads=[dy], writes=[dy])
                gb_, dgb_ = gbr.next()
                S.op("dve", lambda e: e.tensor_copy(out=gb_[:], in_=y[:]), reads=[dy, dgb_], writes=[dgb_])
                return (y, dy, gb_, dgb_)

            def gC(tt, hB):
                y, dy, gb_, dgb_ = hB
                pt, dpt = self.ps.next()
                ptb = pt[:].bitcast(BF16)
                for k in range(8):
                    S.op("pe", lambda e, k=k: e.transpose(out=ptb[:, k * P:(k + 1) * P], in_=gb_[:, k * P:(k + 1) * P], identity=self.identb[:]),
                         reads=[dgb_, self.dconst], writes=[dpt])
                gT, dgT = gTr.next()
                S.op("act", lambda e: e.activation(out=gT[:], in_=ptb.rearrange("p (k t) -> p k t", k=8), func=AF.Copy), reads=[dpt, dgT], writes=[dgT])
                return (y, dy, gT, dgT)

            def g2(tt, h1):
                y, dy, gT, dgT = h1
                sg, dsg = sr.next()
                for nb in range(2):
                    pm, dpm = self.ps.next()
                    for k in range(8):
                        S.op("pe", lambda e, k=k, pm=pm, nb=nb: e.matmul(pm[:], lhsT=gT[:, k, :], rhs=W[:, k, nb * 512:(nb + 1) * 512],
                                                                   start=(k == 0), stop=(k == 7)), reads=[dgT, dW], writes=[dpm])
                    S.op("dve", lambda e, pm=pm, nb=nb: e.tensor_tensor(out=sg[:, nb * 512:(nb + 1) * 512], in0=pm[:], in1=bg[:, nb * 512:(nb + 1) * 512], op=ALU.add),
                         reads=[dpm, dpl, dsg], writes=[dsg])
                S.op("act", lambda e: e.activation(out=sg[:], in_=sg[:], func=AF.Tanh, scale=0.5), reads=[dsg], writes=[dsg])
                S.op("dve", lambda e: e.scalar_tensor_tensor(out=y[:], in0=sg[:], scalar=1.0, in1=y[:], op0=ALU.add, op1=ALU.mult), reads=[dy, dsg], writes=[dy])
                ss, dss = ss_r.next()
                S.op("act", lambda e: e.activation(out=sg[:], in_=y[:], func=AF.Square, accum_out=ss[:, 0:1]), reads=[dy, dsg], writes=[dsg, dss])
                S.op("act", lambda e: e.activation(out=ss[:, 1:2], in_=ss[:, 0:1], func=AF.Sqrt, bias=4.0 * EPS, scale=1.0 / D), reads=[dss], writes=[dss])
                S.op("dve", lambda e: e.reciprocal(out=ss[:, 1:2], in_=ss[:, 1:2]), reads=[dss], writes=[dss])
                yb, dyb = yb_r.next()
                S.op("dve", lambda e: e.scalar_tensor_tensor(out=yb[:], in0=y[:], scalar=ss[:, 1:2], in1=nw[:], op0=ALU.mult, op1=ALU.mult),
                     reads=[dy, dss, dpl, dyb], writes=[dyb])
                return (yb, dyb)

            hq = {}
            for i in range(NT + 5):
                if i < NT:
                    hq["A", i] = gA(i)
                if 0 <= i - 1 < NT:
                    hq["B", i - 1] = gB(i - 1, hq.pop(("A", i - 1)))
                if 0 <= i - 2 < NT:
                    hq["C", i - 2] = gC(i - 2, hq.pop(("B", i - 2)))
                if 0 <= i - 3 < NT:
                    hq["2", i - 3] = g2(i - 3, hq.pop(("C", i - 3)))
                if 0 <= i - 4 < NT:
                    yb, dyb = hq.pop(("2", i - 4))
                    self.store_T(yb, dyb, yT_r, D, i - 4)
            S.barrier()

    def out_proj(self, L):
        nc, S, p = self.nc, self.S, self.prm
        with ExitStack() as st:
            W = st.enter_context(nc.sbuf_tensor(uname("op_w"), [P, 16, D], BF16))
            dW = Dep()
            W3 = self.WOUTB.rearrange("(k p) n -> p k n", p=P)
            for q in range(4):
                S.dma("sp", W[:, q * 4:(q + 1) * 4, :], W3[:, q * 4:(q + 1) * 4, :], writes=[dW])
            g_t, b_t, dgb = self.load_gb(st, p["ln1_g"][L:L + 1, :], p["ln1_b"][L:L + 1, :])
            tmp = self.ln_tmp(st)
            ar = Ring(nc, st, "op_a", 2, [P, 16, 512], BF16)
            hr = Ring(nc, st, "op_h", 3, [P, D], F32)
            xr = Ring(nc, st, "op_x", 5, [P, D], F32)
            A3 = self.YCATT.rearrange("(k p) t -> p k t", p=P)
            pend = None
            for tb in range(8):
                a, da = ar.next()
                for q in range(2):
                    S.dma("sp", a[:, q * 8:(q + 1) * 8, :], A3[:, q * 8:(q + 1) * 8, tb * 512:(tb + 1) * 512], writes=[da])
                for j in range(4):
                    tt = tb * 4 + j
                    hres, dhr = hr.next()
                    S.dma("sp", hres[:], self.H32[tt * P:(tt + 1) * P, :], writes=[dhr])
                    xt, dx = xr.next()
                    for nb in range(2):
                        pt, dpt = self.ps.next()
                        for k in range(16):
                            S.op("pe", lambda e, k=k, pt=pt, a=a, j=j, nb=nb: e.matmul(
                                pt[:], lhsT=a[:, k, j * P:(j + 1) * P], rhs=W[:, k, nb * 512:(nb + 1) * 512],
                                start=(k == 0), stop=(k == 15)), reads=[da, dW], writes=[dpt])
                        S.op("dve", lambda e, xt=xt, hres=hres, pt=pt, nb=nb: e.scalar_tensor_tensor(
                            out=xt[:, nb * 512:(nb + 1) * 512], in0=hres[:, nb * 512:(nb + 1) * 512], scalar=ALPHA, in1=pt[:],
                            op0=ALU.mult, op1=ALU.add), reads=[dhr, dpt, dx], writes=[dx])
                    if pend is not None:
                        self.ln_core(st, *pend)
                        self.ln_flush(tmp, 1)
                    pend = (xt, dx, g_t, b_t, dgb, tt, self.H32, tmp)
            self.ln_core(st, *pend)
            self.ln_flush(tmp, 0)
            S.barrier()

    def mlp(self, L):
        nc, S, p = self.nc, self.S, self.prm
        with ExitStack() as st:
            hT = st.enter_context(nc.sbuf_tensor(uname("ml_hT"), [P, 8, T], BF16))
            dh = Dep()
            for k in range(8):
                S.dma("sp", hT[:, k, :], self.HT[k * P:(k + 1) * P, :], writes=[dh])
            S.barrier()
            g_t, b_t, dgb = self.load_gb(st, p["ln2_g"][L:L + 1, :], p["ln2_b"][L:L + 1, :])
            tmp = self.ln_tmp(st)
            h1T = st.enter_context(nc.sbuf_tensor(uname("ml_h1T"), [P, 32, 512], BF16))
            dh1 = [Dep() for _ in range(32)]
            w1r = Ring(nc, st, "ml_w1", 2, [P, 8, 512], BF16)
            w2r = Ring(nc, st, "ml_w2", 3, [P, 8, 512], BF16)
            rr = Ring(nc, st, "ml_r", 3, [P, 512], BF16)
            hr = Ring(nc, st, "ml_h", 3, [P, 512], F32)
            xr = Ring(nc, st, "ml_x", 9, [P, D], F32)
            W13 = self.W1B.rearrange("(k p) n -> p k n", p=P)
            W23 = self.W2B.rearrange("(k p) n -> p k n", p=P)
            pend_ln = []
            for tb in range(8):
                for jq in range(8):
                    w1, dw1 = w1r.next()
                    S.dma("sp", w1[:], W13[:, :, jq * 512:(jq + 1) * 512], writes=[dw1])
                    for jj in range(4):
                        j = jq * 4 + jj
                        pt, dpt = self.ps.next()
                        for k in range(8):
                            S.op("pe", lambda e, k=k, pt=pt, w1=w1, jj=jj, tb=tb: e.matmul(
                                pt[:], lhsT=w1[:, k, jj * P:(jj + 1) * P], rhs=hT[:, k, tb * 512:(tb + 1) * 512],
                                start=(k == 0), stop=(k == 7)), reads=[dh, dw1], writes=[dpt])
                        r, dr = rr.next()
                        S.op("act", lambda e, r=r, pt=pt: e.activation(out=r[:], in_=pt[:], func=AF.Relu), reads=[dpt], writes=[dr])
                        eng = "pool" if j % 2 == 0 else "dve"
                        S.op(eng, lambda e, r=r, j=j: e.tensor_tensor(out=h1T[:, j, :], in0=r[:], in1=r[:], op=ALU.mult),
                             reads=[dr], writes=[dh1[j]])
                for a_ in pend_ln:
                    self.ln_core(st, *a_)
                pend_ln = []
                xts = []
                for tj in range(4):
                    hres, dhr = hr.next() if False else (None, None)
                    xts.append(xr.next())
                for nb in range(2):
                    pts = [self.ps.next() for _ in range(4)]
                    for jg in range(4):
                        w2, dw2 = w2r.next()
                        S.dma("sp", w2[:], W23[:, jg * 8:(jg + 1) * 8, nb * 512:(nb + 1) * 512], writes=[dw2])
                        for tj in range(4):
                            pt, dpt = pts[tj]
                            for k in range(8):
                                kk = jg * 8 + k
                                S.op("pe", lambda e, k=k, kk=kk, pt=pt, w2=w2, tj=tj: e.matmul(
                                    pt[:], lhsT=h1T[:, kk, tj * P:(tj + 1) * P], rhs=w2[:, k, :],
                                    start=(kk == 0), stop=(kk == 31)), reads=[dh1[kk], dw2], writes=[dpt])
                    for tj in range(4):
                        tt = tb * 4 + tj
                        pt, dpt = pts[tj]
                        xt, dx = xts[tj]
                        hres, dhr = hr.next()
                        S.dma("sp", hres[:, 0:512], self.H32[tt * P:(tt + 1) * P, nb * 512:(nb + 1) * 512], writes=[dhr])
                        S.op("dve", lambda e, xt=xt, hres=hres, pt=pt, nb=nb: e.scalar_tensor_tensor(
                            out=xt[:, nb * 512:(nb + 1) * 512], in0=hres[:, 0:512], scalar=ALPHA, in1=pt[:],
                            op0=ALU.mult, op1=ALU.add), reads=[dhr, dpt, dx], writes=[dx])
                self.ln_flush(tmp, 0)
                pend_ln = [(xts[tj][0], xts[tj][1], g_t, b_t, dgb, tb * 4 + tj, self.H32, tmp) for tj in range(4)]
            for a_ in pend_ln:
                self.ln_core(st, *a_)
                self.ln_flush(tmp, 1)
            self.ln_flush(tmp, 0)
            S.barrier()


SSD_ONLY = False
OVERLAP_CAST = True
STOP = None
_CACHE = {}


def get_nc(dbg=None, nlayers=DEPTH):
    key = (dbg, nlayers)
    if key not in _CACHE:
        _CACHE[key] = K(dbg, nlayers).nc
    return _CACHE[key]


def kernel(**inputs):
    nc = get_nc()
    x = np.ascontiguousarray(inputs["x"], dtype=np.float32)
    in_maps = []
    for c in range(8):
        m = {"x": x[c]}
        for n, s in PARAMS:
            m[n] = np.ascontiguousarray(inputs[n], dtype=np.float32)
        in_maps.append(m)
    res = run_bass_kernel_spmd(nc, in_maps, core_ids=list(range(8)))
    return np.stack([r["out"] for r in res.results], axis=0)
```
